# Optimizing a Trainium2 kernel written in Bass

```python
import math
import jax, jax.numpy as jnp
from jax import lax
import numpy as np

D_MODEL = 1024
BATCH = 1
SEQ = 16384
DEPTH = 1
DEC_BATCH = 128
DEC_SEQ = 1
PAST_LEN = 16384
PAGE_SIZE = 128

N_HEADS = 8
N_KV_HEADS = 2
HEAD_DIM = 64
GROUP = N_HEADS // N_KV_HEADS
ATTN_W = N_HEADS * HEAD_DIM
KV_W = N_KV_HEADS * HEAD_DIM
CONV_CH = D_MODEL - ATTN_W
MIX_W = ATTN_W + CONV_CH
IN_W = ATTN_W + 2 * KV_W + 2 * CONV_CH
CONV_WIDTH = 31
WINDOW = 128
BLOCK = 128
N_BUCKETS = 32
MAX_DISTANCE = WINDOW
N_META = 16
D_FF = 4 * D_MODEL
EPS = 1e-6

kernel_name = "hymba_swa_sink_conformer_conv_decoder_step"


def rmsnorm(x, g):
    xf = x.astype(jnp.float32)
    y = xf * lax.rsqrt(jnp.mean(xf * xf, axis=-1, keepdims=True) + EPS)
    return (y * g.astype(jnp.float32)).astype(x.dtype)


def layernorm(x, g, b):
    xf = x.astype(jnp.float32)
    mu = jnp.mean(xf, axis=-1, keepdims=True)
    xc = xf - mu
    y = xc * lax.rsqrt(jnp.mean(xc * xc, axis=-1, keepdims=True) + EPS)
    return (y * g.astype(jnp.float32) + b.astype(jnp.float32)).astype(x.dtype)


def t5_bucket(d):
    max_exact = N_BUCKETS // 2
    d_f = jnp.maximum(d, 1).astype(jnp.float32)
    large = max_exact + (jnp.log(d_f / max_exact) / math.log(MAX_DISTANCE / max_exact)
                         * (N_BUCKETS - max_exact)).astype(jnp.int32)
    large = jnp.minimum(large, N_BUCKETS - 1)
    return jnp.where(d < max_exact, d, large)


def sink_attention(q, k, v, dist, valid, rel_bias, sinks):
    q = q.reshape(q.shape[:-2] + (N_KV_HEADS, GROUP, HEAD_DIM))
    s = jnp.einsum('...qhgd,...khd->...hgqk', q, k,
                   preferred_element_type=jnp.float32) * (HEAD_DIM ** -0.5)
    bias = rel_bias.astype(jnp.float32)[t5_bucket(jnp.clip(dist, 0, WINDOW))]
    bias = jnp.moveaxis(bias, -1, 0).reshape((N_KV_HEADS, GROUP) + dist.shape)
    s = jnp.where(valid, s + bias, -jnp.inf)
    sk = sinks.astype(jnp.float32).reshape(N_KV_HEADS, GROUP, 1, 1)
    m = jnp.maximum(jnp.max(s, axis=-1, keepdims=True), sk)
    p = jnp.exp(s - m)
    p = p / (jnp.sum(p, axis=-1, keepdims=True) + jnp.exp(sk - m))
    o = jnp.einsum('...hgqk,...khd->...qhgd', p.astype(v.dtype), v)
    return o.reshape(o.shape[:-3] + (ATTN_W,))


def project_in(x, norm1_g, w_in):
    h = rmsnorm(x, norm1_g)
    p = h @ w_in
    lead = p.shape[:-1]
    o1, o2, o3, o4 = ATTN_W, ATTN_W + KV_W, ATTN_W + 2 * KV_W, ATTN_W + 2 * KV_W + CONV_CH
    q = p[..., :o1].reshape(lead + (N_HEADS, HEAD_DIM))
    k = p[..., o1:o2].reshape(lead + (N_KV_HEADS, HEAD_DIM))
    v = p[..., o2:o3].reshape(lead + (N_KV_HEADS, HEAD_DIM))
    u = p[..., o3:o4] * jax.nn.sigmoid(p[..., o4:])
    return q, k, v, u


def depthwise_conv(u_padded, conv_w, conv_b):
    out = lax.conv_general_dilated(u_padded, conv_w[:, None, :].astype(u_padded.dtype),
                                   window_strides=(1,), padding='VALID',
                                   dimension_numbers=('NWC', 'WIO', 'NWC'),
                                   feature_group_count=CONV_CH)
    return out + conv_b


def finish_layer(x, attn_o, conv_c, conv_ln_g, conv_ln_b, w_out, norm2_g, w_up, w_down):
    c = jax.nn.silu(layernorm(conv_c, conv_ln_g, conv_ln_b))
    x = x + jnp.concatenate([attn_o.astype(x.dtype), c.astype(x.dtype)], axis=-1) @ w_out
    hid = jnp.square(jax.nn.relu(rmsnorm(x, norm2_g) @ w_up))
    return x + hid @ w_down


def setup_inputs(seed: int = 0) -> dict:
    key = jax.random.key(seed)
    ks = jax.random.split(key, 20)
    f = jnp.float32
    nrm = lambda k, shape, sc: jax.random.normal(k, shape, f) * sc
    return {
        'x_prompt': nrm(ks[0], (BATCH, SEQ, D_MODEL), 1.0),
        'x_sample': nrm(ks[1], (DEC_BATCH, DEC_SEQ, D_MODEL), 1.0),
        'cache_k': nrm(ks[2], (DEPTH, DEC_BATCH, WINDOW, N_KV_HEADS, HEAD_DIM), 1.0),
        'cache_v': nrm(ks[3], (DEPTH, DEC_BATCH, WINDOW, N_KV_HEADS, HEAD_DIM), 1.0),
        'state_conv': nrm(ks[4], (DEPTH, DEC_BATCH, CONV_WIDTH - 1, CONV_CH), 0.5),
        'meta_tokens': nrm(ks[5], (N_META, D_MODEL), 1.0),
        'rel_bias': nrm(ks[6], (N_BUCKETS, N_HEADS), 0.5),
        'norm1_g': 1.0 + nrm(ks[7], (DEPTH, D_MODEL), 0.02),
        'w_in': nrm(ks[8], (DEPTH, D_MODEL, IN_W), D_MODEL ** -0.5),
        'attn_sinks': nrm(ks[9], (DEPTH, N_HEADS), 0.5),
        'conv_w': nrm(ks[10], (DEPTH, CONV_WIDTH, CONV_CH), CONV_WIDTH ** -0.5),
        'conv_b': nrm(ks[11], (DEPTH, CONV_CH), 0.02),
        'conv_ln_g': 1.0 + nrm(ks[12], (DEPTH, CONV_CH), 0.02),
        'conv_ln_b': nrm(ks[13], (DEPTH, CONV_CH), 0.02),
        'w_out': nrm(ks[14], (DEPTH, MIX_W, D_MODEL), MIX_W ** -0.5),
        'norm2_g': 1.0 + nrm(ks[15], (DEPTH, D_MODEL), 0.02),
        'w_up': nrm(ks[16], (DEPTH, D_MODEL, D_FF), D_MODEL ** -0.5),
        'w_down': nrm(ks[17], (DEPTH, D_FF, D_MODEL), D_FF ** -0.5),
        'norm_f_g': 1.0 + nrm(ks[18], (D_MODEL,), 0.02),
    }


def reference(x_prompt, x_sample, cache_k, cache_v, state_conv, meta_tokens, rel_bias,
              norm1_g, w_in, attn_sinks, conv_w, conv_b, conv_ln_g, conv_ln_b,
              w_out, norm2_g, w_up, w_down, norm_f_g):
    B, S_p, _ = x_prompt.shape
    DB, S_s, _ = x_sample.shape
    xp = jnp.concatenate([jnp.broadcast_to(meta_tokens[None].astype(x_prompt.dtype), (B, N_META, D_MODEL)),
                          x_prompt], axis=1)
    L = S_p + N_META
    pad = (-N_META) % BLOCK
    Lp = L + pad
    nblk = Lp // BLOCK
    bi = jnp.arange(BLOCK)[:, None]
    bj = jnp.arange(2 * BLOCK)[None, :]
    dist_p = bi + BLOCK - bj
    key_idx = (jnp.arange(nblk)[:, None] - 1) * BLOCK + jnp.arange(2 * BLOCK)[None, :]
    valid_p = ((dist_p >= 0) & (dist_p <= WINDOW))[None] & (key_idx >= pad)[:, None, :]
    valid_p = valid_p[:, None, None]
    qi = jnp.arange(S_s)[:, None]
    kj = jnp.arange(WINDOW + S_s)[None, :]
    dist_s = qi + WINDOW - kj
    valid_s = ((dist_s >= 0) & (dist_s <= WINDOW))[None, None, None]

    def blocks(t):
        t = jnp.pad(t, ((0, 0), (pad, 0)) + ((0, 0),) * (t.ndim - 2))
        return t.reshape((B, nblk, BLOCK) + t.shape[2:])

    def with_prev(tb):
        prev = jnp.pad(tb, ((0, 0), (1, 0)) + ((0, 0),) * (tb.ndim - 2))[:, :-1]
        return jnp.concatenate([prev, tb], axis=2)

    xs = x_sample
    nk_p, nv_p, nc_p, nk_s, nv_s, nc_s = [], [], [], [], [], []
    for l in range(DEPTH):
        q, k, v, u = project_in(xp, norm1_g[l], w_in[l])
        ao = sink_attention(blocks(q), with_prev(blocks(k)), with_prev(blocks(v)),
                            dist_p, valid_p, rel_bias, attn_sinks[l])
        ao = ao.reshape(B, Lp, ATTN_W)[:, pad:]
        cc = depthwise_conv(jnp.pad(u, ((0, 0), (CONV_WIDTH - 1, 0), (0, 0))), conv_w[l], conv_b[l])
        xp = finish_layer(xp, ao, cc, conv_ln_g[l], conv_ln_b[l], w_out[l], norm2_g[l], w_up[l], w_down[l])
        nk_p.append(k[:, -WINDOW:])
        nv_p.append(v[:, -WINDOW:])
        nc_p.append(u[:, -(CONV_WIDTH - 1):])
        q, k, v, u = project_in(xs, norm1_g[l], w_in[l])
        kk = jnp.concatenate([cache_k[l].astype(k.dtype), k], axis=1)
        vv = jnp.concatenate([cache_v[l].astype(v.dtype), v], axis=1)
        ao = sink_attention(q, kk, vv, dist_s, valid_s, rel_bias, attn_sinks[l])
        uu = jnp.concatenate([state_conv[l].astype(u.dtype), u], axis=1)
        cc = depthwise_conv(uu, conv_w[l], conv_b[l])
        xs = finish_layer(xs, ao, cc, conv_ln_g[l], conv_ln_b[l], w_out[l], norm2_g[l], w_up[l], w_down[l])
        nk_s.append(kk[:, -WINDOW:])
        nv_s.append(vv[:, -WINDOW:])
        nc_s.append(uu[:, -(CONV_WIDTH - 1):])

    y_prompt = rmsnorm(xp, norm_f_g)[:, N_META:]
    y_sample = rmsnorm(xs, norm_f_g)
    return (y_prompt, y_sample, jnp.stack(nk_p), jnp.stack(nv_p), jnp.stack(nc_p),
            jnp.stack(nk_s), jnp.stack(nv_s), jnp.stack(nc_s))
```

```python
import math
import numpy as np
import concourse.bass as bass
import concourse.mybir as mybir
from concourse.bass_utils import run_bass_kernel_spmd

F32 = mybir.dt.float32
BF16 = mybir.dt.bfloat16
AF = mybir.ActivationFunctionType
ALU = mybir.AluOpType
AX = mybir.AxisListType

ENGS = ("pe", "act", "dve", "pool", "sp")
NEG = -1.0e30
EPS = 1e-6
NCORES = 8
NTILE = 8
TT = 256
NB = TT // 128
SAMPLE_TILE = -1
APF = 5
D = 1024
DFF = 4096
NS = 16
SNC = 128 + NS


class Buf:
    __slots__ = ("name", "last_w", "readers", "dsem", "dcount")

    def __init__(self, name):
        self.name = name
        self.last_w = None
        self.readers = []
        self.dsem = None
        self.dcount = 0


class Op:
    __slots__ = ("eng", "fn", "deps", "idx", "pos", "sig", "tick", "dma", "dval", "dbuf")

    def __init__(self, eng, fn, deps):
        self.eng = eng
        self.fn = fn
        self.deps = deps
        self.sig = False
        self.tick = 0
        self.dma = False
        self.dval = 0
        self.dbuf = None


class Sched:
    def __init__(self, nc):
        self.nc = nc
        self.ops = []
        self.streams = {e: [] for e in ENGS}
        self.dma_bufs = []
        self.prepared = False

    def _mk(self, eng, fn, reads, writes):
        deps = []
        seen = set()
        for b in reads:
            if b.last_w is not None and id(b.last_w) not in seen:
                seen.add(id(b.last_w)); deps.append(b.last_w)
            if b.name.startswith("ps"):
                for r in b.readers:
                    if r.eng != eng and id(r) not in seen:
                        seen.add(id(r)); deps.append(r)
        for b in writes:
            if b.last_w is not None and id(b.last_w) not in seen:
                seen.add(id(b.last_w)); deps.append(b.last_w)
            for r in b.readers:
                if id(r) not in seen:
                    seen.add(id(r)); deps.append(r)
        o = Op(eng, fn, deps)
        o.idx = len(self.ops)
        o.pos = len(self.streams[eng])
        self.ops.append(o)
        self.streams[eng].append(o)
        for b in reads:
            b.readers.append(o)
        for b in writes:
            b.last_w = o
            b.readers = []
        return o

    def op(self, eng, fn, reads=(), writes=()):
        return self._mk(eng, fn, list(reads), list(writes))

    def dma(self, eng, pairs, reads=(), writes=(), sem_buf=None):
        reads = list(reads); writes = list(writes)
        if sem_buf is None:
            sem_buf = writes[0]
        o = self._mk(eng, None, reads, writes)
        o.dma = True
        o.fn = pairs
        o.dbuf = sem_buf
        sem_buf.dcount += 16 * len(pairs)
        o.dval = sem_buf.dcount
        if sem_buf not in self.dma_bufs:
            self.dma_bufs.append(sem_buf)
        return o

    @staticmethod
    def _skip(o, d):
        if o.dma:
            return False
        if d.eng == o.eng:
            if d.eng == "pe":
                return True
            if (o.pos - d.pos) > 3:
                return True
        return False

    def prepare(self):
        nc = self.nc
        for o in self.ops:
            for d in o.deps:
                if not d.dma and not self._skip(o, d):
                    d.sig = True
        self.esem = {}
        for e in ENGS:
            if any(o.sig for o in self.streams[e]):
                self.esem[e] = nc.alloc_semaphore(name=f"es_{e}")
        for b in self.dma_bufs:
            b.dsem = nc.alloc_semaphore(name=f"ds_{b.name}")
        for e in ENGS:
            t = 0
            for o in self.streams[e]:
                if o.sig:
                    t += 1
                o.tick = t
        self.prepared = True

    def emit_one(self, e, h, final_eng="sp"):
        if not self.prepared:
            self.prepare()
        esem = self.esem
        waited = {}
        for o in self.streams[e]:
            need = {}
            for d in o.deps:
                if d.dma:
                    key = ("d", id(d.dbuf)); sem = d.dbuf.dsem; val = d.dval
                else:
                    if self._skip(o, d):
                        continue
                    key = ("e", d.eng); sem = esem[d.eng]; val = d.tick
                if key not in need or need[key][1] < val:
                    need[key] = (sem, val)
            for key, (sem, val) in need.items():
                if waited.get(key, 0) >= val:
                    continue
                waited[key] = val
                h.wait_ge(sem, val)
            if o.dma:
                for (out_ap, in_ap) in o.fn:
                    h.dma_start(out=out_ap, in_=in_ap).then_inc(o.dbuf.dsem, 16)
            else:
                ins = o.fn(h)
                if o.sig:
                    ins.then_inc(esem[e], 1)
        if e == final_eng:
            for b in self.dma_bufs:
                if b.dcount > 0:
                    h.wait_ge(b.dsem, b.dcount)


def _t5_bucket_np(d):
    max_exact = 16
    d = np.asarray(d)
    d_f = np.maximum(d, 1).astype(np.float32)
    large = max_exact + (np.log(d_f / np.float32(max_exact)) / np.float32(math.log(128 / max_exact))
                         * np.float32(32 - max_exact)).astype(np.int32)
    large = np.minimum(large, 31)
    return np.where(d < max_exact, d, large)


def _pair_perm():
    idx = np.zeros(512, dtype=np.int64)
    for g in range(4):
        for kvh in range(2):
            for d in range(64):
                idx[g * 128 + kvh * 64 + d] = (kvh * 4 + g) * 64 + d
    return idx


class _Stop(Exception):
    pass


def build_program():
    import os
    nc = bass.Bass("TRN2", target_bir_lowering=False)
    S = Sched(nc)
    STAGE = int(os.environ.get("KSTAGE", "99"))

    def stage(n):
        if STAGE < n:
            raise _Stop()
    try:
        _record(nc, S, stage)
    except _Stop:
        pass
    with nc.Block() as block:
        @block.tensor
        def _(e): S.emit_one("pe", e)

        @block.scalar
        def _(e): S.emit_one("act", e)

        @block.vector
        def _(e): S.emit_one("dve", e)

        @block.gpsimd
        def _(e): S.emit_one("pool", e)

        @block.sync
        def _(e): S.emit_one("sp", e)
    return nc


def _record(nc, S, stage):
    import os

    def din(name, shape, dt=F32):
        return nc.dram_tensor(name, list(shape), dt, kind="ExternalInput").ap()

    def dout(name, shape):
        return nc.dram_tensor(name, list(shape), F32, kind="ExternalOutput").ap()

    xin = din("xin", [128 + NTILE * TT, D])
    xs_d = din("xs", [NS, D])
    ck_d = din("ck", [NS, 128, 128])
    cv_d = din("cv", [NS, 128, 128])
    sc_d = din("sc", [NS, 30, 512])
    win_d = din("win", [128, 8, 1792])
    wout_d = din("wout", [128, 8, 1024])
    wup_d = din("wup", [16 * 128, 2048])
    wdn_d = din("wdn", [16 * 128, 2048])
    g1_d = din("g1cm", [128, 8]); g2_d = din("g2cm", [128, 8]); gf_d = din("gf", [D])
    cw_d = din("cwcm", [128, 4, 31])
    cvec_d = din("cvec", [128, 12])
    rbx_d = din("rbx", [33, 8])
    sinks_d = din("sinks", [8])
    oh_d = din("onehot", [33, 384])
    J_d = din("jmat", [128, 128])
    id_d = din("ident", [128, 128])
    kmask_d = din("kmask", [256])
    rowm_d = din("rowmask", [128, NS])
    ind_d = din("ind", [96, NS])
    cwrep_d = din("cwrep", [96, 2560])

    y_d = dout("y", [NTILE * TT, D])
    ys_d = dout("ys", [NS, D])
    nk_d = dout("nk", [128, 128]); nv_d = dout("nv", [128, 128])
    ncv_d = dout("ncv", [128, 512])
    nks_d = dout("nks", [NS, 128, 128]); nvs_d = dout("nvs", [NS, 128, 128])
    ncs_d = dout("ncs", [NS, 30, 512])

    wup_s = nc.dram_tensor("wup_s", [16 * 128, 2048], BF16, kind="Internal").ap()
    wdn_s = nc.dram_tensor("wdn_s", [16 * 128, 2048], BF16, kind="Internal").ap()
    biasG = nc.dram_tensor("biasG", [8, 384], F32, kind="Internal").ap()

    bufs = {}

    def T(name, shape, dt=F32):
        t = nc.alloc_sbuf_tensor("s_" + name, list(shape), dt)
        bufs[name] = Buf(name)
        return t

    def B(*names):
        return [bufs[n] for n in names]

    NBANK = 8
    banks = [nc.alloc_psum_tensor(f"ps{i}", [128, 512], F32) for i in range(NBANK)]
    bank_bufs = [Buf(f"ps{i}") for i in range(NBANK)]
    bank_ctr = [0]

    reserved = set()

    def nextbank():
        while True:
            i = bank_ctr[0] % NBANK
            bank_ctr[0] += 1
            if i not in reserved:
                return banks[i], bank_bufs[i]

    def bank_index(bk_):
        for i_, b_ in enumerate(banks):
            if b_ is bk_:
                return i_
        raise KeyError

    win = T("win", [128, 8, 1792], BF16)
    wout = T("wout", [128, 8, 1024], BF16)
    g1b = T("g1b", [128, 8]); g2b = T("g2b", [128, 8]); gfb = T("gfb", [128, D])
    cwcm = T("cwcm", [128, 4, 31]); cvec = T("cvec", [128, 12]); ncvec = T("ncvec", [128, 12])
    identf = T("identf", [128, 128]); identb = T("identb", [128, 128], BF16)
    jf = T("jf", [128, 128]); onesf = T("onesf", [128, 128])
    rowm = T("rowm", [128, NS]); zcol = T("zcol", [128, 1])
    bias = T("bias", [128, 8, 256])
    kmask = T("kmask", [128, 256])
    Ssb2 = [T(f"Ssbx{p}", [128, 8, 260]) for p in range(2)]
    for p_ in range(2):
        for hp_ in range(4):
            bufs[f"Ssb{p_}_{hp_}"] = Buf(f"Ssb{p_}_{hp_}")
    sinkb = T("sinkb", [128, 8])

    S.dma("sp", [(identf[:], id_d), (jf[:], J_d)], writes=B("identf", "jf"), sem_buf=bufs["identf"])
    S.dma("sp", [(cwcm[:], cw_d), (cvec[:], cvec_d), (rowm[:], rowm_d)], writes=B("cwcm", "cvec", "rowm"), sem_buf=bufs["cwcm"])
    S.dma("sp", [(g1b[:], g1_d), (g2b[:], g2_d),
                 (gfb[:], bass.AP(gf_d.tensor, 0, [[0, 128], [1, D]])),
                 (kmask[:], bass.AP(kmask_d.tensor, 0, [[0, 128], [1, 256]])),
                 (sinkb[:], bass.AP(sinks_d.tensor, 0, [[0, 128], [1, 8]]))],
          writes=B("g1b", "g2b", "gfb", "kmask", "sinkb"), sem_buf=bufs["g1b"])
    S.dma("pool", [(win[:, kc, :], win_d[:, kc, :]) for kc in range(8)], writes=B("win"))
    S.dma("pool", [(wout[:, kc, :], wout_d[:, kc, :]) for kc in range(8)], writes=B("wout"))
    S.op("dve", lambda e: e.tensor_copy(identb[:], identf[:]), reads=B("identf"), writes=B("identb"))
    cwcmb = T("cwcmb", [128, 4, 31], BF16)
    S.op("dve", lambda e: e.tensor_copy(cwcmb[:], cwcm[:]), reads=B("cwcm"), writes=B("cwcmb"))
    S.op("pool", lambda e: e.memset(onesf[:], 1.0), writes=B("onesf"))
    S.op("pool", lambda e: e.memset(zcol[:], 0.0), writes=B("zcol"))
    S.op("dve", lambda e: e.tensor_scalar(ncvec[:], cvec[:], -1.0, None, ALU.mult), reads=B("cvec"), writes=B("ncvec"))
    for p_ in range(2):
        S.op("dve", lambda e, p_=p_: e.tensor_copy(Ssb2[p_][:, :, 256], sinkb[:]), reads=B("sinkb"),
             writes=[bufs[f"Ssb{p_}_{hp_}"] for hp_ in range(4)])

    stage(1)
    rbx = T("rbx", [33, 8]); cch = T("cch", [128, 4, TT])
    arena = T("arena", [128, 12288], BF16)
    biasT = arena[:, 0:4096].bitcast(F32); bufs["biasT"] = Buf("biasT")
    utok = T("utok", [128, 512]); kvtok = utok[:, 0:256]; bufs["kvtok"] = bufs["utok"]
    oh = biasT[0:33, 0:384]; bufs["oh"] = bufs["biasT"]; gsb = utok[0:8, 0:384]; bufs["gsb"] = bufs["utok"]
    S.dma("sp", [(rbx[:], rbx_d), (oh, oh_d)], writes=B("rbx", "oh"), sem_buf=bufs["rbx"])
    bk, bb = nextbank()
    S.op("pe", lambda e: e.matmul(bk[0:8, 0:384], rbx[:, :], oh, start=True, stop=True), reads=B("rbx", "oh"), writes=[bb])
    S.op("dve", lambda e: e.tensor_copy(gsb, bk[0:8, 0:384]), reads=[bb], writes=B("gsb"))
    bufs["biasG"] = Buf("biasG")
    S.dma("sp", [(biasG, gsb)], reads=B("gsb"), writes=B("biasG"))
    S.dma("sp", [(biasT[:, h * 256:(h + 1) * 256], bass.AP(biasG.tensor, h * 384, [[1, 128], [1, 256]])) for h in range(8)],
          reads=B("biasG"), writes=B("biasT"))
    def bias_flip():
        for hp in range(4):
            bk, bb = nextbank()
            S.op("pe", lambda e, bk=bk, hp=hp: e.matmul(bk[:, :], jf[:, :], biasT[:, hp * 512:(hp + 1) * 512], start=True, stop=True),
                 reads=B("jf", "biasT"), writes=[bb])
            S.op("dve", lambda e, bk=bk, hp=hp: e.tensor_copy(bias[:, 2 * hp:2 * hp + 2, :], bk[:, :].rearrange("p (a b) -> p a b", a=2)),
                 reads=[bb], writes=B("bias"))

    xr = T("xr", [128, 2 * NB, D])
    xrb = [Buf(f"xr_{i}") for i in range(2 * NB)]
    hb = [T(f"hb{i}", [128, D], BF16) for i in range(2)]
    actT2 = [T(f"actT{i}", [128, 8, TT + 16], BF16) for i in range(2)]
    hT = T("hT", [128, 8, TT], BF16)
    qT = T("qT", [128, 4, TT], BF16)
    kT = T("kT", [128, 128 + TT], BF16)
    vtok = T("vtok", [128, NB + 1, 128], BF16)
    uT = T("uT", [128, 4, 32 + TT], BF16)
    Pb = [T(f"Pb{i}", [128, 2, 260], BF16) for i in range(2)]
    PTs = [T(f"PTs{i}", [128, 4, 128], BF16) for i in range(2)]
    attn = [T(f"attn{i}", [128, 512], BF16) for i in range(2)]
    mixT = T("mixT", [128, 8, TT], BF16)
    diag = [T(f"diag{i}", [128, 31, 128], BF16) for i in range(2)]
    tmpA = [T(f"tmpA{i}", [128, TT]) for i in range(3)]
    cchb = [Buf(f"cch_{i}") for i in range(4)]
    rstdb = T("rstdb", [128, TT])
    wupb = [arena[:, i * 2048:(i + 1) * 2048].rearrange("p (k f) -> p k f", k=8) for i in range(3)]
    wdnb = [arena[:, 6144 + i * 2048:6144 + (i + 1) * 2048].rearrange("p (c n) -> p c n", c=2) for i in range(3)]
    for i in range(3):
        bufs[f"wupb{i}"] = Buf(f"wupb{i}")
        bufs[f"wdnb{i}"] = Buf(f"wdnb{i}")
    hidr = [T(f"hidr{i}", [128, 2, TT + 16], BF16) for i in range(3)]
    rl = [T(f"rl{i}", [128, 2, TT + 16]) for i in range(2)]
    jkt = T("jkt", [128, D], BF16)
    NST = 24
    stt = [T(f"st{i}", [128, 4]) for i in range(NST)]
    st_ctr = [0]

    def nextst():
        i = st_ctr[0] % NST
        st_ctr[0] += 1
        return stt[i], bufs[f"st{i}"]

    rr = {"hb": 0, "P": 0, "PT": 0, "attn": 0, "tmp": 0, "rl": 0, "wup": 0, "wdn": 0, "diag": 0, "hid": 0}

    def rot(key, lst, prefix):
        i = rr[key] % len(lst)
        rr[key] += 1
        return lst[i], bufs[f"{prefix}{i}"]

    def rmsnorm(xap, P, xbufs, gb, outap, outbuf, junk=None):
        st, stb = nextst()
        jap, jbuf = (outap, outbuf) if junk is None else junk
        S.op("act", lambda e: e.activation(jap, xap, AF.Square, accum_out=st[0:P, 0:1]), reads=xbufs, writes=[jbuf, stb])
        S.op("act", lambda e: e.activation(st[0:P, 1:2], st[0:P, 0:1], AF.Ln, scale=1.0 / D, bias=epsc[0:P, :]), reads=[stb, bufs["epsc"]], writes=[stb])
        S.op("act", lambda e: e.activation(st[0:P, 2:3], st[0:P, 1:2], AF.Exp, scale=-0.5), reads=[stb], writes=[stb])
        if gb is None:
            S.op("dve", lambda e: e.tensor_scalar(outap, xap, st[0:P, 2:3], None, ALU.mult), reads=xbufs + [stb], writes=[outbuf])
        else:
            S.op("dve", lambda e: e.scalar_tensor_tensor(outap, xap, st[0:P, 2:3], gb[0:P, :], ALU.mult, ALU.mult),
                 reads=xbufs + [stb, gbuf[id(gb)]], writes=[outbuf])

    epsc = T("epsc", [128, 1])
    onec = T("onec", [128, 1])
    S.op("pool", lambda e: e.memset(epsc[:], EPS), writes=B("epsc"))
    S.op("pool", lambda e: e.memset(onec[:], 1.0), writes=B("onec"))
    gbuf = {id(g1b): bufs["g1b"], id(g2b): bufs["g2b"], id(gfb): bufs["gfb"]}

    def transpose_to(src_bf, P, nchunk, dst_fn, srcbuf, dstbuf, evac_eng, gcm=None):
        bk, bb = nextbank()
        pv = bk[:, :].bitcast(BF16).rearrange("p (c n) -> p c n", c=8)

        def f(e):
            ins = None
            for c in range(nchunk):
                ins = e.transpose(pv[:, c, 0:P], src_bf[0:P, c * 128:(c + 1) * 128], identb[0:P, 0:P])
            return ins
        S.op("pe", f, reads=[srcbuf, bufs["identb"]], writes=[bb])
        if gcm is not None:
            S.op("dve", lambda e: e.tensor_tensor(dst_fn(), pv[:, 0:nchunk, 0:P], gcm[:, 0:nchunk].unsqueeze(2).to_broadcast([128, nchunk, P]), ALU.mult),
                 reads=[bb, gbuf[id(gcm)]], writes=[dstbuf])
        elif evac_eng == "act":
            S.op("act", lambda e: e.copy(dst_fn(), pv[:, 0:nchunk, 0:P]), reads=[bb], writes=[dstbuf])
        else:
            S.op("dve", lambda e: e.tensor_copy(dst_fn(), pv[:, 0:nchunk, 0:P]), reads=[bb], writes=[dstbuf])

    def sigmoid_from(src_ap, P, N, srcbufs, scale_neg, bias_neg):
        t, tb = rot("tmp", tmpA, "tmpA")
        kw = {}
        if bias_neg is not None:
            kw["bias"] = bias_neg
        S.op("act", lambda e: e.activation(t[0:P, 0:N], src_ap, AF.Exp, scale=scale_neg, **kw), reads=srcbufs, writes=[tb])
        S.op("act", lambda e: e.activation(t[0:P, 0:N], t[0:P, 0:N], AF.Ln, bias=onec[0:P, :]), reads=[tb, bufs["onec"]], writes=[tb])
        S.op("act", lambda e: e.activation(t[0:P, 0:N], t[0:P, 0:N], AF.Exp, scale=-1.0), reads=[tb], writes=[tb])
        return t, tb

    def attn_A(par, q_ap_fn, kT_ap_fn, qbufs, kbufs, first_mask, rowmask_col, ncol=256):
        Sx = Ssb2[par]
        sts = []
        for hp in range(4):
            sbk, sbb = nextbank()
            heads = [2 * hp, 2 * hp + 1]
            sxb = bufs[f"Ssb{par}_{hp}"]

            def fS(e, sbk=sbk, heads=heads):
                ins = None
                for i, h in enumerate(heads):
                    kvh, g = h // 4, h % 4
                    ins = e.matmul(sbk[:, i * 256:i * 256 + ncol], q_ap_fn(g, kvh), kT_ap_fn(kvh), start=True, stop=True)
                return ins
            S.op("pe", fS, reads=qbufs + kbufs, writes=[sbb])
            S.op("dve", lambda e, sbk=sbk, hp=hp, Sx=Sx: e.tensor_tensor(Sx[:, 2 * hp:2 * hp + 2, 0:ncol], sbk[:, :].rearrange("p (a b) -> p a b", a=2)[:, :, 0:ncol],
                                                                         bias[:, 2 * hp:2 * hp + 2, 0:ncol], ALU.add),
                 reads=[sbb, bufs["bias"]], writes=[sxb])
            if ncol < 256:
                S.op("dve", lambda e, hp=hp, Sx=Sx: e.tensor_copy(Sx[:, 2 * hp:2 * hp + 2, ncol], sinkb[:, 2 * hp:2 * hp + 2]),
                     reads=B("sinkb"), writes=[sxb])
            if first_mask:
                for h in heads:
                    S.op("dve", lambda e, h=h, Sx=Sx: e.tensor_tensor(Sx[:, h, 0:256], Sx[:, h, 0:256], kmask[:, :], ALU.add),
                         reads=[sxb, bufs["kmask"]], writes=[sxb])
            st, stb = nextst()
            S.op("dve", lambda e, hp=hp, st=st, Sx=Sx: e.tensor_reduce(st[:, 0:2], Sx[:, 2 * hp:2 * hp + 2, 0:ncol + 1], AX.X, ALU.max),
                 reads=[sxb], writes=[stb])
            if rowmask_col is None:
                S.op("dve", lambda e, st=st: e.tensor_scalar(st[:, 2:4], st[:, 0:2], -1.0, None, ALU.mult), reads=[stb], writes=[stb])
            else:
                S.op("dve", lambda e, st=st: e.tensor_scalar(st[:, 2:4], st[:, 0:2], -1.0, rowmask_col, ALU.mult, ALU.add),
                     reads=[stb, bufs["rowm"]], writes=[stb])
            sts.append((st, stb))
        return sts

    def attn_B(par, sts, v_ap_fn, vbufs, out_mode, out_state, filler=None, ncol=256):
        Sx = Ssb2[par]
        if out_mode == "prompt":
            at, atb = rot("attn", attn, "attn")
            atv = at[:, :].rearrange("p (g k d) -> p g k d", g=4, k=2)
        pbs = {}
        pts = {}

        def do_exp(hp):
            st, stb = sts[hp]
            sxb = bufs[f"Ssb{par}_{hp}"]
            pb, pbb = rot("P", Pb, "Pb")
            pbs[hp] = (pb, pbb)
            for i, h in enumerate((2 * hp, 2 * hp + 1)):
                if out_mode == "prompt":
                    acc_ap = st[:, i:i + 1]
                    wr = [pbb, stb]
                else:
                    c = out_state["s"] * 8 + h
                    acc_ap = out_state["smat"][:, c:c + 1]
                    wr = [pbb, out_state["smatb"]]
                S.op("act", lambda e, pb=pb, h=h, st=st, i=i, acc_ap=acc_ap: e.activation(pb[:, i, 0:ncol + 1], Sx[:, h, 0:ncol + 1], AF.Exp,
                                                                                         bias=st[:, 2 + i:3 + i], accum_out=acc_ap),
                     reads=[sxb, stb], writes=wr)

        def do_T(hp):
            pb, pbb = pbs[hp]
            tbk, tbb = nextbank()
            tv = tbk[:, :].bitcast(BF16).rearrange("p (c n) -> p c n", c=8)

            def fT(e, pb=pb, tv=tv):
                ins = None
                w1 = ncol - 128
                for i in range(2):
                    ins = e.transpose(tv[:, i * 2, :], pb[:, i, 0:128], identb[:, :])
                    ins = e.transpose(tv[0:w1, i * 2 + 1, :], pb[:, i, 128:ncol], identb[:, :])
                return ins
            S.op("pe", fT, reads=[pbb, bufs["identb"]], writes=[tbb])
            pt, ptb = rot("PT", PTs, "PTs")
            pts[hp] = (pt, ptb)
            if ncol == 256:
                S.op("act", lambda e, pt=pt, tv=tv: e.copy(pt[:, :, :], tv[:, 0:4, :]), reads=[tbb], writes=[ptb])
            else:
                w1 = ncol - 128
                tv2 = tv[:, 0:4, :].rearrange("p (i h) n -> p i h n", h=2)
                pt2 = pt[:, :, :].rearrange("p (i h) n -> p i h n", h=2)

                def fCp(e, pt2=pt2, tv2=tv2, w1=w1):
                    e.copy(pt2[:, :, 0, :], tv2[:, :, 0, :])
                    return e.copy(pt2[0:w1, :, 1, :], tv2[0:w1, :, 1, :])
                S.op("act", fCp, reads=[tbb], writes=[ptb])

        def do_PV(hp):
            st, stb = sts[hp]
            pt, ptb = pts[hp]
            kvh = (2 * hp) // 4
            g0 = (2 * hp) % 4
            if out_mode == "prompt":
                obk, obb = nextbank()

                def fO(e, pt=pt, obk=obk, kvh=kvh):
                    ins = None
                    for i in range(2):
                        for half in range(2):
                            ins = e.matmul(obk[:, i * 64:(i + 1) * 64], pt[:, i * 2 + half, :], v_ap_fn(half, kvh), start=(half == 0), stop=(half == 1))
                    return ins
                S.op("pe", fO, reads=[ptb] + vbufs, writes=[obb])
                S.op("dve", lambda e, st=st: e.reciprocal(st[:, 0:2], st[:, 0:2]), reads=[stb], writes=[stb])
                S.op("dve", lambda e, obk=obk, st=st, g0=g0, kvh=kvh: e.tensor_tensor(
                    atv[:, g0:g0 + 2, kvh, :], obk[:, 0:128].rearrange("p (a d) -> p a d", a=2),
                    st[:, 0:2].unsqueeze(2).to_broadcast([128, 2, 64]), ALU.mult),
                    reads=[obb, stb], writes=[atb])
            else:
                s_ = out_state["s"]
                ob = out_state["obank"][kvh]
                obb = out_state["obankb"][kvh]

                def fO2(e, pt=pt, ob=ob, g0=g0, kvh=kvh, s_=s_):
                    ins = None
                    w1 = ncol - 128
                    for i in range(2):
                        c0 = (g0 + i) * 64
                        ins = e.matmul(ob[:, c0:c0 + 64], pt[:, i * 2, :], v_ap_fn(0, kvh), start=False, stop=False)
                        ins = e.matmul(ob[:, c0:c0 + 64], pt[0:w1, i * 2 + 1, :], v_ap_fn(1, kvh)[0:w1, :],
                                       start=False, stop=(s_ == NS - 1 and (g0 + i) == 3 and kvh == 1))
                    return ins
                S.op("pe", fO2, reads=[ptb] + vbufs, writes=[obb])

        def step():
            if filler is not None:
                filler()

        do_exp(0)
        do_exp(1)
        yield
        do_T(0)
        do_exp(2)
        step()
        yield
        do_T(1)
        do_PV(0)
        do_exp(3)
        step()
        yield
        do_T(2)
        do_PV(1)
        step()
        yield
        do_T(3)
        do_PV(2)
        step()
        yield
        do_PV(3)
        yield
        if out_mode == "prompt":
            return at, atb
        return None, None

    def conv_diag(cc):
        dg, dgb = rot("diag", diag, "diag")
        S.op("dve", lambda e, dg=dg: e.tensor_tensor(dg[:, :, :], identb[:, :].unsqueeze(1).to_broadcast([128, 31, 128]),
                                                     cwcmb[:, cc, :].unsqueeze(2).to_broadcast([128, 31, 128]), ALU.mult),
             reads=B("identb", "cwcmb"), writes=[dgb])
        return dg, dgb

    def glu(a_ap, b_ap, P, N, abufs, bbufs, out_ap, outbufs):
        sg, sgb = sigmoid_from(b_ap, P, N, bbufs, -1.0, None)
        S.op("dve", lambda e: e.tensor_tensor(out_ap, a_ap, sg[0:P, 0:N], ALU.mult), reads=abufs + [sgb], writes=outbufs)

    def win_fm(rhs_fn, N, col0, rbufs):
        bk, bb = nextbank()

        def f(e):
            ins = None
            for kc in range(8):
                ins = e.matmul(bk[:, 0:N], win[:, kc, col0:col0 + 128], rhs_fn(kc), start=(kc == 0), stop=(kc == 7))
            return ins
        S.op("pe", f, reads=[bufs["win"]] + rbufs, writes=[bb])
        return bk, bb

    def win_tm(lhs_fn, P, col0, ncol, lbufs):
        bk, bb = nextbank()

        def f(e):
            ins = None
            for kc in range(8):
                ins = e.matmul(bk[0:P, 0:ncol], lhs_fn(kc), win[:, kc, col0:col0 + ncol], start=(kc == 0), stop=(kc == 7))
            return ins
        S.op("pe", f, reads=[bufs["win"]] + lbufs, writes=[bb])
        return bk, bb

    CQ, CK, CV, CA, CB = 0, 512, 640, 768, 1280

    def conv_tail(N, dst_fn, dstbuf):
        mbk, mbb = nextbank()

        def fM(e, mbk=mbk):
            ins = None
            for cc in range(4):
                ins = e.matmul(mbk[:, 0:N], onesf[:, :], cch[:, cc, 0:N], start=(cc == 0), stop=(cc == 3))
            return ins
        S.op("pe", fM, reads=B("onesf", "cch"), writes=[mbb])
        for cc in range(4):
            S.op("dve", lambda e, mbk=mbk, cc=cc: e.scalar_tensor_tensor(cch[:, cc, 0:N], mbk[:, 0:N], -1.0 / 512, cch[:, cc, 0:N], ALU.mult, ALU.add),
                 reads=[mbb] + B("cch"), writes=B("cch"))
        vbk, vbb = nextbank()
        for cc in range(4):
            tq, tqb = rot("tmp", tmpA, "tmpA")
            S.op("act", lambda e, tq=tq, cc=cc: e.activation(tq[:, 0:N], cch[:, cc, 0:N], AF.Square), reads=B("cch"), writes=[tqb])
            S.op("pe", lambda e, tq=tq, cc=cc, vbk=vbk: e.matmul(vbk[:, 0:N], onesf[:, :], tq[:, 0:N], start=(cc == 0), stop=(cc == 3)),
                 reads=[tqb, bufs["onesf"]], writes=[vbb])
        S.op("act", lambda e, vbk=vbk: e.activation(rstdb[:, 0:N], vbk[:, 0:N], AF.Ln, scale=1.0 / 512, bias=epsc[:, :]), reads=[vbb, bufs["epsc"]], writes=B("rstdb"))
        S.op("act", lambda e: e.activation(rstdb[:, 0:N], rstdb[:, 0:N], AF.Exp, scale=-0.5), reads=B("rstdb"), writes=B("rstdb"))
        for cc in range(4):
            S.op("dve", lambda e, cc=cc: e.tensor_tensor(cch[:, cc, 0:N], cch[:, cc, 0:N], rstdb[:, 0:N], ALU.mult), reads=B("cch", "rstdb"), writes=B("cch"))
            S.op("dve", lambda e, cc=cc: e.tensor_scalar(cch[:, cc, 0:N], cch[:, cc, 0:N], cvec[:, 4 + cc:5 + cc], cvec[:, 8 + cc:9 + cc], ALU.mult, ALU.add),
                 reads=B("cch", "cvec"), writes=B("cch"))
            sg, sgb = sigmoid_from(cch[:, cc, 0:N], 128, N, B("cch"), -1.0, None)
            S.op("dve", lambda e, cc=cc, sg=sg: e.tensor_tensor(dst_fn(cc), cch[:, cc, 0:N], sg[:, 0:N], ALU.mult), reads=B("cch") + [sgb], writes=[dstbuf])

    def wout_block(mix_fn, P, xres_ap, xresbufs, mixbufs):
        for half in range(2):
            bk_, bb_ = nextbank()

            def f(e, bk_=bk_, half=half):
                ins = None
                for kc in range(8):
                    ins = e.matmul(bk_[0:P, :], mix_fn(kc), wout[:, kc, half * 512:(half + 1) * 512], start=(kc == 0), stop=(kc == 7))
                return ins
            S.op("pe", f, reads=mixbufs + B("wout"), writes=[bb_])
            S.op("dve", lambda e, bk_=bk_, half=half: e.tensor_tensor(xres_ap[:, half * 512:(half + 1) * 512], xres_ap[:, half * 512:(half + 1) * 512],
                                                                     bk_[0:P, :], ALU.add), reads=[bb_] + xresbufs, writes=xresbufs)

    stage(2)
    xh, xhb = xr[:, 2, :], xrb[2]
    S.dma("sp", [(xh, xin[0:128, :])], writes=[xhb])
    hh, hhb = rot("hb", hb, "hb")
    rmsnorm(xh, 128, [xhb], None, hh[:, :], hhb)
    transpose_to(hh, 128, 8, lambda: hT[:, :, 0:128], hhb, bufs["hT"], "dve", gcm=g1b)
    bk, bb = win_fm(lambda kc: hT[:, kc, 0:128], 128, CK, B("hT"))
    S.op("act", lambda e, bk=bk: e.copy(kT[:, 0:128], bk[:, 0:128]), reads=[bb], writes=B("kT"))
    bk, bb = win_tm(lambda kc: hT[:, kc, 0:128], 128, CV, 128, B("hT"))
    S.op("dve", lambda e, bk=bk: e.tensor_copy(vtok[:, 0, :], bk[:, 0:128]), reads=[bb], writes=B("vtok"))
    for cc in range(4):
        ak, ab = win_fm(lambda kc: hT[:, kc, 0:128], 128, CA + cc * 128, B("hT"))
        bk2, bb2 = win_fm(lambda kc: hT[:, kc, 0:128], 128, CB + cc * 128, B("hT"))
        sg, sgb = sigmoid_from(bk2[:, 96:128], 128, 32, [bb2], -1.0, None)
        S.op("dve", lambda e, ak=ak, sg=sg, cc=cc: e.tensor_tensor(uT[:, cc, 0:32], ak[:, 96:128], sg[:, 0:32], ALU.mult),
             reads=[ab, sgb], writes=B("uT"))

    stage(3)
    xs_t = T("xs_t", [NS, D])
    S.dma("sp", [(xs_t[:], xs_d)], writes=B("xs_t"))
    hs, hsb = rot("hb", hb, "hb")
    rmsnorm(xs_t[:, :], NS, B("xs_t"), None, hs[0:NS, :], hsb)
    hsT = T("hsT", [128, 8, NS], BF16)
    transpose_to(hs, NS, 8, lambda: hsT[:, :, :], hsb, bufs["hsT"], "dve", gcm=g1b)
    qsT = T("qsT", [128, 4, 128], BF16)
    S.op("pool", lambda e: e.memset(qsT[:], 0.0), writes=B("qsT"))
    for g in range(4):
        bk, bb = win_fm(lambda kc: hsT[:, kc, :], NS, CQ + g * 128, B("hsT"))
        S.op("act", lambda e, bk=bk, g=g: e.activation(qsT[:, g, 0:NS], bk[:, 0:NS], AF.Copy, scale=0.125), reads=[bb], writes=B("qsT"))
    usT = T("usT", [128, 4, NS])
    for cc in range(4):
        ak, ab = win_fm(lambda kc: hsT[:, kc, :], NS, CA + cc * 128, B("hsT"))
        bk2, bb2 = win_fm(lambda kc: hsT[:, kc, :], NS, CB + cc * 128, B("hsT"))
        glu(ak[:, 0:NS], bk2[:, 0:NS], 128, NS, [ab], [bb2], usT[:, cc, :], B("usT"))
    kvs_f = T("kvs_f", [NS, 256]); us_f = T("us_f", [NS, 512])
    bk, bb = win_tm(lambda kc: hsT[:, kc, :], NS, CK, 256, B("hsT"))
    S.op("dve", lambda e, bk=bk: e.tensor_copy(kvs_f[:, :], bk[0:NS, 0:256]), reads=[bb], writes=B("kvs_f"))
    ak, ab = win_tm(lambda kc: hsT[:, kc, :], NS, CA, 512, B("hsT"))
    bk2, bb2 = win_tm(lambda kc: hsT[:, kc, :], NS, CB, 512, B("hsT"))
    S.op("act", lambda e, bk2=bk2: e.activation(utok[0:NS, :], bk2[0:NS, :], AF.Exp, scale=-1.0), reads=[bb2], writes=B("utok"))
    S.op("act", lambda e: e.activation(utok[0:NS, :], utok[0:NS, :], AF.Ln, bias=onec[0:NS, :]), reads=B("utok", "onec"), writes=B("utok"))
    S.op("act", lambda e: e.activation(utok[0:NS, :], utok[0:NS, :], AF.Exp, scale=-1.0), reads=B("utok"), writes=B("utok"))
    S.op("dve", lambda e, ak=ak: e.tensor_tensor(us_f[:, :], ak[0:NS, :], utok[0:NS, :], ALU.mult), reads=[ab] + B("utok"), writes=B("us_f"))
    bias_flip()
    stage(4)
    ob1 = Buf("ob1"); bufs["ob1"] = ob1
    S.dma("sp", [(nks_d[:, 0:127, :], ck_d[:, 1:128, :]), (nvs_d[:, 0:127, :], cv_d[:, 1:128, :]),
                 (ncs_d[:, 0:29, :], sc_d[:, 1:30, :])], writes=[ob1])
    S.dma("sp", [(nks_d[:, 127, :], kvs_f[:, 0:128]), (nvs_d[:, 127, :], kvs_f[:, 128:256])], reads=B("kvs_f"), sem_buf=bufs["kvs_f"])
    S.dma("sp", [(ncs_d[:, 29, :], us_f[:, :])], reads=B("us_f"), sem_buf=bufs["us_f"])

    stage(5)
    wsc = [Buf(f"wsc{i}") for i in range(32)]
    wchain = Buf("wchain"); bufs["wchain"] = wchain
    for fg in range(16):
        S.dma("pool", [(wup_s[fg * 128:(fg + 1) * 128, :], wup_d[fg * 128:(fg + 1) * 128, :])], reads=B("kT"), writes=[wsc[fg]])
        S.dma("pool", [(wdn_s[fg * 128:(fg + 1) * 128, :], wdn_d[fg * 128:(fg + 1) * 128, :])], reads=B("kT"), writes=[wsc[16 + fg]])

    stage(6)
    stA = arena[0:96, 0:5120].bitcast(F32)
    cwrep = arena[0:96, 5120:10240].bitcast(F32)
    part = arena[0:96, 10240:11264].bitcast(F32)
    for n_ in ("stA", "cwrep", "part"):
        bufs[n_] = Buf(n_)
    arena_alias = B("stA", "cwrep", "part")
    ind = T("ind", [96, NS])
    S.dma("sp", [(stA, sc_d.rearrange("s (a t) c -> (s a) (t c)", a=6)), (cwrep, cwrep_d), (ind[:], ind_d)],
          writes=B("stA", "cwrep", "ind", "biasT"), sem_buf=bufs["stA"])
    S.op("dve", lambda e: e.tensor_tensor(stA, stA, cwrep, ALU.mult), reads=B("stA", "cwrep"), writes=B("stA"))
    S.op("dve", lambda e: e.tensor_reduce(part, stA.rearrange("p (t c) -> p c t", t=5), AX.X, ALU.add), reads=B("stA"), writes=B("part"))
    for cc in range(4):
        cbk, cbb = nextbank()
        S.op("pe", lambda e, cbk=cbk, cc=cc: e.matmul(cbk[:, 0:NS], part[:, cc * 128:(cc + 1) * 128], ind[:, :], start=True, stop=True),
             reads=B("ind", "part"), writes=[cbb])
        S.op("dve", lambda e, cbk=cbk, cc=cc: e.scalar_tensor_tensor(cch[:, cc, 0:NS], usT[:, cc, :], cwcm[:, cc, 30:31], cbk[:, 0:NS], ALU.mult, ALU.add),
             reads=[cbb] + B("usT", "cwcm", "cch"), writes=B("cch"))
        S.op("dve", lambda e, cc=cc: e.tensor_scalar(cch[:, cc, 0:NS], cch[:, cc, 0:NS], cvec[:, cc:cc + 1], None, ALU.add), reads=B("cch", "cvec"), writes=B("cch"))
    mixsT = T("mixsT", [128, 8, 128], BF16)
    conv_tail(NS, lambda cc: mixsT[:, 4 + cc, 0:NS], bufs["mixsT"])

    def sample_gen():
        NFR = 6
        kvfr = [T(f"kvfr{i}", [128, 2, 256], BF16) for i in range(NFR)]
        for i in range(NFR):
            S.op("pool", lambda e, i=i: e.memset(kvfr[i][:], 0.0), writes=[bufs[f"kvfr{i}"]])
        smat = T("smat", [128, NS * 8])
        S.op("pool", lambda e: e.memset(smat[:], 0.0), writes=B("smat"))
        NKT = 4
        kTs = [T(f"kTs{i}", [128, 256], BF16) for i in range(NKT)]
        obk_, obb_ = nextbank()
        reserved.add(bank_index(obk_))
        ostate = {"smat": smat, "smatb": bufs["smat"], "obank": [obk_[:, 0:256], obk_[:, 256:512]], "obankb": [obb_, obb_], "s": 0}
        zb = T("zb", [128, 512], BF16)
        S.op("pool", lambda e: e.memset(zb[:], 0.0), writes=B("zb"))
        S.op("pe", lambda e: e.matmul(obk_[:, :], zb[:, 0:128], zb[:, :], start=True, stop=False), reads=B("zb"), writes=[obb_])

        def prep(s):
            fr = kvfr[s % NFR]; frb = bufs[f"kvfr{s % NFR}"]
            n0 = 128 - s
            prs = [(fr[s:128, 0, 0:128], ck_d[s, 0:n0, :]), (fr[s:128, 0, 128:256], cv_d[s, 0:n0, :]),
                   (fr[s:s + 1, 1, :], kvs_f[s:s + 1, :])]
            if s > 0:
                prs += [(fr[0:s, 1, 0:128], ck_d[s, n0:128, :]), (fr[0:s, 1, 128:256], cv_d[s, n0:128, :])]
            S.dma("pool", prs, reads=B("kvs_f"), writes=[frb])
            kt = kTs[s % NKT]; ktb = bufs[f"kTs{s % NKT}"]
            tbk, tbb = nextbank()
            tv = tbk[:, :].bitcast(BF16).rearrange("p (c n) -> p c n", c=8)

            def fK(e, fr=fr, tv=tv):
                ins = None
                for slot in range(2):
                    ins = e.transpose(tv[:, slot, :], fr[:, slot, 0:128], identb[:, :])
                return ins
            S.op("pe", fK, reads=[frb, bufs["identb"]], writes=[tbb])
            S.op("dve", lambda e, kt=kt, tv=tv: e.tensor_copy(kt[:, :].rearrange("p (a b) -> p a b", a=2), tv[:, 0:2, :]), reads=[tbb], writes=[ktb])

        def samp_A(s):
            kt = kTs[s % NKT]; ktb = bufs[f"kTs{s % NKT}"]
            return attn_A(s % 2, lambda g, kvh: qsT[kvh * 64:(kvh + 1) * 64, g, :],
                          lambda kvh, kt=kt: kt[kvh * 64:(kvh + 1) * 64, 0:SNC],
                          B("qsT"), [ktb], False, rowm[:, s:s + 1], ncol=SNC)

        def samp_B(s, sts):
            st_ = dict(ostate); st_["s"] = s
            fr = kvfr[s % NFR]; frb = bufs[f"kvfr{s % NFR}"]
            yield from attn_B(s % 2, sts, lambda half, kvh, fr=fr: fr[:, half, 128 + kvh * 64:128 + (kvh + 1) * 64], [frb], "sample", st_, ncol=SNC)

        for s0 in range(3):
            prep(s0)
            yield
        pend = samp_A(0)
        yield
        for s in range(NS):
            if s + 3 < NS:
                prep(s + 3)
                yield
            nxt = samp_A(s + 1) if s + 1 < NS else None
            yield
            yield from samp_B(s, pend)
            pend = nxt
        reserved.discard(bank_index(obk_))
        smt = T("smt", [128, 8])
        S.op("dve", lambda e: e.tensor_reduce(smt[:, :], smat[:, :].rearrange("p (s h) -> p h s", h=8), AX.X, ALU.add), reads=B("smat"), writes=B("smt"))
        S.op("dve", lambda e: e.tensor_scalar(smt[:, :], smt[:, :], 1e-30, None, ALU.add), reads=B("smt"), writes=B("smt"))
        S.op("dve", lambda e: e.reciprocal(smt[:, :], smt[:, :]), reads=B("smt"), writes=B("smt"))
        ats, atsb = rot("attn", attn, "attn")
        for h in range(8):
            kvh, g = h // 4, h % 4
            ob = ostate["obank"][kvh]; obb = ostate["obankb"][kvh]
            col = g * 128 + kvh * 64
            S.op("act", lambda e, ob=ob, g=g, h=h, col=col: e.activation(ats[:, col:col + 64], ob[:, g * 64:(g + 1) * 64], AF.Copy, scale=smt[:, h:h + 1]),
                 reads=[obb, bufs["smt"]], writes=[atsb])
        yield
        transpose_to(ats, 128, 4, lambda: mixsT[:, 0:4, :], atsb, bufs["mixsT"], "dve")
        yield

        wout_block(lambda kc: mixsT[:, kc, 0:NS], NS, xs_t[:, :], B("xs_t"), B("mixsT"))
        yield
        h2s, h2sb = rot("hb", hb, "hb")
        rmsnorm(xs_t[:, :], NS, B("xs_t"), None, h2s[0:NS, :], h2sb)
        transpose_to(h2s, NS, 8, lambda: actT2[(NTILE - 1) % 2][:, :, TT:TT + NS], h2sb, bufs[f"actT{(NTILE - 1) % 2}"], "dve", gcm=g2b)


    stage(9)
    first_ffn = [True]
    prev_acc = [set()]
    ysems = [Buf(f"ysem{i}") for i in range(2 * NB)]
    xloaded = {}

    def load_x(t_):
        base_ = 128 + t_ * TT
        for b_ in range(NB):
            sl = (t_ % 2) * NB + b_
            S.dma("sp", [(xr[:, sl, :], xin[base_ + b_ * 128: base_ + (b_ + 1) * 128, :])], writes=[xrb[sl]])
            xloaded[(t_, b_)] = sl
    def conv_tail_gen(N, dst_fn, dstbuf):
        mbk, mbb = nextbank()

        def fM(e, mbk=mbk):
            ins = None
            for cc in range(4):
                ins = e.matmul(mbk[:, 0:N], onesf[:, :], cch[:, cc, 0:N], start=(cc == 0), stop=(cc == 3))
            return ins
        S.op("pe", fM, reads=B("onesf", "cch"), writes=[mbb])
        for cc in range(4):
            S.op("dve", lambda e, mbk=mbk, cc=cc: e.scalar_tensor_tensor(cch[:, cc, 0:N], mbk[:, 0:N], -1.0 / 512, cch[:, cc, 0:N], ALU.mult, ALU.add),
                 reads=[mbb] + B("cch"), writes=B("cch"))
        yield
        vbk, vbb = nextbank()
        vi = bank_index(vbk)
        reserved.add(vi)
        for cc in range(4):
            tq, tqb = rot("tmp", tmpA, "tmpA")
            S.op("act", lambda e, tq=tq, cc=cc: e.activation(tq[:, 0:N], cch[:, cc, 0:N], AF.Square), reads=B("cch"), writes=[tqb])
            S.op("pe", lambda e, tq=tq, cc=cc, vbk=vbk: e.matmul(vbk[:, 0:N], onesf[:, :], tq[:, 0:N], start=(cc == 0), stop=(cc == 3)),
                 reads=[tqb, bufs["onesf"]], writes=[vbb])
            if cc == 3:
                reserved.discard(vi)
            yield
        S.op("act", lambda e, vbk=vbk: e.activation(rstdb[:, 0:N], vbk[:, 0:N], AF.Ln, scale=1.0 / 512, bias=epsc[:, :]), reads=[vbb, bufs["epsc"]], writes=B("rstdb"))
        S.op("act", lambda e: e.activation(rstdb[:, 0:N], rstdb[:, 0:N], AF.Exp, scale=-0.5), reads=B("rstdb"), writes=B("rstdb"))
        yield
        for cc in range(4):
            S.op("dve", lambda e, cc=cc: e.tensor_tensor(cch[:, cc, 0:N], cch[:, cc, 0:N], rstdb[:, 0:N], ALU.mult), reads=B("cch", "rstdb"), writes=[cchb[cc]])
            S.op("dve", lambda e, cc=cc: e.tensor_scalar(cch[:, cc, 0:N], cch[:, cc, 0:N], cvec[:, 4 + cc:5 + cc], cvec[:, 8 + cc:9 + cc], ALU.mult, ALU.add),
                 reads=[cchb[cc], bufs["cvec"]], writes=[cchb[cc]])
        yield
        sgs = []
        for cc in range(4):
            if cc == 3:
                yield
            sgs.append(sigmoid_from(cch[:, cc, 0:N], 128, N, [cchb[cc]], -1.0, None))
            if cc >= 1:
                c2 = cc - 1
                sg, sgb = sgs[c2]
                S.op("dve", lambda e, c2=c2, sg=sg: e.tensor_tensor(dst_fn(c2), cch[:, c2, 0:N], sg[:, 0:N], ALU.mult), reads=[cchb[c2], sgb], writes=[dstbuf])
        yield
        sg, sgb = sgs[3]
        S.op("dve", lambda e, sg=sg: e.tensor_tensor(dst_fn(3), cch[:, 3, 0:N], sg[:, 0:N], ALU.mult), reads=[cchb[3], sgb], writes=[dstbuf])
        S.op("dve", lambda e: e.tensor_copy(zcol[:, :], zcol[:, :]), reads=cchb, writes=B("cch", "zcol"))
        yield

    def aphase(t):
        last = (t == NTILE - 1)
        dgs = {0: conv_diag(0), 1: conv_diag(1)}
        hs_ = []
        for b in range(NB):
            sl = xloaded[(t, b)]
            h_, hbb = rot("hb", hb, "hb")
            rmsnorm(xr[:, sl, :], 128, [xrb[sl]], None, h_[:, :], hbb)
            hs_.append((h_, hbb))
        for _ in range(6):
            yield
        for b in range(NB):
            h_, hbb = hs_[b]
            transpose_to(h_, 128, 8, lambda b=b: hT[:, :, b * 128:(b + 1) * 128], hbb, bufs["hT"], "dve", gcm=g1b)
            yield
        yield
        for g in range(4):
            bk, bb = win_fm(lambda kc: hT[:, kc, 0:TT], TT, CQ + g * 128, B("hT"))
            S.op("act", lambda e, bk=bk, g=g: e.activation(qT[:, g, :], bk[:, 0:TT], AF.Copy, scale=0.125), reads=[bb], writes=B("qT"))
            yield
        bk, bb = win_fm(lambda kc: hT[:, kc, 0:TT], TT, CK, B("hT"))
        S.op("act", lambda e, bk=bk: e.copy(kT[:, 128:128 + TT], bk[:, 0:TT]), reads=[bb], writes=B("kT"))
        yield
        for b in range(NB):
            bk, bb = win_tm(lambda kc, b=b: hT[:, kc, b * 128:(b + 1) * 128], 128, CK, 256, B("hT"))
            S.op("dve", lambda e, bk=bk, b=b: e.tensor_copy(vtok[:, 1 + b, :], bk[:, 128:256]), reads=[bb], writes=B("vtok"))
            if last and b == NB - 1:
                S.op("dve", lambda e, bk=bk: e.tensor_copy(kvtok, bk[:, 0:256]), reads=[bb], writes=B("kvtok"))
                S.dma("sp", [(nk_d, kvtok[:, 0:128]), (nv_d, kvtok[:, 128:256])], reads=B("kvtok"), sem_buf=bufs["kvtok"])
            yield
        for cc in range(4):
            ak, ab = win_fm(lambda kc: hT[:, kc, 0:TT], TT, CA + cc * 128, B("hT"))
            bk2, bb2 = win_fm(lambda kc: hT[:, kc, 0:TT], TT, CB + cc * 128, B("hT"))
            glu(ak[:, 0:TT], bk2[:, 0:TT], 128, TT, [ab], [bb2], uT[:, cc, 32:32 + TT], B("uT"))
            yield
        if last:
            ak, ab = win_tm(lambda kc: hT[:, kc, TT - 128:TT], 128, CA, 512, B("hT"))
            bk2, bb2 = win_tm(lambda kc: hT[:, kc, TT - 128:TT], 128, CB, 512, B("hT"))
            for half in range(2):
                hs_ = slice(half * 256, (half + 1) * 256)
                sg, sgb = sigmoid_from(bk2[:, hs_], 128, 256, [bb2], -1.0, None)
                S.op("dve", lambda e, ak=ak, sg=sg, hs_=hs_: e.tensor_tensor(utok[:, hs_], ak[:, hs_], sg[:, 0:256], ALU.mult), reads=[ab, sgb], writes=B("utok"))
            S.dma("sp", [(ncv_d, utok[:, :])], reads=B("utok"), sem_buf=bufs["utok"])
            yield

        def pr_A(b, t=t):
            return attn_A(b % 2, lambda g, kvh, b=b: qT[kvh * 64:(kvh + 1) * 64, g, b * 128:(b + 1) * 128],
                          lambda kvh, b=b: kT[kvh * 64:(kvh + 1) * 64, b * 128:b * 128 + 256],
                          B("qT"), B("kT"), (t == 0 and b == 0), None)

        def conv_chunks():
            for cc in range(4):
                dg, dgb = dgs[cc]
                bk, bb = nextbank()
                bi = bank_index(bk)
                reserved.add(bi)
                for j0, j1 in ((0, 16), (16, 31)):
                    def fC(e, bk=bk, dg=dg, cc=cc, j0=j0, j1=j1):
                        ins = None
                        for j in range(j0, j1):
                            ins = e.matmul(bk[:, 0:TT], dg[:, j, :], uT[:, cc, 2 + j:2 + j + TT], start=(j == 0), stop=(j == 30))
                        return ins
                    S.op("pe", fC, reads=[dgb, bufs["uT"]], writes=[bb])
                    if j1 == 31:
                        reserved.discard(bi)
                        S.op("act", lambda e, bk=bk, cc=cc: e.activation(cch[:, cc, 0:TT], bk[:, 0:TT], AF.Identity, bias=cvec[:, cc:cc + 1]),
                             reads=[bb, bufs["cvec"]], writes=B("cch"))
                        if cc + 2 < 4:
                            dgs[cc + 2] = conv_diag(cc + 2)
                    yield

        cgen = conv_chunks()
        yield "ATTN"
        pend = pr_A(0)
        yield
        for b in range(NB):
            nxt = pr_A(b + 1) if b < NB - 1 else None
            yield
            bgen = attn_B(b % 2, pend, lambda half, kvh, b=b: vtok[:, b + half, kvh * 64:(kvh + 1) * 64], B("vtok"), "prompt", None,
                          filler=lambda: next(cgen, None))
            res = None
            while True:
                try:
                    next(bgen)
                    yield
                except StopIteration as stp:
                    res = stp.value
                    break
            at, atb = res
            transpose_to(at, 128, 4, lambda b=b: mixT[:, 0:4, b * 128:(b + 1) * 128], atb, bufs["mixT"], "dve")
            pend = nxt
            yield
        for _ in cgen:
            yield
        yield from conv_tail_gen(TT, lambda cc: mixT[:, 4 + cc, :], bufs["mixT"])
        if not last:
            S.op("dve", lambda e: e.tensor_copy(kT[:, 0:128], kT[:, TT:TT + 128]), reads=B("kT"), writes=B("kT"))
            S.op("dve", lambda e: e.tensor_copy(vtok[:, 0, :], vtok[:, NB, :]), reads=B("vtok"), writes=B("vtok"))
            S.op("dve", lambda e: e.tensor_copy(uT[:, :, 0:32], uT[:, :, TT:TT + 32]), reads=B("uT"), writes=B("uT"))
        yield
        for _ in range(8):
            yield
        aT = actT2[t % 2]; aTb = bufs[f"actT{t % 2}"]
        hs2 = []
        for b in range(NB):
            sl = xloaded[(t, b)]
            wout_block(lambda kc, b=b: mixT[:, kc, b * 128:(b + 1) * 128], 128, xr[:, sl, :], [xrb[sl]], B("mixT"))
            yield
            h_, hbb = rot("hb", hb, "hb")
            rmsnorm(xr[:, sl, :], 128, [xrb[sl]], None, h_[:, :], hbb)
            hs2.append((h_, hbb))
            yield
        yield
        yield
        for b in range(NB):
            h_, hbb = hs2[b]
            transpose_to(h_, 128, 8, lambda b=b, aT=aT: aT[:, :, b * 128:(b + 1) * 128], hbb, aTb, "dve", gcm=g2b)
            yield

    def run_all(g):
        for _ in g:
            pass

    sgen = sample_gen()
    load_x(0)
    gA0 = aphase(0)
    _SENT = object()
    pre_done = False
    while True:
        prog = False
        for _ in range(9):
            if next(sgen, _SENT) is not _SENT:
                prog = True
        if not pre_done:
            v_ = next(gA0, _SENT)
            if v_ == "ATTN" or v_ is _SENT:
                pre_done = True
        if not prog:
            break
    run_all(gA0)
    def tile_setup(t):
        base = 128 + t * TT
        last = (t == NTILE - 1)
        aT = actT2[t % 2]; aTb = bufs[f"actT{t % 2}"]
        xs_ = [xloaded[(t, b)] for b in range(NB)]
        stage(13 + 10 * t)
        old_acc = prev_acc[0]
        fresh = [i for i in range(NBANK) if i not in old_acc and i not in reserved]
        stale = [i for i in range(NBANK) if i in old_acc]
        need = 2 * NB + (2 if last else 0)
        pick = fresh[:need // 2] + stale[:need - need // 2]
        pick += [i for i in fresh + stale if i not in pick][:need - len(pick)]
        flat = [(banks[i], bank_bufs[i]) for i in pick[:need]]
        acc = [[flat[2 * b], flat[2 * b + 1]] for b in range(len(flat) // 2)]
        for row in acc:
            for a_ in row:
                reserved.add(bank_index(a_[0]))
        prev_acc[0] = set(bank_index(a_[0]) for row in acc for a_ in row)
        rest_fresh = [i for i in fresh if i not in prev_acc[0]]
        if rest_fresh:
            bank_ctr[0] = rest_fresh[0]
        accbufs = [a_[1] for row in acc for a_ in row]
        grp = {}

        def emit_U(fg, last=last, aT=aT, aTb=aTb):
            wu, wub = rot("wup", wupb, "wupb")
            wd, wdb = rot("wdn", wdnb, "wdnb")
            hd, hdb = rot("hid", hidr, "hidr")
            grp[fg] = (wd, wdb, hd, hdb)
            extra = arena_alias if first_ffn[0] else []
            S.dma("sp", [(wu, wup_s[fg * 128:(fg + 1) * 128, :].rearrange("p (k f) -> p k f", k=8))], reads=[wsc[fg]], writes=[wub] + extra)
            S.dma("sp", [(wd, wdn_s[fg * 128:(fg + 1) * 128, :].rearrange("p (c n) -> p c n", c=2))], reads=[wsc[16 + fg]], writes=[wdb] + extra)
            if not last:
                bk, bb = nextbank()

                def fU(e, bk=bk, wu=wu, aT=aT):
                    ins = None
                    for fc in range(2):
                        for kc in range(8):
                            ins = e.matmul(bk[:, fc * TT:(fc + 1) * TT], wu[:, kc, fc * 128:(fc + 1) * 128], aT[:, kc, 0:TT], start=(kc == 0), stop=(kc == 7))
                    return ins
                S.op("pe", fU, reads=[wub, aTb], writes=[bb])
                r_, rb_ = rot("rl", rl, "rl")
                S.op("act", lambda e, bk=bk, r_=r_: e.activation(r_[:, :, 0:TT], bk[:, :].rearrange("p (c n) -> p c n", c=2), AF.Relu), reads=[bb], writes=[rb_])
                S.op("dve", lambda e, r_=r_, hd=hd: e.tensor_tensor(hd[:, :, 0:TT], r_[:, :, 0:TT], r_[:, :, 0:TT], ALU.mult), reads=[rb_], writes=[hdb])
                return
            for fc in range(2):
                bk, bb = nextbank()

                def fU(e, bk=bk, wu=wu, fc=fc, aT=aT):
                    ins = None
                    for kc in range(8):
                        ins = e.matmul(bk[:, 0:TT], wu[:, kc, fc * 128:(fc + 1) * 128], aT[:, kc, 0:TT], start=(kc == 0), stop=(kc == 7))
                    for kc in range(8):
                        ins = e.matmul(bk[:, TT:TT + NS], wu[:, kc, fc * 128:(fc + 1) * 128], aT[:, kc, TT:TT + NS], start=(kc == 0), stop=(kc == 7))
                    return ins
                S.op("pe", fU, reads=[wub, aTb], writes=[bb])
                ncol = TT + NS
                r_, rb_ = rot("rl", rl, "rl")
                S.op("act", lambda e, bk=bk, r_=r_, ncol=ncol: e.activation(r_[:, 0, 0:ncol], bk[:, 0:ncol], AF.Relu), reads=[bb], writes=[rb_])
                S.op("dve", lambda e, r_=r_, hd=hd, fc=fc, ncol=ncol: e.tensor_tensor(hd[:, fc, 0:ncol], r_[:, 0, 0:ncol], r_[:, 0, 0:ncol], ALU.mult), reads=[rb_], writes=[hdb])

        def emit_D(fg, last=last, acc=acc):
            wd, wdb, hd, hdb = grp[fg]

            def fD(e, wd=wd, hd=hd, fg=fg, acc=acc):
                ins = None
                for fc in range(2):
                    f = fg * 2 + fc
                    for b in range(NB):
                        for half in range(2):
                            ins = e.matmul(acc[b][half][0][:, :], hd[:, fc, b * 128:(b + 1) * 128], wd[:, fc, half * 512:(half + 1) * 512],
                                           start=(f == 0), stop=(f == 31))
                    if last:
                        for half in range(2):
                            ins = e.matmul(acc[NB][half][0][0:NS, :], hd[:, fc, TT:TT + NS], wd[:, fc, half * 512:(half + 1) * 512],
                                           start=(f == 0), stop=(f == 31))
                return ins
            S.op("pe", fD, reads=[wdb, hdb], writes=accbufs)


        return dict(t=t, last=last, acc=acc, accbufs=accbufs, grp=grp, emit_U=emit_U, emit_D=emit_D, xs_=xs_)

    def tile_body(cx):
        t = cx["t"]; last = cx["last"]; emit_U = cx["emit_U"]; emit_D = cx["emit_D"]
        nxtA = None
        for fg in range(16):
            if fg + 1 < 16:
                emit_U(fg + 1)
            if fg == 2:
                first_ffn[0] = False
            emit_D(fg)
            if t == SAMPLE_TILE:
                if fg == 1:
                    load_x(t + 1)
                if fg < 7:
                    for _ in range(14):
                        next(sgen, None)
                elif fg == 7:
                    run_all(sgen)
                    nxtA = aphase(t + 1)
                    next(nxtA, None)
                elif fg >= 9:
                    for _ in range(13):
                        next(nxtA, None)
                continue
            if fg == 1 and not last:
                load_x(t + 1)
                nxtA = aphase(t + 1)
                next(nxtA, None)
            if nxtA is not None and fg >= 2:
                for _ in range(APF):
                    next(nxtA, None)
        return nxtA

    def tile_release(cx):
        acc = cx["acc"]
        first_ffn[0] = False
        for row in acc:
            for a_ in row:
                reserved.discard(bank_index(a_[0]))

    def tile_epilogue(cx):
        t = cx["t"]; last = cx["last"]; acc = cx["acc"]; xs_ = cx["xs_"]
        stage(14 + 10 * t)
        for b in range(NB):
            for half in range(2):
                S.op("dve", lambda e, b=b, half=half, acc=acc, sl=xs_[b]: e.tensor_tensor(xr[:, sl, half * 512:(half + 1) * 512], xr[:, sl, half * 512:(half + 1) * 512],
                                                                     acc[b][half][0][:, :], ALU.add), reads=[acc[b][half][1], xrb[xs_[b]]], writes=[xrb[xs_[b]]])
            rmsnorm(xr[:, xs_[b], :], 128, [xrb[xs_[b]]], gfb, xr[:, xs_[b], :], xrb[xs_[b]], junk=(jkt[:, :], bufs["jkt"]))
            S.dma("pool", [(y_d[t * TT + b * 128: t * TT + (b + 1) * 128, :], xr[:, xs_[b], :])], reads=[xrb[xs_[b]]], sem_buf=ysems[xs_[b]])
        if last:
            for half in range(2):
                S.op("dve", lambda e, half=half, acc=acc: e.tensor_tensor(xs_t[:, half * 512:(half + 1) * 512], xs_t[:, half * 512:(half + 1) * 512],
                                                                acc[NB][half][0][0:NS, :], ALU.add), reads=[acc[NB][half][1]] + B("xs_t"), writes=B("xs_t"))
            rmsnorm(xs_t[:, :], NS, B("xs_t"), gfb, xr[0:NS, 0, :], xrb[0])
            S.dma("sp", [(ys_d, xr[0:NS, 0, :])], reads=[xrb[0]], sem_buf=xrb[0])

    cx = tile_setup(0)
    cx["emit_U"](0)
    for t in range(NTILE):
        nxtA = tile_body(cx)
        first_ffn[0] = False
        if nxtA is not None:
            run_all(nxtA)
        cur = cx
        if t + 1 < NTILE:
            cx = tile_setup(t + 1)
            cx["emit_U"](0)
        tile_epilogue(cur)
        keep_ = set(bank_index(a_[0]) for row in cx["acc"] for a_ in row) if cx is not cur else set()
        for row in cur["acc"]:
            for a_ in row:
                if bank_index(a_[0]) not in keep_:
                    reserved.discard(bank_index(a_[0]))


def _prep_shared(meta_tokens, rel_bias, norm1_g, w_in, attn_sinks, conv_w, conv_b, conv_ln_g, conv_ln_b,
                 w_out, norm2_g, w_up, w_down, norm_f_g):
    f = np.float32
    perm = _pair_perm()
    wi = np.asarray(w_in[0], f)
    cols = np.concatenate([perm, np.arange(512, 1792)])
    wi = wi[:, cols]
    win = np.ascontiguousarray(wi.reshape(8, 128, 1792).transpose(1, 0, 2))
    wo = np.asarray(w_out[0], f)
    rows = np.concatenate([perm, np.arange(512, 1024)])
    wo = wo[rows, :]
    wout = np.ascontiguousarray(wo.reshape(8, 128, 1024).transpose(1, 0, 2))
    wu = np.asarray(w_up[0], f)
    wup = np.ascontiguousarray(wu.reshape(8, 128, 16, 256).transpose(2, 1, 0, 3)).reshape(16 * 128, 2048)
    wd = np.asarray(w_down[0], f)
    wdn = np.ascontiguousarray(wd.reshape(16, 2, 128, 1024).transpose(0, 2, 1, 3)).reshape(16 * 128, 2048)
    cw = np.asarray(conv_w[0], f)
    cwcm = np.ascontiguousarray(cw.T.reshape(4, 128, 31).transpose(1, 0, 2))
    cvec = np.ascontiguousarray(np.concatenate([np.asarray(conv_b[0], f).reshape(4, 128).T,
                                                np.asarray(conv_ln_g[0], f).reshape(4, 128).T,
                                                np.asarray(conv_ln_b[0], f).reshape(4, 128).T], axis=1))
    rbx = np.concatenate([np.asarray(rel_bias, f), np.full((1, 8), NEG, f)], axis=0)
    onehot = np.zeros((33, 384), f)
    for j in range(384):
        dist = 255 - j
        if 0 <= dist <= 128:
            onehot[int(_t5_bucket_np(np.array([dist]))[0]), j] = 1.0
        else:
            onehot[32, j] = 1.0
    jmat = np.ascontiguousarray(np.eye(128, dtype=f)[::-1])
    ident = np.eye(128, dtype=f)
    rowmask = np.full((128, NS), NEG, f)
    for s in range(NS):
        rowmask[s, s] = 0.0
    ind = np.zeros((96, NS), f)
    for s in range(NS):
        ind[s * 6:(s + 1) * 6, s] = 1.0
    cwrep = np.ascontiguousarray(np.tile(cw[:30].reshape(6, 2560), (NS, 1)))
    srow = np.ascontiguousarray(np.stack([cw[30], np.asarray(conv_b[0], f), np.asarray(conv_ln_g[0], f), np.asarray(conv_ln_b[0], f)]))
    return dict(win=win, wout=wout, wup=wup, wdn=wdn, g1cm=np.ascontiguousarray(np.asarray(norm1_g[0], f).reshape(8, 128).T), g2cm=np.ascontiguousarray(np.asarray(norm2_g[0], f).reshape(8, 128).T),
                gf=np.asarray(norm_f_g, f), cwcm=cwcm, cvec=cvec, rbx=rbx, sinks=np.asarray(attn_sinks[0], f),
                onehot=onehot, jmat=jmat, ident=ident, rowmask=rowmask, ind=ind, cwrep=cwrep)


_NC_CACHE = {}


def kernel(x_prompt, x_sample, cache_k, cache_v, state_conv, meta_tokens, rel_bias,
           norm1_g, w_in, attn_sinks, conv_w, conv_b, conv_ln_g, conv_ln_b,
           w_out, norm2_g, w_up, w_down, norm_f_g):
    f = np.float32
    shared = _prep_shared(meta_tokens, rel_bias, norm1_g, w_in, attn_sinks, conv_w, conv_b, conv_ln_g, conv_ln_b,
                          w_out, norm2_g, w_up, w_down, norm_f_g)
    xp = np.asarray(x_prompt, f)[0]
    halo0 = np.concatenate([np.zeros((112, D), f), np.asarray(meta_tokens, f)], axis=0)
    in_maps = []
    for c in range(NCORES):
        halo = halo0 if c == 0 else xp[c * 2048 - 128:c * 2048]
        m = dict(shared)
        m["xin"] = np.ascontiguousarray(np.concatenate([halo, xp[c * 2048:(c + 1) * 2048]], axis=0))
        m["xs"] = np.ascontiguousarray(np.asarray(x_sample, f)[c * NS:(c + 1) * NS, 0, :])
        m["ck"] = np.ascontiguousarray(np.asarray(cache_k, f)[0, c * NS:(c + 1) * NS].reshape(NS, 128, 128))
        m["cv"] = np.ascontiguousarray(np.asarray(cache_v, f)[0, c * NS:(c + 1) * NS].reshape(NS, 128, 128))
        m["sc"] = np.ascontiguousarray(np.asarray(state_conv, f)[0, c * NS:(c + 1) * NS])
        km = np.zeros((256,), f)
        if c == 0:
            km[:112] = NEG
        m["kmask"] = km
        in_maps.append(m)
    if "nc" not in _NC_CACHE:
        _NC_CACHE["nc"] = build_program()
    nc = _NC_CACHE["nc"]
    res = run_bass_kernel_spmd(nc, in_maps, core_ids=list(range(NCORES)))
    R = res.results
    y_prompt = np.concatenate([R[c]["y"] for c in range(NCORES)], axis=0)[None]
    y_sample = np.concatenate([R[c]["ys"] for c in range(NCORES)], axis=0)[:, None, :]
    nk = R[NCORES - 1]["nk"].reshape(1, 1, 128, 2, 64)
    nv = R[NCORES - 1]["nv"].reshape(1, 1, 128, 2, 64)
    ncv = R[NCORES - 1]["ncv"][98:128].reshape(1, 1, 30, 512)
    nks = np.concatenate([R[c]["nks"] for c in range(NCORES)], axis=0).reshape(1, 128, 128, 2, 64)
    nvs = np.concatenate([R[c]["nvs"] for c in range(NCORES)], axis=0).reshape(1, 128, 128, 2, 64)
    ncs = np.concatenate([R[c]["ncs"] for c in range(NCORES)], axis=0).reshape(1, 128, 30, 512)
    return (y_prompt.astype(f), y_sample.astype(f), nk.astype(f), nv.astype(f), ncv.astype(f),
            nks.astype(f), nvs.astype(f), ncs.astype(f))
```

```python
import math
import numpy as np
import concourse.bass as bass
import concourse.mybir as mybir
from concourse.bass_utils import run_bass_kernel_spmd

F32 = mybir.dt.float32
BF16 = mybir.dt.bfloat16
AF = mybir.ActivationFunctionType
ALU = mybir.AluOpType
AX = mybir.AxisListType

ENGS = ("pe", "act", "dve", "pool", "sp")
NEG = -1.0e30
EPS = 1e-6
NCORES = 8
NTILE = 8
TT = 256
NB = TT // 128
SAMPLE_TILE = -1
APF = 5
D = 1024
DFF = 4096
NS = 16
SNC = 128 + NS


class Buf:
    __slots__ = ("name", "last_w", "readers", "dsem", "dcount")

    def __init__(self, name):
        self.name = name
        self.last_w = None
        self.readers = []
        self.dsem = None
        self.dcount = 0


class Op:
    __slots__ = ("eng", "fn", "deps", "idx", "pos", "sig", "tick", "dma", "dval", "dbuf")

    def __init__(self, eng, fn, deps):
        self.eng = eng
        self.fn = fn
        self.deps = deps
        self.sig = False
        self.tick = 0
        self.dma = False
        self.dval = 0
        self.dbuf = None


class Sched:
    def __init__(self, nc):
        self.nc = nc
        self.ops = []
        self.streams = {e: [] for e in ENGS}
        self.dma_bufs = []
        self.prepared = False

    def _mk(self, eng, fn, reads, writes):
        deps = []
        seen = set()
        for b in reads:
            if b.last_w is not None and id(b.last_w) not in seen:
                seen.add(id(b.last_w)); deps.append(b.last_w)
            if b.name.startswith("ps"):
                for r in b.readers:
                    if r.eng != eng and id(r) not in seen:
                        seen.add(id(r)); deps.append(r)
        for b in writes:
            if b.last_w is not None and id(b.last_w) not in seen:
                seen.add(id(b.last_w)); deps.append(b.last_w)
            for r in b.readers:
                if id(r) not in seen:
                    seen.add(id(r)); deps.append(r)
        o = Op(eng, fn, deps)
        o.idx = len(self.ops)
        o.pos = len(self.streams[eng])
        self.ops.append(o)
        self.streams[eng].append(o)
        for b in reads:
            b.readers.append(o)
        for b in writes:
            b.last_w = o
            b.readers = []
        return o

    def op(self, eng, fn, reads=(), writes=()):
        return self._mk(eng, fn, list(reads), list(writes))

    def dma(self, eng, pairs, reads=(), writes=(), sem_buf=None):
        reads = list(reads); writes = list(writes)
        if sem_buf is None:
            sem_buf = writes[0]
        o = self._mk(eng, None, reads, writes)
        o.dma = True
        o.fn = pairs
        o.dbuf = sem_buf
        sem_buf.dcount += 16 * len(pairs)
        o.dval = sem_buf.dcount
        if sem_buf not in self.dma_bufs:
            self.dma_bufs.append(sem_buf)
        return o

    @staticmethod
    def _skip(o, d):
        if o.dma:
            return False
        if d.eng == o.eng:
            if d.eng == "pe":
                return True
            if (o.pos - d.pos) > 3:
                return True
        return False

    def prepare(self):
        nc = self.nc
        for o in self.ops:
            for d in o.deps:
                if not d.dma and not self._skip(o, d):
                    d.sig = True
        self.esem = {}
        for e in ENGS:
            if any(o.sig for o in self.streams[e]):
                self.esem[e] = nc.alloc_semaphore(name=f"es_{e}")
        for b in self.dma_bufs:
            b.dsem = nc.alloc_semaphore(name=f"ds_{b.name}")
        for e in ENGS:
            t = 0
            for o in self.streams[e]:
                if o.sig:
                    t += 1
                o.tick = t
        self.prepared = True

    def emit_one(self, e, h, final_eng="sp"):
        if not self.prepared:
            self.prepare()
        esem = self.esem
        waited = {}
        for o in self.streams[e]:
            need = {}
            for d in o.deps:
                if d.dma:
                    key = ("d", id(d.dbuf)); sem = d.dbuf.dsem; val = d.dval
                else:
                    if self._skip(o, d):
                        continue
                    key = ("e", d.eng); sem = esem[d.eng]; val = d.tick
                if key not in need or need[key][1] < val:
                    need[key] = (sem, val)
            for key, (sem, val) in need.items():
                if waited.get(key, 0) >= val:
                    continue
                waited[key] = val
                h.wait_ge(sem, val)
            if o.dma:
                for (out_ap, in_ap) in o.fn:
                    h.dma_start(out=out_ap, in_=in_ap).then_inc(o.dbuf.dsem, 16)
            else:
                ins = o.fn(h)
                if o.sig:
                    ins.then_inc(esem[e], 1)
        if e == final_eng:
            for b in self.dma_bufs:
                if b.dcount > 0:
                    h.wait_ge(b.dsem, b.dcount)


def _t5_bucket_np(d):
    max_exact = 16
    d = np.asarray(d)
    d_f = np.maximum(d, 1).astype(np.float32)
    large = max_exact + (np.log(d_f / np.float32(max_exact)) / np.float32(math.log(128 / max_exact))
                         * np.float32(32 - max_exact)).astype(np.int32)
    large = np.minimum(large, 31)
    return np.where(d < max_exact, d, large)


def _pair_perm():
    idx = np.zeros(512, dtype=np.int64)
    for g in range(4):
        for kvh in range(2):
            for d in range(64):
                idx[g * 128 + kvh * 64 + d] = (kvh * 4 + g) * 64 + d
    return idx


class _Stop(Exception):
    pass


def build_program():
    import os
    nc = bass.Bass("TRN2", target_bir_lowering=False)
    S = Sched(nc)
    STAGE = int(os.environ.get("KSTAGE", "99"))

    def stage(n):
        if STAGE < n:
            raise _Stop()
    try:
        _record(nc, S, stage)
    except _Stop:
        pass
    with nc.Block() as block:
        @block.tensor
        def _(e): S.emit_one("pe", e)

        @block.scalar
        def _(e): S.emit_one("act", e)

        @block.vector
        def _(e): S.emit_one("dve", e)

        @block.gpsimd
        def _(e): S.emit_one("pool", e)

        @block.sync
        def _(e): S.emit_one("sp", e)
    return nc


def _record(nc, S, stage):
    import os

    def din(name, shape, dt=F32):
        return nc.dram_tensor(name, list(shape), dt, kind="ExternalInput").ap()

    def dout(name, shape):
        return nc.dram_tensor(name, list(shape), F32, kind="ExternalOutput").ap()

    xin = din("xin", [128 + NTILE * TT, D])
    xs_d = din("xs", [NS, D])
    ck_d = din("ck", [NS, 128, 128])
    cv_d = din("cv", [NS, 128, 128])
    sc_d = din("sc", [NS, 30, 512])
    win_d = din("win", [128, 8, 1792])
    wout_d = din("wout", [128, 8, 1024])
    wup_d = din("wup", [16 * 128, 2048])
    wdn_d = din("wdn", [16 * 128, 2048])
    g1_d = din("g1cm", [128, 8]); g2_d = din("g2cm", [128, 8]); gf_d = din("gf", [D])
    cw_d = din("cwcm", [128, 4, 31])
    cvec_d = din("cvec", [128, 12])
    rbx_d = din("rbx", [33, 8])
    sinks_d = din("sinks", [8])
    oh_d = din("onehot", [33, 384])
    J_d = din("jmat", [128, 128])
    id_d = din("ident", [128, 128])
    kmask_d = din("kmask", [256])
    rowm_d = din("rowmask", [128, NS])
    ind_d = din("ind", [96, NS])
    cwrep_d = din("cwrep", [96, 2560])

    y_d = dout("y", [NTILE * TT, D])
    ys_d = dout("ys", [NS, D])
    nk_d = dout("nk", [128, 128]); nv_d = dout("nv", [128, 128])
    ncv_d = dout("ncv", [128, 512])
    nks_d = dout("nks", [NS, 128, 128]); nvs_d = dout("nvs", [NS, 128, 128])
    ncs_d = dout("ncs", [NS, 30, 512])

    wup_s = nc.dram_tensor("wup_s", [16 * 128, 2048], BF16, kind="Internal").ap()
    wdn_s = nc.dram_tensor("wdn_s", [16 * 128, 2048], BF16, kind="Internal").ap()
    biasG = nc.dram_tensor("biasG", [8, 384], F32, kind="Internal").ap()

    bufs = {}

    def T(name, shape, dt=F32):
        t = nc.alloc_sbuf_tensor("s_" + name, list(shape), dt)
        bufs[name] = Buf(name)
        return t

    def B(*names):
        return [bufs[n] for n in names]

    NBANK = 8
    banks = [nc.alloc_psum_tensor(f"ps{i}", [128, 512], F32) for i in range(NBANK)]
    bank_bufs = [Buf(f"ps{i}") for i in range(NBANK)]
    bank_ctr = [0]

    reserved = set()

    def nextbank():
        while True:
            i = bank_ctr[0] % NBANK
            bank_ctr[0] += 1
            if i not in reserved:
                return banks[i], bank_bufs[i]

    def bank_index(bk_):
        for i_, b_ in enumerate(banks):
            if b_ is bk_:
                return i_
        raise KeyError

    win = T("win", [128, 8, 1792], BF16)
    wout = T("wout", [128, 8, 1024], BF16)
    g1b = T("g1b", [128, 8]); g2b = T("g2b", [128, 8]); gfb = T("gfb", [128, D])
    cwcm = T("cwcm", [128, 4, 31]); cvec = T("cvec", [128, 12]); ncvec = T("ncvec", [128, 12])
    identf = T("identf", [128, 128]); identb = T("identb", [128, 128], BF16)
    jf = T("jf", [128, 128]); onesf = T("onesf", [128, 128])
    rowm = T("rowm", [128, NS]); zcol = T("zcol", [128, 1])
    bias = T("bias", [128, 8, 256])
    kmask = T("kmask", [128, 256])
    Ssb2 = [T(f"Ssbx{p}", [128, 8, 260]) for p in range(2)]
    for p_ in range(2):
        for hp_ in range(4):
            bufs[f"Ssb{p_}_{hp_}"] = Buf(f"Ssb{p_}_{hp_}")
    sinkb = T("sinkb", [128, 8])

    S.dma("sp", [(identf[:], id_d), (jf[:], J_d)], writes=B("identf", "jf"), sem_buf=bufs["identf"])
    S.dma("sp", [(cwcm[:], cw_d), (cvec[:], cvec_d), (rowm[:], rowm_d)], writes=B("cwcm", "cvec", "rowm"), sem_buf=bufs["cwcm"])
    S.dma("sp", [(g1b[:], g1_d), (g2b[:], g2_d),
                 (gfb[:], bass.AP(gf_d.tensor, 0, [[0, 128], [1, D]])),
                 (kmask[:], bass.AP(kmask_d.tensor, 0, [[0, 128], [1, 256]])),
                 (sinkb[:], bass.AP(sinks_d.tensor, 0, [[0, 128], [1, 8]]))],
          writes=B("g1b", "g2b", "gfb", "kmask", "sinkb"), sem_buf=bufs["g1b"])
    S.dma("pool", [(win[:, kc, :], win_d[:, kc, :]) for kc in range(8)], writes=B("win"))
    S.dma("pool", [(wout[:, kc, :], wout_d[:, kc, :]) for kc in range(8)], writes=B("wout"))
    S.op("dve", lambda e: e.tensor_copy(identb[:], identf[:]), reads=B("identf"), writes=B("identb"))
    cwcmb = T("cwcmb", [128, 4, 31], BF16)
    S.op("dve", lambda e: e.tensor_copy(cwcmb[:], cwcm[:]), reads=B("cwcm"), writes=B("cwcmb"))
    S.op("pool", lambda e: e.memset(onesf[:], 1.0), writes=B("onesf"))
    S.op("pool", lambda e: e.memset(zcol[:], 0.0), writes=B("zcol"))
    S.op("dve", lambda e: e.tensor_scalar(ncvec[:], cvec[:], -1.0, None, ALU.mult), reads=B("cvec"), writes=B("ncvec"))
    for p_ in range(2):
        S.op("dve", lambda e, p_=p_: e.tensor_copy(Ssb2[p_][:, :, 256], sinkb[:]), reads=B("sinkb"),
             writes=[bufs[f"Ssb{p_}_{hp_}"] for hp_ in range(4)])

    stage(1)
    rbx = T("rbx", [33, 8]); cch = T("cch", [128, 4, TT])
    arena = T("arena", [128, 12288], BF16)
    biasT = arena[:, 0:4096].bitcast(F32); bufs["biasT"] = Buf("biasT")
    utok = T("utok", [128, 512]); kvtok = utok[:, 0:256]; bufs["kvtok"] = bufs["utok"]
    oh = biasT[0:33, 0:384]; bufs["oh"] = bufs["biasT"]; gsb = utok[0:8, 0:384]; bufs["gsb"] = bufs["utok"]
    S.dma("sp", [(rbx[:], rbx_d), (oh, oh_d)], writes=B("rbx", "oh"), sem_buf=bufs["rbx"])
    bk, bb = nextbank()
    S.op("pe", lambda e: e.matmul(bk[0:8, 0:384], rbx[:, :], oh, start=True, stop=True), reads=B("rbx", "oh"), writes=[bb])
    S.op("dve", lambda e: e.tensor_copy(gsb, bk[0:8, 0:384]), reads=[bb], writes=B("gsb"))
    bufs["biasG"] = Buf("biasG")
    S.dma("sp", [(biasG, gsb)], reads=B("gsb"), writes=B("biasG"))
    S.dma("sp", [(biasT[:, h * 256:(h + 1) * 256], bass.AP(biasG.tensor, h * 384, [[1, 128], [1, 256]])) for h in range(8)],
          reads=B("biasG"), writes=B("biasT"))
    def bias_flip():
        for hp in range(4):
            bk, bb = nextbank()
            S.op("pe", lambda e, bk=bk, hp=hp: e.matmul(bk[:, :], jf[:, :], biasT[:, hp * 512:(hp + 1) * 512], start=True, stop=True),
                 reads=B("jf", "biasT"), writes=[bb])
            S.op("dve", lambda e, bk=bk, hp=hp: e.tensor_copy(bias[:, 2 * hp:2 * hp + 2, :], bk[:, :].rearrange("p (a b) -> p a b", a=2)),
                 reads=[bb], writes=B("bias"))

    xr = T("xr", [128, 2 * NB, D])
    xrb = [Buf(f"xr_{i}") for i in range(2 * NB)]
    hb = [T(f"hb{i}", [128, D], BF16) for i in range(2)]
    actT2 = [T(f"actT{i}", [128, 8, TT + 16], BF16) for i in range(2)]
    hT = T("hT", [128, 8, TT], BF16)
    qT = T("qT", [128, 4, TT], BF16)
    kT = T("kT", [128, 128 + TT], BF16)
    vtok = T("vtok", [128, NB + 1, 128], BF16)
    uT = T("uT", [128, 4, 32 + TT], BF16)
    Pb = [T(f"Pb{i}", [128, 2, 260], BF16) for i in range(2)]
    PTs = [T(f"PTs{i}", [128, 4, 128], BF16) for i in range(2)]
    attn = [T(f"attn{i}", [128, 512], BF16) for i in range(2)]
    mixT = T("mixT", [128, 8, TT], BF16)
    diag = [T(f"diag{i}", [128, 31, 128], BF16) for i in range(2)]
    tmpA = [T(f"tmpA{i}", [128, TT]) for i in range(3)]
    cchb = [Buf(f"cch_{i}") for i in range(4)]
    rstdb = T("rstdb", [128, TT])
    wupb = [arena[:, i * 2048:(i + 1) * 2048].rearrange("p (k f) -> p k f", k=8) for i in range(3)]
    wdnb = [arena[:, 6144 + i * 2048:6144 + (i + 1) * 2048].rearrange("p (c n) -> p c n", c=2) for i in range(3)]
    for i in range(3):
        bufs[f"wupb{i}"] = Buf(f"wupb{i}")
        bufs[f"wdnb{i}"] = Buf(f"wdnb{i}")
    hidr = [T(f"hidr{i}", [128, 2, TT + 16], BF16) for i in range(3)]
    rl = [T(f"rl{i}", [128, 2, TT + 16]) for i in range(2)]
    jkt = T("jkt", [128, D], BF16)
    NST = 24
    stt = [T(f"st{i}", [128, 4]) for i in range(NST)]
    st_ctr = [0]

    def nextst():
        i = st_ctr[0] % NST
        st_ctr[0] += 1
        return stt[i], bufs[f"st{i}"]

    rr = {"hb": 0, "P": 0, "PT": 0, "attn": 0, "tmp": 0, "rl": 0, "wup": 0, "wdn": 0, "diag": 0, "hid": 0}

    def rot(key, lst, prefix):
        i = rr[key] % len(lst)
        rr[key] += 1
        return lst[i], bufs[f"{prefix}{i}"]

    def rmsnorm(xap, P, xbufs, gb, outap, outbuf, junk=None):
        st, stb = nextst()
        jap, jbuf = (outap, outbuf) if junk is None else junk
        S.op("act", lambda e: e.activation(jap, xap, AF.Square, accum_out=st[0:P, 0:1]), reads=xbufs, writes=[jbuf, stb])
        S.op("act", lambda e: e.activation(st[0:P, 1:2], st[0:P, 0:1], AF.Ln, scale=1.0 / D, bias=epsc[0:P, :]), reads=[stb, bufs["epsc"]], writes=[stb])
        S.op("act", lambda e: e.activation(st[0:P, 2:3], st[0:P, 1:2], AF.Exp, scale=-0.5), reads=[stb], writes=[stb])
        if gb is None:
            S.op("dve", lambda e: e.tensor_scalar(outap, xap, st[0:P, 2:3], None, ALU.mult), reads=xbufs + [stb], writes=[outbuf])
        else:
            S.op("dve", lambda e: e.scalar_tensor_tensor(outap, xap, st[0:P, 2:3], gb[0:P, :], ALU.mult, ALU.mult),
                 reads=xbufs + [stb, gbuf[id(gb)]], writes=[outbuf])

    epsc = T("epsc", [128, 1])
    onec = T("onec", [128, 1])
    S.op("pool", lambda e: e.memset(epsc[:], EPS), writes=B("epsc"))
    S.op("pool", lambda e: e.memset(onec[:], 1.0), writes=B("onec"))
    gbuf = {id(g1b): bufs["g1b"], id(g2b): bufs["g2b"], id(gfb): bufs["gfb"]}

    def transpose_to(src_bf, P, nchunk, dst_fn, srcbuf, dstbuf, evac_eng, gcm=None):
        bk, bb = nextbank()
        pv = bk[:, :].bitcast(BF16).rearrange("p (c n) -> p c n", c=8)

        def f(e):
            ins = None
            for c in range(nchunk):
                ins = e.transpose(pv[:, c, 0:P], src_bf[0:P, c * 128:(c + 1) * 128], identb[0:P, 0:P])
            return ins
        S.op("pe", f, reads=[srcbuf, bufs["identb"]], writes=[bb])
        if gcm is not None:
            S.op("dve", lambda e: e.tensor_tensor(dst_fn(), pv[:, 0:nchunk, 0:P], gcm[:, 0:nchunk].unsqueeze(2).to_broadcast([128, nchunk, P]), ALU.mult),
                 reads=[bb, gbuf[id(gcm)]], writes=[dstbuf])
        elif evac_eng == "act":
            S.op("act", lambda e: e.copy(dst_fn(), pv[:, 0:nchunk, 0:P]), reads=[bb], writes=[dstbuf])
        else:
            S.op("dve", lambda e: e.tensor_copy(dst_fn(), pv[:, 0:nchunk, 0:P]), reads=[bb], writes=[dstbuf])

    def sigmoid_from(src_ap, P, N, srcbufs, scale_neg, bias_neg):
        t, tb = rot("tmp", tmpA, "tmpA")
        kw = {}
        if bias_neg is not None:
            kw["bias"] = bias_neg
        S.op("act", lambda e: e.activation(t[0:P, 0:N], src_ap, AF.Exp, scale=scale_neg, **kw), reads=srcbufs, writes=[tb])
        S.op("act", lambda e: e.activation(t[0:P, 0:N], t[0:P, 0:N], AF.Ln, bias=onec[0:P, :]), reads=[tb, bufs["onec"]], writes=[tb])
        S.op("act", lambda e: e.activation(t[0:P, 0:N], t[0:P, 0:N], AF.Exp, scale=-1.0), reads=[tb], writes=[tb])
        return t, tb

    def attn_A(par, q_ap_fn, kT_ap_fn, qbufs, kbufs, first_mask, rowmask_col, ncol=256):
        Sx = Ssb2[par]
        sts = []
        for hp in range(4):
            sbk, sbb = nextbank()
            heads = [2 * hp, 2 * hp + 1]
            sxb = bufs[f"Ssb{par}_{hp}"]

            def fS(e, sbk=sbk, heads=heads):
                ins = None
                for i, h in enumerate(heads):
                    kvh, g = h // 4, h % 4
                    ins = e.matmul(sbk[:, i * 256:i * 256 + ncol], q_ap_fn(g, kvh), kT_ap_fn(kvh), start=True, stop=True)
                return ins
            S.op("pe", fS, reads=qbufs + kbufs, writes=[sbb])
            S.op("dve", lambda e, sbk=sbk, hp=hp, Sx=Sx: e.tensor_tensor(Sx[:, 2 * hp:2 * hp + 2, 0:ncol], sbk[:, :].rearrange("p (a b) -> p a b", a=2)[:, :, 0:ncol],
                                                                         bias[:, 2 * hp:2 * hp + 2, 0:ncol], ALU.add),
                 reads=[sbb, bufs["bias"]], writes=[sxb])
            if ncol < 256:
                S.op("dve", lambda e, hp=hp, Sx=Sx: e.tensor_copy(Sx[:, 2 * hp:2 * hp + 2, ncol], sinkb[:, 2 * hp:2 * hp + 2]),
                     reads=B("sinkb"), writes=[sxb])
            if first_mask:
                for h in heads:
                    S.op("dve", lambda e, h=h, Sx=Sx: e.tensor_tensor(Sx[:, h, 0:256], Sx[:, h, 0:256], kmask[:, :], ALU.add),
                         reads=[sxb, bufs["kmask"]], writes=[sxb])
            st, stb = nextst()
            S.op("dve", lambda e, hp=hp, st=st, Sx=Sx: e.tensor_reduce(st[:, 0:2], Sx[:, 2 * hp:2 * hp + 2, 0:ncol + 1], AX.X, ALU.max),
                 reads=[sxb], writes=[stb])
            if rowmask_col is None:
                S.op("dve", lambda e, st=st: e.tensor_scalar(st[:, 2:4], st[:, 0:2], -1.0, None, ALU.mult), reads=[stb], writes=[stb])
            else:
                S.op("dve", lambda e, st=st: e.tensor_scalar(st[:, 2:4], st[:, 0:2], -1.0, rowmask_col, ALU.mult, ALU.add),
                     reads=[stb, bufs["rowm"]], writes=[stb])
            sts.append((st, stb))
        return sts

    def attn_B(par, sts, v_ap_fn, vbufs, out_mode, out_state, filler=None, ncol=256):
        Sx = Ssb2[par]
        if out_mode == "prompt":
            at, atb = rot("attn", attn, "attn")
            atv = at[:, :].rearrange("p (g k d) -> p g k d", g=4, k=2)
        pbs = {}
        pts = {}

        def do_exp(hp):
            st, stb = sts[hp]
            sxb = bufs[f"Ssb{par}_{hp}"]
            pb, pbb = rot("P", Pb, "Pb")
            pbs[hp] = (pb, pbb)
            for i, h in enumerate((2 * hp, 2 * hp + 1)):
                if out_mode == "prompt":
                    acc_ap = st[:, i:i + 1]
                    wr = [pbb, stb]
                else:
                    c = out_state["s"] * 8 + h
                    acc_ap = out_state["smat"][:, c:c + 1]
                    wr = [pbb, out_state["smatb"]]
                S.op("act", lambda e, pb=pb, h=h, st=st, i=i, acc_ap=acc_ap: e.activation(pb[:, i, 0:ncol + 1], Sx[:, h, 0:ncol + 1], AF.Exp,
                                                                                         bias=st[:, 2 + i:3 + i], accum_out=acc_ap),
                     reads=[sxb, stb], writes=wr)

        def do_T(hp):
            pb, pbb = pbs[hp]
            tbk, tbb = nextbank()
            tv = tbk[:, :].bitcast(BF16).rearrange("p (c n) -> p c n", c=8)

            def fT(e, pb=pb, tv=tv):
                ins = None
                w1 = ncol - 128
                for i in range(2):
                    ins = e.transpose(tv[:, i * 2, :], pb[:, i, 0:128], identb[:, :])
                    ins = e.transpose(tv[0:w1, i * 2 + 1, :], pb[:, i, 128:ncol], identb[:, :])
                return ins
            S.op("pe", fT, reads=[pbb, bufs["identb"]], writes=[tbb])
            pt, ptb = rot("PT", PTs, "PTs")
            pts[hp] = (pt, ptb)
            if ncol == 256:
                S.op("act", lambda e, pt=pt, tv=tv: e.copy(pt[:, :, :], tv[:, 0:4, :]), reads=[tbb], writes=[ptb])
            else:
                w1 = ncol - 128
                tv2 = tv[:, 0:4, :].rearrange("p (i h) n -> p i h n", h=2)
                pt2 = pt[:, :, :].rearrange("p (i h) n -> p i h n", h=2)

                def fCp(e, pt2=pt2, tv2=tv2, w1=w1):
                    e.copy(pt2[:, :, 0, :], tv2[:, :, 0, :])
                    return e.copy(pt2[0:w1, :, 1, :], tv2[0:w1, :, 1, :])
                S.op("act", fCp, reads=[tbb], writes=[ptb])

        def do_PV(hp):
            st, stb = sts[hp]
            pt, ptb = pts[hp]
            kvh = (2 * hp) // 4
            g0 = (2 * hp) % 4
            if out_mode == "prompt":
                obk, obb = nextbank()

                def fO(e, pt=pt, obk=obk, kvh=kvh):
                    ins = None
                    for i in range(2):
                        for half in range(2):
                            ins = e.matmul(obk[:, i * 64:(i + 1) * 64], pt[:, i * 2 + half, :], v_ap_fn(half, kvh), start=(half == 0), stop=(half == 1))
                    return ins
                S.op("pe", fO, reads=[ptb] + vbufs, writes=[obb])
                S.op("dve", lambda e, st=st: e.reciprocal(st[:, 0:2], st[:, 0:2]), reads=[stb], writes=[stb])
                S.op("dve", lambda e, obk=obk, st=st, g0=g0, kvh=kvh: e.tensor_tensor(
                    atv[:, g0:g0 + 2, kvh, :], obk[:, 0:128].rearrange("p (a d) -> p a d", a=2),
                    st[:, 0:2].unsqueeze(2).to_broadcast([128, 2, 64]), ALU.mult),
                    reads=[obb, stb], writes=[atb])
            else:
                s_ = out_state["s"]
                ob = out_state["obank"][kvh]
                obb = out_state["obankb"][kvh]

                def fO2(e, pt=pt, ob=ob, g0=g0, kvh=kvh, s_=s_):
                    ins = None
                    w1 = ncol - 128
                    for i in range(2):
                        c0 = (g0 + i) * 64
                        ins = e.matmul(ob[:, c0:c0 + 64], pt[:, i * 2, :], v_ap_fn(0, kvh), start=False, stop=False)
                        ins = e.matmul(ob[:, c0:c0 + 64], pt[0:w1, i * 2 + 1, :], v_ap_fn(1, kvh)[0:w1, :],
                                       start=False, stop=(s_ == NS - 1 and (g0 + i) == 3 and kvh == 1))
                    return ins
                S.op("pe", fO2, reads=[ptb] + vbufs, writes=[obb])

        def step():
            if filler is not None:
                filler()

        do_exp(0)
        do_exp(1)
        yield
        do_T(0)
        do_exp(2)
        step()
        yield
        do_T(1)
        do_PV(0)
        do_exp(3)
        step()
        yield
        do_T(2)
        do_PV(1)
        step()
        yield
        do_T(3)
        do_PV(2)
        step()
        yield
        do_PV(3)
        yield
        if out_mode == "prompt":
            return at, atb
        return None, None

    def conv_diag(cc):
        dg, dgb = rot("diag", diag, "diag")
        S.op("dve", lambda e, dg=dg: e.tensor_tensor(dg[:, :, :], identb[:, :].unsqueeze(1).to_broadcast([128, 31, 128]),
                                                     cwcmb[:, cc, :].unsqueeze(2).to_broadcast([128, 31, 128]), ALU.mult),
             reads=B("identb", "cwcmb"), writes=[dgb])
        return dg, dgb

    def glu(a_ap, b_ap, P, N, abufs, bbufs, out_ap, outbufs):
        sg, sgb = sigmoid_from(b_ap, P, N, bbufs, -1.0, None)
        S.op("dve", lambda e: e.tensor_tensor(out_ap, a_ap, sg[0:P, 0:N], ALU.mult), reads=abufs + [sgb], writes=outbufs)

    def win_fm(rhs_fn, N, col0, rbufs):
        bk, bb = nextbank()

        def f(e):
            ins = None
            for kc in range(8):
                ins = e.matmul(bk[:, 0:N], win[:, kc, col0:col0 + 128], rhs_fn(kc), start=(kc == 0), stop=(kc == 7))
            return ins
        S.op("pe", f, reads=[bufs["win"]] + rbufs, writes=[bb])
        return bk, bb

    def win_tm(lhs_fn, P, col0, ncol, lbufs):
        bk, bb = nextbank()

        def f(e):
            ins = None
            for kc in range(8):
                ins = e.matmul(bk[0:P, 0:ncol], lhs_fn(kc), win[:, kc, col0:col0 + ncol], start=(kc == 0), stop=(kc == 7))
            return ins
        S.op("pe", f, reads=[bufs["win"]] + lbufs, writes=[bb])
        return bk, bb

    CQ, CK, CV, CA, CB = 0, 512, 640, 768, 1280

    def conv_tail(N, dst_fn, dstbuf):
        mbk, mbb = nextbank()

        def fM(e, mbk=mbk):
            ins = None
            for cc in range(4):
                ins = e.matmul(mbk[:, 0:N], onesf[:, :], cch[:, cc, 0:N], start=(cc == 0), stop=(cc == 3))
            return ins
        S.op("pe", fM, reads=B("onesf", "cch"), writes=[mbb])
        for cc in range(4):
            S.op("dve", lambda e, mbk=mbk, cc=cc: e.scalar_tensor_tensor(cch[:, cc, 0:N], mbk[:, 0:N], -1.0 / 512, cch[:, cc, 0:N], ALU.mult, ALU.add),
                 reads=[mbb] + B("cch"), writes=B("cch"))
        vbk, vbb = nextbank()
        for cc in range(4):
            tq, tqb = rot("tmp", tmpA, "tmpA")
            S.op("act", lambda e, tq=tq, cc=cc: e.activation(tq[:, 0:N], cch[:, cc, 0:N], AF.Square), reads=B("cch"), writes=[tqb])
            S.op("pe", lambda e, tq=tq, cc=cc, vbk=vbk: e.matmul(vbk[:, 0:N], onesf[:, :], tq[:, 0:N], start=(cc == 0), stop=(cc == 3)),
                 reads=[tqb, bufs["onesf"]], writes=[vbb])
        S.op("act", lambda e, vbk=vbk: e.activation(rstdb[:, 0:N], vbk[:, 0:N], AF.Ln, scale=1.0 / 512, bias=epsc[:, :]), reads=[vbb, bufs["epsc"]], writes=B("rstdb"))
        S.op("act", lambda e: e.activation(rstdb[:, 0:N], rstdb[:, 0:N], AF.Exp, scale=-0.5), reads=B("rstdb"), writes=B("rstdb"))
        for cc in range(4):
            S.op("dve", lambda e, cc=cc: e.tensor_tensor(cch[:, cc, 0:N], cch[:, cc, 0:N], rstdb[:, 0:N], ALU.mult), reads=B("cch", "rstdb"), writes=B("cch"))
            S.op("dve", lambda e, cc=cc: e.tensor_scalar(cch[:, cc, 0:N], cch[:, cc, 0:N], cvec[:, 4 + cc:5 + cc], cvec[:, 8 + cc:9 + cc], ALU.mult, ALU.add),
                 reads=B("cch", "cvec"), writes=B("cch"))
            sg, sgb = sigmoid_from(cch[:, cc, 0:N], 128, N, B("cch"), -1.0, None)
            S.op("dve", lambda e, cc=cc, sg=sg: e.tensor_tensor(dst_fn(cc), cch[:, cc, 0:N], sg[:, 0:N], ALU.mult), reads=B("cch") + [sgb], writes=[dstbuf])

    def wout_block(mix_fn, P, xres_ap, xresbufs, mixbufs):
        for half in range(2):
            bk_, bb_ = nextbank()

            def f(e, bk_=bk_, half=half):
                ins = None
                for kc in range(8):
                    ins = e.matmul(bk_[0:P, :], mix_fn(kc), wout[:, kc, half * 512:(half + 1) * 512], start=(kc == 0), stop=(kc == 7))
                return ins
            S.op("pe", f, reads=mixbufs + B("wout"), writes=[bb_])
            S.op("dve", lambda e, bk_=bk_, half=half: e.tensor_tensor(xres_ap[:, half * 512:(half + 1) * 512], xres_ap[:, half * 512:(half + 1) * 512],
                                                                     bk_[0:P, :], ALU.add), reads=[bb_] + xresbufs, writes=xresbufs)

    stage(2)
    xh, xhb = xr[:, 2, :], xrb[2]
    S.dma("sp", [(xh, xin[0:128, :])], writes=[xhb])
    hh, hhb = rot("hb", hb, "hb")
    rmsnorm(xh, 128, [xhb], None, hh[:, :], hhb)
    transpose_to(hh, 128, 8, lambda: hT[:, :, 0:128], hhb, bufs["hT"], "dve", gcm=g1b)
    bk, bb = win_fm(lambda kc: hT[:, kc, 0:128], 128, CK, B("hT"))
    S.op("act", lambda e, bk=bk: e.copy(kT[:, 0:128], bk[:, 0:128]), reads=[bb], writes=B("kT"))
    bk, bb = win_tm(lambda kc: hT[:, kc, 0:128], 128, CV, 128, B("hT"))
    S.op("dve", lambda e, bk=bk: e.tensor_copy(vtok[:, 0, :], bk[:, 0:128]), reads=[bb], writes=B("vtok"))
    for cc in range(4):
        ak, ab = win_fm(lambda kc: hT[:, kc, 0:128], 128, CA + cc * 128, B("hT"))
        bk2, bb2 = win_fm(lambda kc: hT[:, kc, 0:128], 128, CB + cc * 128, B("hT"))
        sg, sgb = sigmoid_from(bk2[:, 96:128], 128, 32, [bb2], -1.0, None)
        S.op("dve", lambda e, ak=ak, sg=sg, cc=cc: e.tensor_tensor(uT[:, cc, 0:32], ak[:, 96:128], sg[:, 0:32], ALU.mult),
             reads=[ab, sgb], writes=B("uT"))

    stage(3)
    xs_t = T("xs_t", [NS, D])
    S.dma("sp", [(xs_t[:], xs_d)], writes=B("xs_t"))
    hs, hsb = rot("hb", hb, "hb")
    rmsnorm(xs_t[:, :], NS, B("xs_t"), None, hs[0:NS, :], hsb)
    hsT = T("hsT", [128, 8, NS], BF16)
    transpose_to(hs, NS, 8, lambda: hsT[:, :, :], hsb, bufs["hsT"], "dve", gcm=g1b)
    qsT = T("qsT", [128, 4, 128], BF16)
    S.op("pool", lambda e: e.memset(qsT[:], 0.0), writes=B("qsT"))
    for g in range(4):
        bk, bb = win_fm(lambda kc: hsT[:, kc, :], NS, CQ + g * 128, B("hsT"))
        S.op("act", lambda e, bk=bk, g=g: e.activation(qsT[:, g, 0:NS], bk[:, 0:NS], AF.Copy, scale=0.125), reads=[bb], writes=B("qsT"))
    usT = T("usT", [128, 4, NS])
    for cc in range(4):
        ak, ab = win_fm(lambda kc: hsT[:, kc, :], NS, CA + cc * 128, B("hsT"))
        bk2, bb2 = win_fm(lambda kc: hsT[:, kc, :], NS, CB + cc * 128, B("hsT"))
        glu(ak[:, 0:NS], bk2[:, 0:NS], 128, NS, [ab], [bb2], usT[:, cc, :], B("usT"))
    kvs_f = T("kvs_f", [NS, 256]); us_f = T("us_f", [NS, 512])
    bk, bb = win_tm(lambda kc: hsT[:, kc, :], NS, CK, 256, B("hsT"))
    S.op("dve", lambda e, bk=bk: e.tensor_copy(kvs_f[:, :], bk[0:NS, 0:256]), reads=[bb], writes=B("kvs_f"))
    ak, ab = win_tm(lambda kc: hsT[:, kc, :], NS, CA, 512, B("hsT"))
    bk2, bb2 = win_tm(lambda kc: hsT[:, kc, :], NS, CB, 512, B("hsT"))
    S.op("act", lambda e, bk2=bk2: e.activation(utok[0:NS, :], bk2[0:NS, :], AF.Exp, scale=-1.0), reads=[bb2], writes=B("utok"))
    S.op("act", lambda e: e.activation(utok[0:NS, :], utok[0:NS, :], AF.Ln, bias=onec[0:NS, :]), reads=B("utok", "onec"), writes=B("utok"))
    S.op("act", lambda e: e.activation(utok[0:NS, :], utok[0:NS, :], AF.Exp, scale=-1.0), reads=B("utok"), writes=B("utok"))
    S.op("dve", lambda e, ak=ak: e.tensor_tensor(us_f[:, :], ak[0:NS, :], utok[0:NS, :], ALU.mult), reads=[ab] + B("utok"), writes=B("us_f"))
    bias_flip()
    stage(4)
    ob1 = Buf("ob1"); bufs["ob1"] = ob1
    S.dma("sp", [(nks_d[:, 0:127, :], ck_d[:, 1:128, :]), (nvs_d[:, 0:127, :], cv_d[:, 1:128, :]),
                 (ncs_d[:, 0:29, :], sc_d[:, 1:30, :])], writes=[ob1])
    S.dma("sp", [(nks_d[:, 127, :], kvs_f[:, 0:128]), (nvs_d[:, 127, :], kvs_f[:, 128:256])], reads=B("kvs_f"), sem_buf=bufs["kvs_f"])
    S.dma("sp", [(ncs_d[:, 29, :], us_f[:, :])], reads=B("us_f"), sem_buf=bufs["us_f"])

    stage(5)
    wsc = [Buf(f"wsc{i}") for i in range(32)]
    wchain = Buf("wchain"); bufs["wchain"] = wchain
    for fg in range(16):
        S.dma("pool", [(wup_s[fg * 128:(fg + 1) * 128, :], wup_d[fg * 128:(fg + 1) * 128, :])], reads=B("kT"), writes=[wsc[fg]])
        S.dma("pool", [(wdn_s[fg * 128:(fg + 1) * 128, :], wdn_d[fg * 128:(fg + 1) * 128, :])], reads=B("kT"), writes=[wsc[16 + fg]])

    stage(6)
    stA = arena[0:96, 0:5120].bitcast(F32)
    cwrep = arena[0:96, 5120:10240].bitcast(F32)
    part = arena[0:96, 10240:11264].bitcast(F32)
    for n_ in ("stA", "cwrep", "part"):
        bufs[n_] = Buf(n_)
    arena_alias = B("stA", "cwrep", "part")
    ind = T("ind", [96, NS])
    S.dma("sp", [(stA, sc_d.rearrange("s (a t) c -> (s a) (t c)", a=6)), (cwrep, cwrep_d), (ind[:], ind_d)],
          writes=B("stA", "cwrep", "ind", "biasT"), sem_buf=bufs["stA"])
    S.op("dve", lambda e: e.tensor_tensor(stA, stA, cwrep, ALU.mult), reads=B("stA", "cwrep"), writes=B("stA"))
    S.op("dve", lambda e: e.tensor_reduce(part, stA.rearrange("p (t c) -> p c t", t=5), AX.X, ALU.add), reads=B("stA"), writes=B("part"))
    for cc in range(4):
        cbk, cbb = nextbank()
        S.op("pe", lambda e, cbk=cbk, cc=cc: e.matmul(cbk[:, 0:NS], part[:, cc * 128:(cc + 1) * 128], ind[:, :], start=True, stop=True),
             reads=B("ind", "part"), writes=[cbb])
        S.op("dve", lambda e, cbk=cbk, cc=cc: e.scalar_tensor_tensor(cch[:, cc, 0:NS], usT[:, cc, :], cwcm[:, cc, 30:31], cbk[:, 0:NS], ALU.mult, ALU.add),
             reads=[cbb] + B("usT", "cwcm", "cch"), writes=B("cch"))
        S.op("dve", lambda e, cc=cc: e.tensor_scalar(cch[:, cc, 0:NS], cch[:, cc, 0:NS], cvec[:, cc:cc + 1], None, ALU.add), reads=B("cch", "cvec"), writes=B("cch"))
    mixsT = T("mixsT", [128, 8, 128], BF16)
    conv_tail(NS, lambda cc: mixsT[:, 4 + cc, 0:NS], bufs["mixsT"])

    def sample_gen():
        NFR = 6
        kvfr = [T(f"kvfr{i}", [128, 2, 256], BF16) for i in range(NFR)]
        for i in range(NFR):
            S.op("pool", lambda e, i=i: e.memset(kvfr[i][:], 0.0), writes=[bufs[f"kvfr{i}"]])
        smat = T("smat", [128, NS * 8])
        S.op("pool", lambda e: e.memset(smat[:], 0.0), writes=B("smat"))
        NKT = 4
        kTs = [T(f"kTs{i}", [128, 256], BF16) for i in range(NKT)]
        obk_, obb_ = nextbank()
        reserved.add(bank_index(obk_))
        ostate = {"smat": smat, "smatb": bufs["smat"], "obank": [obk_[:, 0:256], obk_[:, 256:512]], "obankb": [obb_, obb_], "s": 0}
        zb = T("zb", [128, 512], BF16)
        S.op("pool", lambda e: e.memset(zb[:], 0.0), writes=B("zb"))
        S.op("pe", lambda e: e.matmul(obk_[:, :], zb[:, 0:128], zb[:, :], start=True, stop=False), reads=B("zb"), writes=[obb_])

        def prep(s):
            fr = kvfr[s % NFR]; frb = bufs[f"kvfr{s % NFR}"]
            n0 = 128 - s
            prs = [(fr[s:128, 0, 0:128], ck_d[s, 0:n0, :]), (fr[s:128, 0, 128:256], cv_d[s, 0:n0, :]),
                   (fr[s:s + 1, 1, :], kvs_f[s:s + 1, :])]
            if s > 0:
                prs += [(fr[0:s, 1, 0:128], ck_d[s, n0:128, :]), (fr[0:s, 1, 128:256], cv_d[s, n0:128, :])]
            S.dma("pool", prs, reads=B("kvs_f"), writes=[frb])
            kt = kTs[s % NKT]; ktb = bufs[f"kTs{s % NKT}"]
            tbk, tbb = nextbank()
            tv = tbk[:, :].bitcast(BF16).rearrange("p (c n) -> p c n", c=8)

            def fK(e, fr=fr, tv=tv):
                ins = None
                for slot in range(2):
                    ins = e.transpose(tv[:, slot, :], fr[:, slot, 0:128], identb[:, :])
                return ins
            S.op("pe", fK, reads=[frb, bufs["identb"]], writes=[tbb])
            S.op("dve", lambda e, kt=kt, tv=tv: e.tensor_copy(kt[:, :].rearrange("p (a b) -> p a b", a=2), tv[:, 0:2, :]), reads=[tbb], writes=[ktb])

        def samp_A(s):
            kt = kTs[s % NKT]; ktb = bufs[f"kTs{s % NKT}"]
            return attn_A(s % 2, lambda g, kvh: qsT[kvh * 64:(kvh + 1) * 64, g, :],
                          lambda kvh, kt=kt: kt[kvh * 64:(kvh + 1) * 64, 0:SNC],
                          B("qsT"), [ktb], False, rowm[:, s:s + 1], ncol=SNC)

        def samp_B(s, sts):
            st_ = dict(ostate); st_["s"] = s
            fr = kvfr[s % NFR]; frb = bufs[f"kvfr{s % NFR}"]
            yield from attn_B(s % 2, sts, lambda half, kvh, fr=fr: fr[:, half, 128 + kvh * 64:128 + (kvh + 1) * 64], [frb], "sample", st_, ncol=SNC)

        for s0 in range(3):
            prep(s0)
            yield
        pend = samp_A(0)
        yield
        for s in range(NS):
            if s + 3 < NS:
                prep(s + 3)
                yield
            nxt = samp_A(s + 1) if s + 1 < NS else None
            yield
            yield from samp_B(s, pend)
            pend = nxt
        reserved.discard(bank_index(obk_))
        smt = T("smt", [128, 8])
        S.op("dve", lambda e: e.tensor_reduce(smt[:, :], smat[:, :].rearrange("p (s h) -> p h s", h=8), AX.X, ALU.add), reads=B("smat"), writes=B("smt"))
        S.op("dve", lambda e: e.tensor_scalar(smt[:, :], smt[:, :], 1e-30, None, ALU.add), reads=B("smt"), writes=B("smt"))
        S.op("dve", lambda e: e.reciprocal(smt[:, :], smt[:, :]), reads=B("smt"), writes=B("smt"))
        ats, atsb = rot("attn", attn, "attn")
        for h in range(8):
            kvh, g = h // 4, h % 4
            ob = ostate["obank"][kvh]; obb = ostate["obankb"][kvh]
            col = g * 128 + kvh * 64
            S.op("act", lambda e, ob=ob, g=g, h=h, col=col: e.activation(ats[:, col:col + 64], ob[:, g * 64:(g + 1) * 64], AF.Copy, scale=smt[:, h:h + 1]),
                 reads=[obb, bufs["smt"]], writes=[atsb])
        yield
        transpose_to(ats, 128, 4, lambda: mixsT[:, 0:4, :], atsb, bufs["mixsT"], "dve")
        yield

        wout_block(lambda kc: mixsT[:, kc, 0:NS], NS, xs_t[:, :], B("xs_t"), B("mixsT"))
        yield
        h2s, h2sb = rot("hb", hb, "hb")
        rmsnorm(xs_t[:, :], NS, B("xs_t"), None, h2s[0:NS, :], h2sb)
        transpose_to(h2s, NS, 8, lambda: actT2[(NTILE - 1) % 2][:, :, TT:TT + NS], h2sb, bufs[f"actT{(NTILE - 1) % 2}"], "dve", gcm=g2b)


    stage(9)
    first_ffn = [True]
    prev_acc = [set()]
    ysems = [Buf(f"ysem{i}") for i in range(2 * NB)]
    xloaded = {}

    def load_x(t_):
        base_ = 128 + t_ * TT
        for b_ in range(NB):
            sl = (t_ % 2) * NB + b_
            S.dma("sp", [(xr[:, sl, :], xin[base_ + b_ * 128: base_ + (b_ + 1) * 128, :])], writes=[xrb[sl]])
            xloaded[(t_, b_)] = sl
    def conv_tail_gen(N, dst_fn, dstbuf):
        mbk, mbb = nextbank()

        def fM(e, mbk=mbk):
            ins = None
            for cc in range(4):
                ins = e.matmul(mbk[:, 0:N], onesf[:, :], cch[:, cc, 0:N], start=(cc == 0), stop=(cc == 3))
            return ins
        S.op("pe", fM, reads=B("onesf", "cch"), writes=[mbb])
        for cc in range(4):
            S.op("dve", lambda e, mbk=mbk, cc=cc: e.scalar_tensor_tensor(cch[:, cc, 0:N], mbk[:, 0:N], -1.0 / 512, cch[:, cc, 0:N], ALU.mult, ALU.add),
                 reads=[mbb] + B("cch"), writes=B("cch"))
        yield
        vbk, vbb = nextbank()
        vi = bank_index(vbk)
        reserved.add(vi)
        for cc in range(4):
            tq, tqb = rot("tmp", tmpA, "tmpA")
            S.op("act", lambda e, tq=tq, cc=cc: e.activation(tq[:, 0:N], cch[:, cc, 0:N], AF.Square), reads=B("cch"), writes=[tqb])
            S.op("pe", lambda e, tq=tq, cc=cc, vbk=vbk: e.matmul(vbk[:, 0:N], onesf[:, :], tq[:, 0:N], start=(cc == 0), stop=(cc == 3)),
                 reads=[tqb, bufs["onesf"]], writes=[vbb])
            if cc == 3:
                reserved.discard(vi)
            yield
        S.op("act", lambda e, vbk=vbk: e.activation(rstdb[:, 0:N], vbk[:, 0:N], AF.Ln, scale=1.0 / 512, bias=epsc[:, :]), reads=[vbb, bufs["epsc"]], writes=B("rstdb"))
        S.op("act", lambda e: e.activation(rstdb[:, 0:N], rstdb[:, 0:N], AF.Exp, scale=-0.5), reads=B("rstdb"), writes=B("rstdb"))
        yield
        for cc in range(4):
            S.op("dve", lambda e, cc=cc: e.tensor_tensor(cch[:, cc, 0:N], cch[:, cc, 0:N], rstdb[:, 0:N], ALU.mult), reads=B("cch", "rstdb"), writes=[cchb[cc]])
            S.op("dve", lambda e, cc=cc: e.tensor_scalar(cch[:, cc, 0:N], cch[:, cc, 0:N], cvec[:, 4 + cc:5 + cc], cvec[:, 8 + cc:9 + cc], ALU.mult, ALU.add),
                 reads=[cchb[cc], bufs["cvec"]], writes=[cchb[cc]])
        yield
        sgs = []
        for cc in range(4):
            if cc == 3:
                yield
            sgs.append(sigmoid_from(cch[:, cc, 0:N], 128, N, [cchb[cc]], -1.0, None))
            if cc >= 1:
                c2 = cc - 1
                sg, sgb = sgs[c2]
                S.op("dve", lambda e, c2=c2, sg=sg: e.tensor_tensor(dst_fn(c2), cch[:, c2, 0:N], sg[:, 0:N], ALU.mult), reads=[cchb[c2], sgb], writes=[dstbuf])
        yield
        sg, sgb = sgs[3]
        S.op("dve", lambda e, sg=sg: e.tensor_tensor(dst_fn(3), cch[:, 3, 0:N], sg[:, 0:N], ALU.mult), reads=[cchb[3], sgb], writes=[dstbuf])
        S.op("dve", lambda e: e.tensor_copy(zcol[:, :], zcol[:, :]), reads=cchb, writes=B("cch", "zcol"))
        yield

    def aphase(t):
        last = (t == NTILE - 1)
        dgs = {0: conv_diag(0), 1: conv_diag(1)}
        hs_ = []
        for b in range(NB):
            sl = xloaded[(t, b)]
            h_, hbb = rot("hb", hb, "hb")
            rmsnorm(xr[:, sl, :], 128, [xrb[sl]], None, h_[:, :], hbb)
            hs_.append((h_, hbb))
        yield
        yield
        yield
        for b in range(NB):
            h_, hbb = hs_[b]
            transpose_to(h_, 128, 8, lambda b=b: hT[:, :, b * 128:(b + 1) * 128], hbb, bufs["hT"], "dve", gcm=g1b)
            yield
        yield
        for g in range(4):
            bk, bb = win_fm(lambda kc: hT[:, kc, 0:TT], TT, CQ + g * 128, B("hT"))
            S.op("act", lambda e, bk=bk, g=g: e.activation(qT[:, g, :], bk[:, 0:TT], AF.Copy, scale=0.125), reads=[bb], writes=B("qT"))
            yield
        bk, bb = win_fm(lambda kc: hT[:, kc, 0:TT], TT, CK, B("hT"))
        S.op("act", lambda e, bk=bk: e.copy(kT[:, 128:128 + TT], bk[:, 0:TT]), reads=[bb], writes=B("kT"))
        yield
        for b in range(NB):
            bk, bb = win_tm(lambda kc, b=b: hT[:, kc, b * 128:(b + 1) * 128], 128, CK, 256, B("hT"))
            S.op("dve", lambda e, bk=bk, b=b: e.tensor_copy(vtok[:, 1 + b, :], bk[:, 128:256]), reads=[bb], writes=B("vtok"))
            if last and b == NB - 1:
                S.op("dve", lambda e, bk=bk: e.tensor_copy(kvtok, bk[:, 0:256]), reads=[bb], writes=B("kvtok"))
                S.dma("sp", [(nk_d, kvtok[:, 0:128]), (nv_d, kvtok[:, 128:256])], reads=B("kvtok"), sem_buf=bufs["kvtok"])
            yield
        for cc in range(4):
            ak, ab = win_fm(lambda kc: hT[:, kc, 0:TT], TT, CA + cc * 128, B("hT"))
            bk2, bb2 = win_fm(lambda kc: hT[:, kc, 0:TT], TT, CB + cc * 128, B("hT"))
            glu(ak[:, 0:TT], bk2[:, 0:TT], 128, TT, [ab], [bb2], uT[:, cc, 32:32 + TT], B("uT"))
            yield
        if last:
            ak, ab = win_tm(lambda kc: hT[:, kc, TT - 128:TT], 128, CA, 512, B("hT"))
            bk2, bb2 = win_tm(lambda kc: hT[:, kc, TT - 128:TT], 128, CB, 512, B("hT"))
            for half in range(2):
                hs_ = slice(half * 256, (half + 1) * 256)
                sg, sgb = sigmoid_from(bk2[:, hs_], 128, 256, [bb2], -1.0, None)
                S.op("dve", lambda e, ak=ak, sg=sg, hs_=hs_: e.tensor_tensor(utok[:, hs_], ak[:, hs_], sg[:, 0:256], ALU.mult), reads=[ab, sgb], writes=B("utok"))
            S.dma("sp", [(ncv_d, utok[:, :])], reads=B("utok"), sem_buf=bufs["utok"])
            yield

        def pr_A(b, t=t):
            return attn_A(b % 2, lambda g, kvh, b=b: qT[kvh * 64:(kvh + 1) * 64, g, b * 128:(b + 1) * 128],
                          lambda kvh, b=b: kT[kvh * 64:(kvh + 1) * 64, b * 128:b * 128 + 256],
                          B("qT"), B("kT"), (t == 0 and b == 0), None)

        def conv_chunks():
            for cc in range(4):
                dg, dgb = dgs[cc]
                bk, bb = nextbank()
                bi = bank_index(bk)
                reserved.add(bi)
                for j0, j1 in ((0, 16), (16, 31)):
                    def fC(e, bk=bk, dg=dg, cc=cc, j0=j0, j1=j1):
                        ins = None
                        for j in range(j0, j1):
                            ins = e.matmul(bk[:, 0:TT], dg[:, j, :], uT[:, cc, 2 + j:2 + j + TT], start=(j == 0), stop=(j == 30))
                        return ins
                    S.op("pe", fC, reads=[dgb, bufs["uT"]], writes=[bb])
                    if j1 == 31:
                        reserved.discard(bi)
                        S.op("act", lambda e, bk=bk, cc=cc: e.activation(cch[:, cc, 0:TT], bk[:, 0:TT], AF.Identity, bias=cvec[:, cc:cc + 1]),
                             reads=[bb, bufs["cvec"]], writes=B("cch"))
                        if cc + 2 < 4:
                            dgs[cc + 2] = conv_diag(cc + 2)
                    yield

        if t == 0:
            yield from conv_chunks()
            yield from conv_tail_gen(TT, lambda cc: mixT[:, 4 + cc, :], bufs["mixT"])
            cgen = iter(())
        else:
            cgen = conv_chunks()
        yield "ATTN"
        pend = pr_A(0)
        yield
        for b in range(NB):
            nxt = pr_A(b + 1) if b < NB - 1 else None
            yield
            bgen = attn_B(b % 2, pend, lambda half, kvh, b=b: vtok[:, b + half, kvh * 64:(kvh + 1) * 64], B("vtok"), "prompt", None,
                          filler=lambda: next(cgen, None))
            res = None
            while True:
                try:
                    next(bgen)
                    yield
                except StopIteration as stp:
                    res = stp.value
                    break
            at, atb = res
            transpose_to(at, 128, 4, lambda b=b: mixT[:, 0:4, b * 128:(b + 1) * 128], atb, bufs["mixT"], "dve")
            pend = nxt
            yield
        for _ in cgen:
            yield
        if t != 0:
            yield from conv_tail_gen(TT, lambda cc: mixT[:, 4 + cc, :], bufs["mixT"])
        if not last:
            S.op("dve", lambda e: e.tensor_copy(kT[:, 0:128], kT[:, TT:TT + 128]), reads=B("kT"), writes=B("kT"))
            S.op("dve", lambda e: e.tensor_copy(vtok[:, 0, :], vtok[:, NB, :]), reads=B("vtok"), writes=B("vtok"))
            S.op("dve", lambda e: e.tensor_copy(uT[:, :, 0:32], uT[:, :, TT:TT + 32]), reads=B("uT"), writes=B("uT"))
        yield
        for _ in range(8):
            yield
        aT = actT2[t % 2]; aTb = bufs[f"actT{t % 2}"]
        hs2 = []
        for b in range(NB):
            sl = xloaded[(t, b)]
            wout_block(lambda kc, b=b: mixT[:, kc, b * 128:(b + 1) * 128], 128, xr[:, sl, :], [xrb[sl]], B("mixT"))
            yield
            h_, hbb = rot("hb", hb, "hb")
            rmsnorm(xr[:, sl, :], 128, [xrb[sl]], None, h_[:, :], hbb)
            hs2.append((h_, hbb))
            yield
        yield
        yield
        for b in range(NB):
            h_, hbb = hs2[b]
            transpose_to(h_, 128, 8, lambda b=b, aT=aT: aT[:, :, b * 128:(b + 1) * 128], hbb, aTb, "dve", gcm=g2b)
            yield

    def run_all(g):
        for _ in g:
            pass

    sgen = sample_gen()
    load_x(0)
    gA0 = aphase(0)
    _SENT = object()
    pre_done = False
    while True:
        prog = False
        for _ in range(9):
            if next(sgen, _SENT) is not _SENT:
                prog = True
        for _ in range(2):
            if not pre_done:
                v_ = next(gA0, _SENT)
                if v_ == "ATTN" or v_ is _SENT:
                    pre_done = True
        if not prog:
            break
    run_all(gA0)
    def tile_setup(t):
        base = 128 + t * TT
        last = (t == NTILE - 1)
        aT = actT2[t % 2]; aTb = bufs[f"actT{t % 2}"]
        xs_ = [xloaded[(t, b)] for b in range(NB)]
        stage(13 + 10 * t)
        old_acc = prev_acc[0]
        fresh = [i for i in range(NBANK) if i not in old_acc and i not in reserved]
        stale = [i for i in range(NBANK) if i in old_acc]
        need = 2 * NB + (2 if last else 0)
        pick = fresh[:need // 2] + stale[:need - need // 2]
        pick += [i for i in fresh + stale if i not in pick][:need - len(pick)]
        flat = [(banks[i], bank_bufs[i]) for i in pick[:need]]
        acc = [[flat[2 * b], flat[2 * b + 1]] for b in range(len(flat) // 2)]
        for row in acc:
            for a_ in row:
                reserved.add(bank_index(a_[0]))
        prev_acc[0] = set(bank_index(a_[0]) for row in acc for a_ in row)
        rest_fresh = [i for i in fresh if i not in prev_acc[0]]
        if rest_fresh:
            bank_ctr[0] = rest_fresh[0]
        accbufs = [a_[1] for row in acc for a_ in row]
        grp = {}

        def emit_U(fg, last=last, aT=aT, aTb=aTb):
            wu, wub = rot("wup", wupb, "wupb")
            wd, wdb = rot("wdn", wdnb, "wdnb")
            hd, hdb = rot("hid", hidr, "hidr")
            grp[fg] = (wd, wdb, hd, hdb)
            extra = arena_alias if first_ffn[0] else []
            S.dma("sp", [(wu, wup_s[fg * 128:(fg + 1) * 128, :].rearrange("p (k f) -> p k f", k=8))], reads=[wsc[fg]], writes=[wub] + extra)
            S.dma("sp", [(wd, wdn_s[fg * 128:(fg + 1) * 128, :].rearrange("p (c n) -> p c n", c=2))], reads=[wsc[16 + fg]], writes=[wdb] + extra)
            if not last:
                bk, bb = nextbank()

                def fU(e, bk=bk, wu=wu, aT=aT):
                    ins = None
                    for fc in range(2):
                        for kc in range(8):
                            ins = e.matmul(bk[:, fc * TT:(fc + 1) * TT], wu[:, kc, fc * 128:(fc + 1) * 128], aT[:, kc, 0:TT], start=(kc == 0), stop=(kc == 7))
                    return ins
                S.op("pe", fU, reads=[wub, aTb], writes=[bb])
                r_, rb_ = rot("rl", rl, "rl")
                S.op("act", lambda e, bk=bk, r_=r_: e.activation(r_[:, :, 0:TT], bk[:, :].rearrange("p (c n) -> p c n", c=2), AF.Relu), reads=[bb], writes=[rb_])
                S.op("dve", lambda e, r_=r_, hd=hd: e.tensor_tensor(hd[:, :, 0:TT], r_[:, :, 0:TT], r_[:, :, 0:TT], ALU.mult), reads=[rb_], writes=[hdb])
                return
            for fc in range(2):
                bk, bb = nextbank()

                def fU(e, bk=bk, wu=wu, fc=fc, aT=aT):
                    ins = None
                    for kc in range(8):
                        ins = e.matmul(bk[:, 0:TT], wu[:, kc, fc * 128:(fc + 1) * 128], aT[:, kc, 0:TT], start=(kc == 0), stop=(kc == 7))
                    for kc in range(8):
                        ins = e.matmul(bk[:, TT:TT + NS], wu[:, kc, fc * 128:(fc + 1) * 128], aT[:, kc, TT:TT + NS], start=(kc == 0), stop=(kc == 7))
                    return ins
                S.op("pe", fU, reads=[wub, aTb], writes=[bb])
                ncol = TT + NS
                r_, rb_ = rot("rl", rl, "rl")
                S.op("act", lambda e, bk=bk, r_=r_, ncol=ncol: e.activation(r_[:, 0, 0:ncol], bk[:, 0:ncol], AF.Relu), reads=[bb], writes=[rb_])
                S.op("dve", lambda e, r_=r_, hd=hd, fc=fc, ncol=ncol: e.tensor_tensor(hd[:, fc, 0:ncol], r_[:, 0, 0:ncol], r_[:, 0, 0:ncol], ALU.mult), reads=[rb_], writes=[hdb])

        def emit_D(fg, last=last, acc=acc):
            wd, wdb, hd, hdb = grp[fg]

            def fD(e, wd=wd, hd=hd, fg=fg, acc=acc):
                ins = None
                for fc in range(2):
                    f = fg * 2 + fc
                    for b in range(NB):
                        for half in range(2):
                            ins = e.matmul(acc[b][half][0][:, :], hd[:, fc, b * 128:(b + 1) * 128], wd[:, fc, half * 512:(half + 1) * 512],
                                           start=(f == 0), stop=(f == 31))
                    if last:
                        for half in range(2):
                            ins = e.matmul(acc[NB][half][0][0:NS, :], hd[:, fc, TT:TT + NS], wd[:, fc, half * 512:(half + 1) * 512],
                                           start=(f == 0), stop=(f == 31))
                return ins
            S.op("pe", fD, reads=[wdb, hdb], writes=accbufs)


        return dict(t=t, last=last, acc=acc, accbufs=accbufs, grp=grp, emit_U=emit_U, emit_D=emit_D, xs_=xs_)

    def tile_body(cx):
        t = cx["t"]; last = cx["last"]; emit_U = cx["emit_U"]; emit_D = cx["emit_D"]
        nxtA = None
        for fg in range(16):
            if fg + 1 < 16:
                emit_U(fg + 1)
            if fg == 2:
                first_ffn[0] = False
            emit_D(fg)
            if t == SAMPLE_TILE:
                if fg == 1:
                    load_x(t + 1)
                if fg < 7:
                    for _ in range(14):
                        next(sgen, None)
                elif fg == 7:
                    run_all(sgen)
                    nxtA = aphase(t + 1)
                    next(nxtA, None)
                elif fg >= 9:
                    for _ in range(13):
                        next(nxtA, None)
                continue
            if fg == 1 and not last:
                load_x(t + 1)
                nxtA = aphase(t + 1)
                next(nxtA, None)
            if nxtA is not None and fg >= 2:
                for _ in range(APF):
                    next(nxtA, None)
        return nxtA

    def tile_release(cx):
        acc = cx["acc"]
        first_ffn[0] = False
        for row in acc:
            for a_ in row:
                reserved.discard(bank_index(a_[0]))

    def tile_epilogue(cx):
        t = cx["t"]; last = cx["last"]; acc = cx["acc"]; xs_ = cx["xs_"]
        stage(14 + 10 * t)
        for b in range(NB):
            for half in range(2):
                S.op("dve", lambda e, b=b, half=half, acc=acc, sl=xs_[b]: e.tensor_tensor(xr[:, sl, half * 512:(half + 1) * 512], xr[:, sl, half * 512:(half + 1) * 512],
                                                                     acc[b][half][0][:, :], ALU.add), reads=[acc[b][half][1], xrb[xs_[b]]], writes=[xrb[xs_[b]]])
            rmsnorm(xr[:, xs_[b], :], 128, [xrb[xs_[b]]], gfb, xr[:, xs_[b], :], xrb[xs_[b]], junk=(jkt[:, :], bufs["jkt"]))
            S.dma("pool", [(y_d[t * TT + b * 128: t * TT + (b + 1) * 128, :], xr[:, xs_[b], :])], reads=[xrb[xs_[b]]], sem_buf=ysems[xs_[b]])
        if last:
            for half in range(2):
                S.op("dve", lambda e, half=half, acc=acc: e.tensor_tensor(xs_t[:, half * 512:(half + 1) * 512], xs_t[:, half * 512:(half + 1) * 512],
                                                                acc[NB][half][0][0:NS, :], ALU.add), reads=[acc[NB][half][1]] + B("xs_t"), writes=B("xs_t"))
            rmsnorm(xs_t[:, :], NS, B("xs_t"), gfb, xr[0:NS, 0, :], xrb[0])
            S.dma("sp", [(ys_d, xr[0:NS, 0, :])], reads=[xrb[0]], sem_buf=xrb[0])

    cx = tile_setup(0)
    cx["emit_U"](0)
    for t in range(NTILE):
        nxtA = tile_body(cx)
        first_ffn[0] = False
        if nxtA is not None:
            run_all(nxtA)
        cur = cx
        if t + 1 < NTILE:
            cx = tile_setup(t + 1)
            cx["emit_U"](0)
        tile_epilogue(cur)
        keep_ = set(bank_index(a_[0]) for row in cx["acc"] for a_ in row) if cx is not cur else set()
        for row in cur["acc"]:
            for a_ in row:
                if bank_index(a_[0]) not in keep_:
                    reserved.discard(bank_index(a_[0]))


def _prep_shared(meta_tokens, rel_bias, norm1_g, w_in, attn_sinks, conv_w, conv_b, conv_ln_g, conv_ln_b,
                 w_out, norm2_g, w_up, w_down, norm_f_g):
    f = np.float32
    perm = _pair_perm()
    wi = np.asarray(w_in[0], f)
    cols = np.concatenate([perm, np.arange(512, 1792)])
    wi = wi[:, cols]
    win = np.ascontiguousarray(wi.reshape(8, 128, 1792).transpose(1, 0, 2))
    wo = np.asarray(w_out[0], f)
    rows = np.concatenate([perm, np.arange(512, 1024)])
    wo = wo[rows, :]
    wout = np.ascontiguousarray(wo.reshape(8, 128, 1024).transpose(1, 0, 2))
    wu = np.asarray(w_up[0], f)
    wup = np.ascontiguousarray(wu.reshape(8, 128, 16, 256).transpose(2, 1, 0, 3)).reshape(16 * 128, 2048)
    wd = np.asarray(w_down[0], f)
    wdn = np.ascontiguousarray(wd.reshape(16, 2, 128, 1024).transpose(0, 2, 1, 3)).reshape(16 * 128, 2048)
    cw = np.asarray(conv_w[0], f)
    cwcm = np.ascontiguousarray(cw.T.reshape(4, 128, 31).transpose(1, 0, 2))
    cvec = np.ascontiguousarray(np.concatenate([np.asarray(conv_b[0], f).reshape(4, 128).T,
                                                np.asarray(conv_ln_g[0], f).reshape(4, 128).T,
                                                np.asarray(conv_ln_b[0], f).reshape(4, 128).T], axis=1))
    rbx = np.concatenate([np.asarray(rel_bias, f), np.full((1, 8), NEG, f)], axis=0)
    onehot = np.zeros((33, 384), f)
    for j in range(384):
        dist = 255 - j
        if 0 <= dist <= 128:
            onehot[int(_t5_bucket_np(np.array([dist]))[0]), j] = 1.0
        else:
            onehot[32, j] = 1.0
    jmat = np.ascontiguousarray(np.eye(128, dtype=f)[::-1])
    ident = np.eye(128, dtype=f)
    rowmask = np.full((128, NS), NEG, f)
    for s in range(NS):
        rowmask[s, s] = 0.0
    ind = np.zeros((96, NS), f)
    for s in range(NS):
        ind[s * 6:(s + 1) * 6, s] = 1.0
    cwrep = np.ascontiguousarray(np.tile(cw[:30].reshape(6, 2560), (NS, 1)))
    srow = np.ascontiguousarray(np.stack([cw[30], np.asarray(conv_b[0], f), np.asarray(conv_ln_g[0], f), np.asarray(conv_ln_b[0], f)]))
    return dict(win=win, wout=wout, wup=wup, wdn=wdn, g1cm=np.ascontiguousarray(np.asarray(norm1_g[0], f).reshape(8, 128).T), g2cm=np.ascontiguousarray(np.asarray(norm2_g[0], f).reshape(8, 128).T),
                gf=np.asarray(norm_f_g, f), cwcm=cwcm, cvec=cvec, rbx=rbx, sinks=np.asarray(attn_sinks[0], f),
                onehot=onehot, jmat=jmat, ident=ident, rowmask=rowmask, ind=ind, cwrep=cwrep)


_NC_CACHE = {}


def kernel(x_prompt, x_sample, cache_k, cache_v, state_conv, meta_tokens, rel_bias,
           norm1_g, w_in, attn_sinks, conv_w, conv_b, conv_ln_g, conv_ln_b,
           w_out, norm2_g, w_up, w_down, norm_f_g):
    f = np.float32
    shared = _prep_shared(meta_tokens, rel_bias, norm1_g, w_in, attn_sinks, conv_w, conv_b, conv_ln_g, conv_ln_b,
                          w_out, norm2_g, w_up, w_down, norm_f_g)
    xp = np.asarray(x_prompt, f)[0]
    halo0 = np.concatenate([np.zeros((112, D), f), np.asarray(meta_tokens, f)], axis=0)
    in_maps = []
    for c in range(NCORES):
        halo = halo0 if c == 0 else xp[c * 2048 - 128:c * 2048]
        m = dict(shared)
        m["xin"] = np.ascontiguousarray(np.concatenate([halo, xp[c * 2048:(c + 1) * 2048]], axis=0))
        m["xs"] = np.ascontiguousarray(np.asarray(x_sample, f)[c * NS:(c + 1) * NS, 0, :])
        m["ck"] = np.ascontiguousarray(np.asarray(cache_k, f)[0, c * NS:(c + 1) * NS].reshape(NS, 128, 128))
        m["cv"] = np.ascontiguousarray(np.asarray(cache_v, f)[0, c * NS:(c + 1) * NS].reshape(NS, 128, 128))
        m["sc"] = np.ascontiguousarray(np.asarray(state_conv, f)[0, c * NS:(c + 1) * NS])
        km = np.zeros((256,), f)
        if c == 0:
            km[:112] = NEG
        m["kmask"] = km
        in_maps.append(m)
    if "nc" not in _NC_CACHE:
        _NC_CACHE["nc"] = build_program()
    nc = _NC_CACHE["nc"]
    res = run_bass_kernel_spmd(nc, in_maps, core_ids=list(range(NCORES)))
    R = res.results
    y_prompt = np.concatenate([R[c]["y"] for c in range(NCORES)], axis=0)[None]
    y_sample = np.concatenate([R[c]["ys"] for c in range(NCORES)], axis=0)[:, None, :]
    nk = R[NCORES - 1]["nk"].reshape(1, 1, 128, 2, 64)
    nv = R[NCORES - 1]["nv"].reshape(1, 1, 128, 2, 64)
    ncv = R[NCORES - 1]["ncv"][98:128].reshape(1, 1, 30, 512)
    nks = np.concatenate([R[c]["nks"] for c in range(NCORES)], axis=0).reshape(1, 128, 128, 2, 64)
    nvs = np.concatenate([R[c]["nvs"] for c in range(NCORES)], axis=0).reshape(1, 128, 128, 2, 64)
    ncs = np.concatenate([R[c]["ncs"] for c in range(NCORES)], axis=0).reshape(1, 128, 30, 512)
    return (y_prompt.astype(f), y_sample.astype(f), nk.astype(f), nv.astype(f), ncv.astype(f),
            nks.astype(f), nvs.astype(f), ncs.astype(f))
```

```python
import math
import numpy as np
import concourse.bass as bass
import concourse.mybir as mybir
from concourse.bass_utils import run_bass_kernel_spmd

F32 = mybir.dt.float32
BF16 = mybir.dt.bfloat16
AF = mybir.ActivationFunctionType
ALU = mybir.AluOpType
AX = mybir.AxisListType

ENGS = ("pe", "act", "dve", "pool", "sp")
NEG = -1.0e30
EPS = 1e-6
NCORES = 8
NTILE = 8
TT = 256
NB = TT // 128
SAMPLE_TILE = -1
APF = 5
D = 1024
DFF = 4096
NS = 16
SNC = 128 + NS


class Buf:
    __slots__ = ("name", "last_w", "readers", "dsem", "dcount")

    def __init__(self, name):
        self.name = name
        self.last_w = None
        self.readers = []
        self.dsem = None
        self.dcount = 0


class Op:
    __slots__ = ("eng", "fn", "deps", "idx", "pos", "sig", "tick", "dma", "dval", "dbuf")

    def __init__(self, eng, fn, deps):
        self.eng = eng
        self.fn = fn
        self.deps = deps
        self.sig = False
        self.tick = 0
        self.dma = False
        self.dval = 0
        self.dbuf = None


class Sched:
    def __init__(self, nc):
        self.nc = nc
        self.ops = []
        self.streams = {e: [] for e in ENGS}
        self.dma_bufs = []
        self.prepared = False

    def _mk(self, eng, fn, reads, writes):
        deps = []
        seen = set()
        for b in reads:
            if b.last_w is not None and id(b.last_w) not in seen:
                seen.add(id(b.last_w)); deps.append(b.last_w)
            if b.name.startswith("ps"):
                for r in b.readers:
                    if r.eng != eng and id(r) not in seen:
                        seen.add(id(r)); deps.append(r)
        for b in writes:
            if b.last_w is not None and id(b.last_w) not in seen:
                seen.add(id(b.last_w)); deps.append(b.last_w)
            for r in b.readers:
                if id(r) not in seen:
                    seen.add(id(r)); deps.append(r)
        o = Op(eng, fn, deps)
        o.idx = len(self.ops)
        o.pos = len(self.streams[eng])
        self.ops.append(o)
        self.streams[eng].append(o)
        for b in reads:
            b.readers.append(o)
        for b in writes:
            b.last_w = o
            b.readers = []
        return o

    def op(self, eng, fn, reads=(), writes=()):
        return self._mk(eng, fn, list(reads), list(writes))

    def dma(self, eng, pairs, reads=(), writes=(), sem_buf=None):
        reads = list(reads); writes = list(writes)
        if sem_buf is None:
            sem_buf = writes[0]
        o = self._mk(eng, None, reads, writes)
        o.dma = True
        o.fn = pairs
        o.dbuf = sem_buf
        sem_buf.dcount += 16 * len(pairs)
        o.dval = sem_buf.dcount
        if sem_buf not in self.dma_bufs:
            self.dma_bufs.append(sem_buf)
        return o

    @staticmethod
    def _skip(o, d):
        if o.dma:
            return False
        if d.eng == o.eng:
            if d.eng == "pe":
                return True
            if (o.pos - d.pos) > 3:
                return True
        return False

    def prepare(self):
        nc = self.nc
        for o in self.ops:
            for d in o.deps:
                if not d.dma and not self._skip(o, d):
                    d.sig = True
        self.esem = {}
        for e in ENGS:
            if any(o.sig for o in self.streams[e]):
                self.esem[e] = nc.alloc_semaphore(name=f"es_{e}")
        for b in self.dma_bufs:
            b.dsem = nc.alloc_semaphore(name=f"ds_{b.name}")
        for e in ENGS:
            t = 0
            for o in self.streams[e]:
                if o.sig:
                    t += 1
                o.tick = t
        self.prepared = True

    def emit_one(self, e, h, final_eng="sp"):
        if not self.prepared:
            self.prepare()
        esem = self.esem
        waited = {}
        for o in self.streams[e]:
            need = {}
            for d in o.deps:
                if d.dma:
                    key = ("d", id(d.dbuf)); sem = d.dbuf.dsem; val = d.dval
                else:
                    if self._skip(o, d):
                        continue
                    key = ("e", d.eng); sem = esem[d.eng]; val = d.tick
                if key not in need or need[key][1] < val:
                    need[key] = (sem, val)
            for key, (sem, val) in need.items():
                if waited.get(key, 0) >= val:
                    continue
                waited[key] = val
                h.wait_ge(sem, val)
            if o.dma:
                for (out_ap, in_ap) in o.fn:
                    h.dma_start(out=out_ap, in_=in_ap).then_inc(o.dbuf.dsem, 16)
            else:
                ins = o.fn(h)
                if o.sig:
                    ins.then_inc(esem[e], 1)
        if e == final_eng:
            for b in self.dma_bufs:
                if b.dcount > 0:
                    h.wait_ge(b.dsem, b.dcount)


def _t5_bucket_np(d):
    max_exact = 16
    d = np.asarray(d)
    d_f = np.maximum(d, 1).astype(np.float32)
    large = max_exact + (np.log(d_f / np.float32(max_exact)) / np.float32(math.log(128 / max_exact))
                         * np.float32(32 - max_exact)).astype(np.int32)
    large = np.minimum(large, 31)
    return np.where(d < max_exact, d, large)


def _pair_perm():
    idx = np.zeros(512, dtype=np.int64)
    for g in range(4):
        for kvh in range(2):
            for d in range(64):
                idx[g * 128 + kvh * 64 + d] = (kvh * 4 + g) * 64 + d
    return idx


class _Stop(Exception):
    pass


def build_program():
    import os
    nc = bass.Bass("TRN2", target_bir_lowering=False)
    S = Sched(nc)
    STAGE = int(os.environ.get("KSTAGE", "99"))

    def stage(n):
        if STAGE < n:
            raise _Stop()
    try:
        _record(nc, S, stage)
    except _Stop:
        pass
    with nc.Block() as block:
        @block.tensor
        def _(e): S.emit_one("pe", e)

        @block.scalar
        def _(e): S.emit_one("act", e)

        @block.vector
        def _(e): S.emit_one("dve", e)

        @block.gpsimd
        def _(e): S.emit_one("pool", e)

        @block.sync
        def _(e): S.emit_one("sp", e)
    return nc


def _record(nc, S, stage):
    import os

    def din(name, shape, dt=F32):
        return nc.dram_tensor(name, list(shape), dt, kind="ExternalInput").ap()

    def dout(name, shape):
        return nc.dram_tensor(name, list(shape), F32, kind="ExternalOutput").ap()

    xin = din("xin", [128 + NTILE * TT, D])
    xs_d = din("xs", [NS, D])
    ck_d = din("ck", [NS, 128, 128])
    cv_d = din("cv", [NS, 128, 128])
    sc_d = din("sc", [NS, 30, 512])
    win_d = din("win", [128, 8, 1792])
    wout_d = din("wout", [128, 8, 1024])
    wup_d = din("wup", [16 * 128, 2048])
    wdn_d = din("wdn", [16 * 128, 2048])
    g1_d = din("g1cm", [128, 8]); g2_d = din("g2cm", [128, 8]); gf_d = din("gf", [D])
    cw_d = din("cwcm", [128, 4, 31])
    cvec_d = din("cvec", [128, 12])
    rbx_d = din("rbx", [33, 8])
    sinks_d = din("sinks", [8])
    oh_d = din("onehot", [33, 384])
    J_d = din("jmat", [128, 128])
    id_d = din("ident", [128, 128])
    kmask_d = din("kmask", [256])
    rowm_d = din("rowmask", [128, NS])
    ind_d = din("ind", [96, NS])
    cwrep_d = din("cwrep", [96, 2560])

    y_d = dout("y", [NTILE * TT, D])
    ys_d = dout("ys", [NS, D])
    nk_d = dout("nk", [128, 128]); nv_d = dout("nv", [128, 128])
    ncv_d = dout("ncv", [128, 512])
    nks_d = dout("nks", [NS, 128, 128]); nvs_d = dout("nvs", [NS, 128, 128])
    ncs_d = dout("ncs", [NS, 30, 512])

    wup_s = nc.dram_tensor("wup_s", [16 * 128, 2048], BF16, kind="Internal").ap()
    wdn_s = nc.dram_tensor("wdn_s", [16 * 128, 2048], BF16, kind="Internal").ap()
    biasG = nc.dram_tensor("biasG", [8, 384], F32, kind="Internal").ap()

    bufs = {}

    def T(name, shape, dt=F32):
        t = nc.alloc_sbuf_tensor("s_" + name, list(shape), dt)
        bufs[name] = Buf(name)
        return t

    def B(*names):
        return [bufs[n] for n in names]

    NBANK = 8
    banks = [nc.alloc_psum_tensor(f"ps{i}", [128, 512], F32) for i in range(NBANK)]
    bank_bufs = [Buf(f"ps{i}") for i in range(NBANK)]
    bank_ctr = [0]

    reserved = set()

    def nextbank():
        while True:
            i = bank_ctr[0] % NBANK
            bank_ctr[0] += 1
            if i not in reserved:
                return banks[i], bank_bufs[i]

    def bank_index(bk_):
        for i_, b_ in enumerate(banks):
            if b_ is bk_:
                return i_
        raise KeyError

    win = T("win", [128, 8, 1792], BF16)
    wout = T("wout", [128, 8, 1024], BF16)
    g1b = T("g1b", [128, 8]); g2b = T("g2b", [128, 8]); gfb = T("gfb", [128, D])
    cwcm = T("cwcm", [128, 4, 31]); cvec = T("cvec", [128, 12]); ncvec = T("ncvec", [128, 12])
    identf = T("identf", [128, 128]); identb = T("identb", [128, 128], BF16)
    jf = T("jf", [128, 128]); onesf = T("onesf", [128, 128])
    rowm = T("rowm", [128, NS]); zcol = T("zcol", [128, 1])
    bias = T("bias", [128, 8, 256])
    kmask = T("kmask", [128, 256])
    Ssb2 = [T(f"Ssbx{p}", [128, 8, 260]) for p in range(2)]
    for p_ in range(2):
        for hp_ in range(4):
            bufs[f"Ssb{p_}_{hp_}"] = Buf(f"Ssb{p_}_{hp_}")
    sinkb = T("sinkb", [128, 8])

    S.dma("sp", [(identf[:], id_d), (jf[:], J_d)], writes=B("identf", "jf"), sem_buf=bufs["identf"])
    S.dma("sp", [(cwcm[:], cw_d), (cvec[:], cvec_d), (rowm[:], rowm_d)], writes=B("cwcm", "cvec", "rowm"), sem_buf=bufs["cwcm"])
    S.dma("sp", [(g1b[:], g1_d), (g2b[:], g2_d),
                 (gfb[:], bass.AP(gf_d.tensor, 0, [[0, 128], [1, D]])),
                 (kmask[:], bass.AP(kmask_d.tensor, 0, [[0, 128], [1, 256]])),
                 (sinkb[:], bass.AP(sinks_d.tensor, 0, [[0, 128], [1, 8]]))],
          writes=B("g1b", "g2b", "gfb", "kmask", "sinkb"), sem_buf=bufs["g1b"])
    S.dma("pool", [(win[:, kc, :], win_d[:, kc, :]) for kc in range(8)], writes=B("win"))
    S.dma("pool", [(wout[:, kc, :], wout_d[:, kc, :]) for kc in range(8)], writes=B("wout"))
    S.op("dve", lambda e: e.tensor_copy(identb[:], identf[:]), reads=B("identf"), writes=B("identb"))
    cwcmb = T("cwcmb", [128, 4, 31], BF16)
    S.op("dve", lambda e: e.tensor_copy(cwcmb[:], cwcm[:]), reads=B("cwcm"), writes=B("cwcmb"))
    S.op("pool", lambda e: e.memset(onesf[:], 1.0), writes=B("onesf"))
    S.op("pool", lambda e: e.memset(zcol[:], 0.0), writes=B("zcol"))
    S.op("dve", lambda e: e.tensor_scalar(ncvec[:], cvec[:], -1.0, None, ALU.mult), reads=B("cvec"), writes=B("ncvec"))
    for p_ in range(2):
        S.op("dve", lambda e, p_=p_: e.tensor_copy(Ssb2[p_][:, :, 256], sinkb[:]), reads=B("sinkb"),
             writes=[bufs[f"Ssb{p_}_{hp_}"] for hp_ in range(4)])

    stage(1)
    rbx = T("rbx", [33, 8]); cch = T("cch", [128, 4, TT])
    arena = T("arena", [128, 12288], BF16)
    biasT = arena[:, 0:4096].bitcast(F32); bufs["biasT"] = Buf("biasT")
    utok = T("utok", [128, 512]); kvtok = utok[:, 0:256]; bufs["kvtok"] = bufs["utok"]
    oh = biasT[0:33, 0:384]; bufs["oh"] = bufs["biasT"]; gsb = utok[0:8, 0:384]; bufs["gsb"] = bufs["utok"]
    S.dma("sp", [(rbx[:], rbx_d), (oh, oh_d)], writes=B("rbx", "oh"), sem_buf=bufs["rbx"])
    bk, bb = nextbank()
    S.op("pe", lambda e: e.matmul(bk[0:8, 0:384], rbx[:, :], oh, start=True, stop=True), reads=B("rbx", "oh"), writes=[bb])
    S.op("dve", lambda e: e.tensor_copy(gsb, bk[0:8, 0:384]), reads=[bb], writes=B("gsb"))
    bufs["biasG"] = Buf("biasG")
    S.dma("sp", [(biasG, gsb)], reads=B("gsb"), writes=B("biasG"))
    S.dma("sp", [(biasT[:, h * 256:(h + 1) * 256], bass.AP(biasG.tensor, h * 384, [[1, 128], [1, 256]])) for h in range(8)],
          reads=B("biasG"), writes=B("biasT"))
    def bias_flip():
        for hp in range(4):
            bk, bb = nextbank()
            S.op("pe", lambda e, bk=bk, hp=hp: e.matmul(bk[:, :], jf[:, :], biasT[:, hp * 512:(hp + 1) * 512], start=True, stop=True),
                 reads=B("jf", "biasT"), writes=[bb])
            S.op("dve", lambda e, bk=bk, hp=hp: e.tensor_copy(bias[:, 2 * hp:2 * hp + 2, :], bk[:, :].rearrange("p (a b) -> p a b", a=2)),
                 reads=[bb], writes=B("bias"))

    xr = T("xr", [128, 2 * NB, D])
    xrb = [Buf(f"xr_{i}") for i in range(2 * NB)]
    hb = [T(f"hb{i}", [128, D], BF16) for i in range(2)]
    actT2 = [T(f"actT{i}", [128, 8, TT + 16], BF16) for i in range(2)]
    hT = T("hT", [128, 8, TT], BF16)
    qT = T("qT", [128, 4, TT], BF16)
    kT = T("kT", [128, 128 + TT], BF16)
    vtok = T("vtok", [128, NB + 1, 128], BF16)
    uT = T("uT", [128, 4, 32 + TT], BF16)
    Pb = [T(f"Pb{i}", [128, 2, 260], BF16) for i in range(2)]
    PTs = [T(f"PTs{i}", [128, 4, 128], BF16) for i in range(2)]
    attn = [T(f"attn{i}", [128, 512], BF16) for i in range(2)]
    mixT = T("mixT", [128, 8, TT], BF16)
    diag = [T(f"diag{i}", [128, 31, 128], BF16) for i in range(2)]
    tmpA = [T(f"tmpA{i}", [128, TT]) for i in range(3)]
    cchb = [Buf(f"cch_{i}") for i in range(4)]
    rstdb = T("rstdb", [128, TT])
    wupb = [arena[:, i * 2048:(i + 1) * 2048].rearrange("p (k f) -> p k f", k=8) for i in range(3)]
    wdnb = [arena[:, 6144 + i * 2048:6144 + (i + 1) * 2048].rearrange("p (c n) -> p c n", c=2) for i in range(3)]
    for i in range(3):
        bufs[f"wupb{i}"] = Buf(f"wupb{i}")
        bufs[f"wdnb{i}"] = Buf(f"wdnb{i}")
    hidr = [T(f"hidr{i}", [128, 2, TT + 16], BF16) for i in range(3)]
    rl = [T(f"rl{i}", [128, 2, TT + 16]) for i in range(2)]
    jkt = T("jkt", [128, D], BF16)
    NST = 24
    stt = [T(f"st{i}", [128, 4]) for i in range(NST)]
    st_ctr = [0]

    def nextst():
        i = st_ctr[0] % NST
        st_ctr[0] += 1
        return stt[i], bufs[f"st{i}"]

    rr = {"hb": 0, "P": 0, "PT": 0, "attn": 0, "tmp": 0, "rl": 0, "wup": 0, "wdn": 0, "diag": 0, "hid": 0}

    def rot(key, lst, prefix):
        i = rr[key] % len(lst)
        rr[key] += 1
        return lst[i], bufs[f"{prefix}{i}"]

    def rmsnorm(xap, P, xbufs, gb, outap, outbuf, junk=None):
        st, stb = nextst()
        jap, jbuf = (outap, outbuf) if junk is None else junk
        S.op("act", lambda e: e.activation(jap, xap, AF.Square, accum_out=st[0:P, 0:1]), reads=xbufs, writes=[jbuf, stb])
        S.op("act", lambda e: e.activation(st[0:P, 1:2], st[0:P, 0:1], AF.Ln, scale=1.0 / D, bias=epsc[0:P, :]), reads=[stb, bufs["epsc"]], writes=[stb])
        S.op("act", lambda e: e.activation(st[0:P, 2:3], st[0:P, 1:2], AF.Exp, scale=-0.5), reads=[stb], writes=[stb])
        if gb is None:
            S.op("dve", lambda e: e.tensor_scalar(outap, xap, st[0:P, 2:3], None, ALU.mult), reads=xbufs + [stb], writes=[outbuf])
        else:
            S.op("dve", lambda e: e.scalar_tensor_tensor(outap, xap, st[0:P, 2:3], gb[0:P, :], ALU.mult, ALU.mult),
                 reads=xbufs + [stb, gbuf[id(gb)]], writes=[outbuf])

    epsc = T("epsc", [128, 1])
    onec = T("onec", [128, 1])
    S.op("pool", lambda e: e.memset(epsc[:], EPS), writes=B("epsc"))
    S.op("pool", lambda e: e.memset(onec[:], 1.0), writes=B("onec"))
    gbuf = {id(g1b): bufs["g1b"], id(g2b): bufs["g2b"], id(gfb): bufs["gfb"]}

    def transpose_to(src_bf, P, nchunk, dst_fn, srcbuf, dstbuf, evac_eng, gcm=None):
        bk, bb = nextbank()
        pv = bk[:, :].bitcast(BF16).rearrange("p (c n) -> p c n", c=8)

        def f(e):
            ins = None
            for c in range(nchunk):
                ins = e.transpose(pv[:, c, 0:P], src_bf[0:P, c * 128:(c + 1) * 128], identb[0:P, 0:P])
            return ins
        S.op("pe", f, reads=[srcbuf, bufs["identb"]], writes=[bb])
        if gcm is not None:
            S.op("dve", lambda e: e.tensor_tensor(dst_fn(), pv[:, 0:nchunk, 0:P], gcm[:, 0:nchunk].unsqueeze(2).to_broadcast([128, nchunk, P]), ALU.mult),
                 reads=[bb, gbuf[id(gcm)]], writes=[dstbuf])
        elif evac_eng == "act":
            S.op("act", lambda e: e.copy(dst_fn(), pv[:, 0:nchunk, 0:P]), reads=[bb], writes=[dstbuf])
        else:
            S.op("dve", lambda e: e.tensor_copy(dst_fn(), pv[:, 0:nchunk, 0:P]), reads=[bb], writes=[dstbuf])

    def sigmoid_from(src_ap, P, N, srcbufs, scale_neg, bias_neg):
        t, tb = rot("tmp", tmpA, "tmpA")
        kw = {}
        if bias_neg is not None:
            kw["bias"] = bias_neg
        S.op("act", lambda e: e.activation(t[0:P, 0:N], src_ap, AF.Exp, scale=scale_neg, **kw), reads=srcbufs, writes=[tb])
        S.op("act", lambda e: e.activation(t[0:P, 0:N], t[0:P, 0:N], AF.Ln, bias=onec[0:P, :]), reads=[tb, bufs["onec"]], writes=[tb])
        S.op("act", lambda e: e.activation(t[0:P, 0:N], t[0:P, 0:N], AF.Exp, scale=-1.0), reads=[tb], writes=[tb])
        return t, tb

    def attn_A(par, q_ap_fn, kT_ap_fn, qbufs, kbufs, first_mask, rowmask_col, ncol=256):
        Sx = Ssb2[par]
        sts = []
        for hp in range(4):
            sbk, sbb = nextbank()
            heads = [2 * hp, 2 * hp + 1]
            sxb = bufs[f"Ssb{par}_{hp}"]

            def fS(e, sbk=sbk, heads=heads):
                ins = None
                for i, h in enumerate(heads):
                    kvh, g = h // 4, h % 4
                    ins = e.matmul(sbk[:, i * 256:i * 256 + ncol], q_ap_fn(g, kvh), kT_ap_fn(kvh), start=True, stop=True)
                return ins
            S.op("pe", fS, reads=qbufs + kbufs, writes=[sbb])
            S.op("dve", lambda e, sbk=sbk, hp=hp, Sx=Sx: e.tensor_tensor(Sx[:, 2 * hp:2 * hp + 2, 0:ncol], sbk[:, :].rearrange("p (a b) -> p a b", a=2)[:, :, 0:ncol],
                                                                         bias[:, 2 * hp:2 * hp + 2, 0:ncol], ALU.add),
                 reads=[sbb, bufs["bias"]], writes=[sxb])
            if ncol < 256:
                S.op("dve", lambda e, hp=hp, Sx=Sx: e.tensor_copy(Sx[:, 2 * hp:2 * hp + 2, ncol], sinkb[:, 2 * hp:2 * hp + 2]),
                     reads=B("sinkb"), writes=[sxb])
            if first_mask:
                for h in heads:
                    S.op("dve", lambda e, h=h, Sx=Sx: e.tensor_tensor(Sx[:, h, 0:256], Sx[:, h, 0:256], kmask[:, :], ALU.add),
                         reads=[sxb, bufs["kmask"]], writes=[sxb])
            st, stb = nextst()
            S.op("dve", lambda e, hp=hp, st=st, Sx=Sx: e.tensor_reduce(st[:, 0:2], Sx[:, 2 * hp:2 * hp + 2, 0:ncol + 1], AX.X, ALU.max),
                 reads=[sxb], writes=[stb])
            if rowmask_col is None:
                S.op("dve", lambda e, st=st: e.tensor_scalar(st[:, 2:4], st[:, 0:2], -1.0, None, ALU.mult), reads=[stb], writes=[stb])
            else:
                S.op("dve", lambda e, st=st: e.tensor_scalar(st[:, 2:4], st[:, 0:2], -1.0, rowmask_col, ALU.mult, ALU.add),
                     reads=[stb, bufs["rowm"]], writes=[stb])
            sts.append((st, stb))
        return sts

    def attn_B(par, sts, v_ap_fn, vbufs, out_mode, out_state, filler=None, ncol=256):
        Sx = Ssb2[par]
        if out_mode == "prompt":
            at, atb = rot("attn", attn, "attn")
            atv = at[:, :].rearrange("p (g k d) -> p g k d", g=4, k=2)
        pbs = {}
        pts = {}

        def do_exp(hp):
            st, stb = sts[hp]
            sxb = bufs[f"Ssb{par}_{hp}"]
            pb, pbb = rot("P", Pb, "Pb")
            pbs[hp] = (pb, pbb)
            for i, h in enumerate((2 * hp, 2 * hp + 1)):
                if out_mode == "prompt":
                    acc_ap = st[:, i:i + 1]
                    wr = [pbb, stb]
                else:
                    c = out_state["s"] * 8 + h
                    acc_ap = out_state["smat"][:, c:c + 1]
                    wr = [pbb, out_state["smatb"]]
                S.op("act", lambda e, pb=pb, h=h, st=st, i=i, acc_ap=acc_ap: e.activation(pb[:, i, 0:ncol + 1], Sx[:, h, 0:ncol + 1], AF.Exp,
                                                                                         bias=st[:, 2 + i:3 + i], accum_out=acc_ap),
                     reads=[sxb, stb], writes=wr)

        def do_T(hp):
            pb, pbb = pbs[hp]
            tbk, tbb = nextbank()
            tv = tbk[:, :].bitcast(BF16).rearrange("p (c n) -> p c n", c=8)

            def fT(e, pb=pb, tv=tv):
                ins = None
                w1 = ncol - 128
                for i in range(2):
                    ins = e.transpose(tv[:, i * 2, :], pb[:, i, 0:128], identb[:, :])
                    ins = e.transpose(tv[0:w1, i * 2 + 1, :], pb[:, i, 128:ncol], identb[:, :])
                return ins
            S.op("pe", fT, reads=[pbb, bufs["identb"]], writes=[tbb])
            pt, ptb = rot("PT", PTs, "PTs")
            pts[hp] = (pt, ptb)
            if ncol == 256:
                S.op("act", lambda e, pt=pt, tv=tv: e.copy(pt[:, :, :], tv[:, 0:4, :]), reads=[tbb], writes=[ptb])
            else:
                w1 = ncol - 128
                tv2 = tv[:, 0:4, :].rearrange("p (i h) n -> p i h n", h=2)
                pt2 = pt[:, :, :].rearrange("p (i h) n -> p i h n", h=2)

                def fCp(e, pt2=pt2, tv2=tv2, w1=w1):
                    e.copy(pt2[:, :, 0, :], tv2[:, :, 0, :])
                    return e.copy(pt2[0:w1, :, 1, :], tv2[0:w1, :, 1, :])
                S.op("act", fCp, reads=[tbb], writes=[ptb])

        def do_PV(hp):
            st, stb = sts[hp]
            pt, ptb = pts[hp]
            kvh = (2 * hp) // 4
            g0 = (2 * hp) % 4
            if out_mode == "prompt":
                obk, obb = nextbank()

                def fO(e, pt=pt, obk=obk, kvh=kvh):
                    ins = None
                    for i in range(2):
                        for half in range(2):
                            ins = e.matmul(obk[:, i * 64:(i + 1) * 64], pt[:, i * 2 + half, :], v_ap_fn(half, kvh), start=(half == 0), stop=(half == 1))
                    return ins
                S.op("pe", fO, reads=[ptb] + vbufs, writes=[obb])
                S.op("dve", lambda e, st=st: e.reciprocal(st[:, 0:2], st[:, 0:2]), reads=[stb], writes=[stb])
                S.op("dve", lambda e, obk=obk, st=st, g0=g0, kvh=kvh: e.tensor_tensor(
                    atv[:, g0:g0 + 2, kvh, :], obk[:, 0:128].rearrange("p (a d) -> p a d", a=2),
                    st[:, 0:2].unsqueeze(2).to_broadcast([128, 2, 64]), ALU.mult),
                    reads=[obb, stb], writes=[atb])
            else:
                s_ = out_state["s"]
                ob = out_state["obank"][kvh]
                obb = out_state["obankb"][kvh]

                def fO2(e, pt=pt, ob=ob, g0=g0, kvh=kvh, s_=s_):
                    ins = None
                    w1 = ncol - 128
                    for i in range(2):
                        c0 = (g0 + i) * 64
                        ins = e.matmul(ob[:, c0:c0 + 64], pt[:, i * 2, :], v_ap_fn(0, kvh), start=False, stop=False)
                        ins = e.matmul(ob[:, c0:c0 + 64], pt[0:w1, i * 2 + 1, :], v_ap_fn(1, kvh)[0:w1, :],
                                       start=False, stop=(s_ == NS - 1 and (g0 + i) == 3 and kvh == 1))
                    return ins
                S.op("pe", fO2, reads=[ptb] + vbufs, writes=[obb])

        def step():
            if filler is not None:
                filler()

        do_exp(0)
        do_exp(1)
        yield
        do_T(0)
        do_exp(2)
        step()
        yield
        do_T(1)
        do_PV(0)
        do_exp(3)
        step()
        yield
        do_T(2)
        do_PV(1)
        step()
        yield
        do_T(3)
        do_PV(2)
        step()
        yield
        do_PV(3)
        yield
        if out_mode == "prompt":
            return at, atb
        return None, None

    def conv_diag(cc):
        dg, dgb = rot("diag", diag, "diag")
        S.op("dve", lambda e, dg=dg: e.tensor_tensor(dg[:, :, :], identb[:, :].unsqueeze(1).to_broadcast([128, 31, 128]),
                                                     cwcmb[:, cc, :].unsqueeze(2).to_broadcast([128, 31, 128]), ALU.mult),
             reads=B("identb", "cwcmb"), writes=[dgb])
        return dg, dgb

    def glu(a_ap, b_ap, P, N, abufs, bbufs, out_ap, outbufs):
        sg, sgb = sigmoid_from(b_ap, P, N, bbufs, -1.0, None)
        S.op("dve", lambda e: e.tensor_tensor(out_ap, a_ap, sg[0:P, 0:N], ALU.mult), reads=abufs + [sgb], writes=outbufs)

    def win_fm(rhs_fn, N, col0, rbufs):
        bk, bb = nextbank()

        def f(e):
            ins = None
            for kc in range(8):
                ins = e.matmul(bk[:, 0:N], win[:, kc, col0:col0 + 128], rhs_fn(kc), start=(kc == 0), stop=(kc == 7))
            return ins
        S.op("pe", f, reads=[bufs["win"]] + rbufs, writes=[bb])
        return bk, bb

    def win_tm(lhs_fn, P, col0, ncol, lbufs):
        bk, bb = nextbank()

        def f(e):
            ins = None
            for kc in range(8):
                ins = e.matmul(bk[0:P, 0:ncol], lhs_fn(kc), win[:, kc, col0:col0 + ncol], start=(kc == 0), stop=(kc == 7))
            return ins
        S.op("pe", f, reads=[bufs["win"]] + lbufs, writes=[bb])
        return bk, bb

    CQ, CK, CV, CA, CB = 0, 512, 640, 768, 1280

    def conv_tail(N, dst_fn, dstbuf):
        mbk, mbb = nextbank()

        def fM(e, mbk=mbk):
            ins = None
            for cc in range(4):
                ins = e.matmul(mbk[:, 0:N], onesf[:, :], cch[:, cc, 0:N], start=(cc == 0), stop=(cc == 3))
            return ins
        S.op("pe", fM, reads=B("onesf", "cch"), writes=[mbb])
        for cc in range(4):
            S.op("dve", lambda e, mbk=mbk, cc=cc: e.scalar_tensor_tensor(cch[:, cc, 0:N], mbk[:, 0:N], -1.0 / 512, cch[:, cc, 0:N], ALU.mult, ALU.add),
                 reads=[mbb] + B("cch"), writes=B("cch"))
        vbk, vbb = nextbank()
        for cc in range(4):
            tq, tqb = rot("tmp", tmpA, "tmpA")
            S.op("act", lambda e, tq=tq, cc=cc: e.activation(tq[:, 0:N], cch[:, cc, 0:N], AF.Square), reads=B("cch"), writes=[tqb])
            S.op("pe", lambda e, tq=tq, cc=cc, vbk=vbk: e.matmul(vbk[:, 0:N], onesf[:, :], tq[:, 0:N], start=(cc == 0), stop=(cc == 3)),
                 reads=[tqb, bufs["onesf"]], writes=[vbb])
        S.op("act", lambda e, vbk=vbk: e.activation(rstdb[:, 0:N], vbk[:, 0:N], AF.Ln, scale=1.0 / 512, bias=epsc[:, :]), reads=[vbb, bufs["epsc"]], writes=B("rstdb"))
        S.op("act", lambda e: e.activation(rstdb[:, 0:N], rstdb[:, 0:N], AF.Exp, scale=-0.5), reads=B("rstdb"), writes=B("rstdb"))
        for cc in range(4):
            S.op("dve", lambda e, cc=cc: e.tensor_tensor(cch[:, cc, 0:N], cch[:, cc, 0:N], rstdb[:, 0:N], ALU.mult), reads=B("cch", "rstdb"), writes=B("cch"))
            S.op("dve", lambda e, cc=cc: e.tensor_scalar(cch[:, cc, 0:N], cch[:, cc, 0:N], cvec[:, 4 + cc:5 + cc], cvec[:, 8 + cc:9 + cc], ALU.mult, ALU.add),
                 reads=B("cch", "cvec"), writes=B("cch"))
            sg, sgb = sigmoid_from(cch[:, cc, 0:N], 128, N, B("cch"), -1.0, None)
            S.op("dve", lambda e, cc=cc, sg=sg: e.tensor_tensor(dst_fn(cc), cch[:, cc, 0:N], sg[:, 0:N], ALU.mult), reads=B("cch") + [sgb], writes=[dstbuf])

    def wout_block(mix_fn, P, xres_ap, xresbufs, mixbufs):
        for half in range(2):
            bk_, bb_ = nextbank()

            def f(e, bk_=bk_, half=half):
                ins = None
                for kc in range(8):
                    ins = e.matmul(bk_[0:P, :], mix_fn(kc), wout[:, kc, half * 512:(half + 1) * 512], start=(kc == 0), stop=(kc == 7))
                return ins
            S.op("pe", f, reads=mixbufs + B("wout"), writes=[bb_])
            S.op("dve", lambda e, bk_=bk_, half=half: e.tensor_tensor(xres_ap[:, half * 512:(half + 1) * 512], xres_ap[:, half * 512:(half + 1) * 512],
                                                                     bk_[0:P, :], ALU.add), reads=[bb_] + xresbufs, writes=xresbufs)

    stage(2)
    xh, xhb = xr[:, 2, :], xrb[2]
    S.dma("sp", [(xh, xin[0:128, :])], writes=[xhb])
    hh, hhb = rot("hb", hb, "hb")
    rmsnorm(xh, 128, [xhb], None, hh[:, :], hhb)
    transpose_to(hh, 128, 8, lambda: hT[:, :, 0:128], hhb, bufs["hT"], "dve", gcm=g1b)
    bk, bb = win_fm(lambda kc: hT[:, kc, 0:128], 128, CK, B("hT"))
    S.op("act", lambda e, bk=bk: e.copy(kT[:, 0:128], bk[:, 0:128]), reads=[bb], writes=B("kT"))
    bk, bb = win_tm(lambda kc: hT[:, kc, 0:128], 128, CV, 128, B("hT"))
    S.op("dve", lambda e, bk=bk: e.tensor_copy(vtok[:, 0, :], bk[:, 0:128]), reads=[bb], writes=B("vtok"))
    for cc in range(4):
        ak, ab = win_fm(lambda kc: hT[:, kc, 0:128], 128, CA + cc * 128, B("hT"))
        bk2, bb2 = win_fm(lambda kc: hT[:, kc, 0:128], 128, CB + cc * 128, B("hT"))
        sg, sgb = sigmoid_from(bk2[:, 96:128], 128, 32, [bb2], -1.0, None)
        S.op("dve", lambda e, ak=ak, sg=sg, cc=cc: e.tensor_tensor(uT[:, cc, 0:32], ak[:, 96:128], sg[:, 0:32], ALU.mult),
             reads=[ab, sgb], writes=B("uT"))

    stage(3)
    xs_t = T("xs_t", [NS, D])
    S.dma("sp", [(xs_t[:], xs_d)], writes=B("xs_t"))
    hs, hsb = rot("hb", hb, "hb")
    rmsnorm(xs_t[:, :], NS, B("xs_t"), None, hs[0:NS, :], hsb)
    hsT = T("hsT", [128, 8, NS], BF16)
    transpose_to(hs, NS, 8, lambda: hsT[:, :, :], hsb, bufs["hsT"], "dve", gcm=g1b)
    qsT = T("qsT", [128, 4, 128], BF16)
    S.op("pool", lambda e: e.memset(qsT[:], 0.0), writes=B("qsT"))
    for g in range(4):
        bk, bb = win_fm(lambda kc: hsT[:, kc, :], NS, CQ + g * 128, B("hsT"))
        S.op("act", lambda e, bk=bk, g=g: e.activation(qsT[:, g, 0:NS], bk[:, 0:NS], AF.Copy, scale=0.125), reads=[bb], writes=B("qsT"))
    usT = T("usT", [128, 4, NS])
    for cc in range(4):
        ak, ab = win_fm(lambda kc: hsT[:, kc, :], NS, CA + cc * 128, B("hsT"))
        bk2, bb2 = win_fm(lambda kc: hsT[:, kc, :], NS, CB + cc * 128, B("hsT"))
        glu(ak[:, 0:NS], bk2[:, 0:NS], 128, NS, [ab], [bb2], usT[:, cc, :], B("usT"))
    kvs_f = T("kvs_f", [NS, 256]); us_f = T("us_f", [NS, 512])
    bk, bb = win_tm(lambda kc: hsT[:, kc, :], NS, CK, 256, B("hsT"))
    S.op("dve", lambda e, bk=bk: e.tensor_copy(kvs_f[:, :], bk[0:NS, 0:256]), reads=[bb], writes=B("kvs_f"))
    ak, ab = win_tm(lambda kc: hsT[:, kc, :], NS, CA, 512, B("hsT"))
    bk2, bb2 = win_tm(lambda kc: hsT[:, kc, :], NS, CB, 512, B("hsT"))
    S.op("act", lambda e, bk2=bk2: e.activation(utok[0:NS, :], bk2[0:NS, :], AF.Exp, scale=-1.0), reads=[bb2], writes=B("utok"))
    S.op("act", lambda e: e.activation(utok[0:NS, :], utok[0:NS, :], AF.Ln, bias=onec[0:NS, :]), reads=B("utok", "onec"), writes=B("utok"))
    S.op("act", lambda e: e.activation(utok[0:NS, :], utok[0:NS, :], AF.Exp, scale=-1.0), reads=B("utok"), writes=B("utok"))
    S.op("dve", lambda e, ak=ak: e.tensor_tensor(us_f[:, :], ak[0:NS, :], utok[0:NS, :], ALU.mult), reads=[ab] + B("utok"), writes=B("us_f"))
    bias_flip()
    stage(4)
    ob1 = Buf("ob1"); bufs["ob1"] = ob1
    S.dma("sp", [(nks_d[:, 0:127, :], ck_d[:, 1:128, :]), (nvs_d[:, 0:127, :], cv_d[:, 1:128, :]),
                 (ncs_d[:, 0:29, :], sc_d[:, 1:30, :])], writes=[ob1])
    S.dma("sp", [(nks_d[:, 127, :], kvs_f[:, 0:128]), (nvs_d[:, 127, :], kvs_f[:, 128:256])], reads=B("kvs_f"), sem_buf=bufs["kvs_f"])
    S.dma("sp", [(ncs_d[:, 29, :], us_f[:, :])], reads=B("us_f"), sem_buf=bufs["us_f"])

    stage(5)
    wsc = [Buf(f"wsc{i}") for i in range(32)]
    wchain = Buf("wchain"); bufs["wchain"] = wchain
    for fg in range(16):
        S.dma("pool", [(wup_s[fg * 128:(fg + 1) * 128, :], wup_d[fg * 128:(fg + 1) * 128, :])], reads=B("kT"), writes=[wsc[fg]])
        S.dma("pool", [(wdn_s[fg * 128:(fg + 1) * 128, :], wdn_d[fg * 128:(fg + 1) * 128, :])], reads=B("kT"), writes=[wsc[16 + fg]])

    stage(6)
    stA = arena[0:96, 0:5120].bitcast(F32)
    cwrep = arena[0:96, 5120:10240].bitcast(F32)
    part = arena[0:96, 10240:11264].bitcast(F32)
    for n_ in ("stA", "cwrep", "part"):
        bufs[n_] = Buf(n_)
    arena_alias = B("stA", "cwrep", "part")
    ind = T("ind", [96, NS])
    S.dma("sp", [(stA, sc_d.rearrange("s (a t) c -> (s a) (t c)", a=6)), (cwrep, cwrep_d), (ind[:], ind_d)],
          writes=B("stA", "cwrep", "ind", "biasT"), sem_buf=bufs["stA"])
    S.op("dve", lambda e: e.tensor_tensor(stA, stA, cwrep, ALU.mult), reads=B("stA", "cwrep"), writes=B("stA"))
    S.op("dve", lambda e: e.tensor_reduce(part, stA.rearrange("p (t c) -> p c t", t=5), AX.X, ALU.add), reads=B("stA"), writes=B("part"))
    for cc in range(4):
        cbk, cbb = nextbank()
        S.op("pe", lambda e, cbk=cbk, cc=cc: e.matmul(cbk[:, 0:NS], part[:, cc * 128:(cc + 1) * 128], ind[:, :], start=True, stop=True),
             reads=B("ind", "part"), writes=[cbb])
        S.op("dve", lambda e, cbk=cbk, cc=cc: e.scalar_tensor_tensor(cch[:, cc, 0:NS], usT[:, cc, :], cwcm[:, cc, 30:31], cbk[:, 0:NS], ALU.mult, ALU.add),
             reads=[cbb] + B("usT", "cwcm", "cch"), writes=B("cch"))
        S.op("dve", lambda e, cc=cc: e.tensor_scalar(cch[:, cc, 0:NS], cch[:, cc, 0:NS], cvec[:, cc:cc + 1], None, ALU.add), reads=B("cch", "cvec"), writes=B("cch"))
    mixsT = T("mixsT", [128, 8, 128], BF16)
    conv_tail(NS, lambda cc: mixsT[:, 4 + cc, 0:NS], bufs["mixsT"])

    def sample_gen():
        NFR = 6
        kvfr = [T(f"kvfr{i}", [128, 2, 256], BF16) for i in range(NFR)]
        for i in range(NFR):
            S.op("pool", lambda e, i=i: e.memset(kvfr[i][:], 0.0), writes=[bufs[f"kvfr{i}"]])
        smat = T("smat", [128, NS * 8])
        S.op("pool", lambda e: e.memset(smat[:], 0.0), writes=B("smat"))
        NKT = 4
        kTs = [T(f"kTs{i}", [128, 256], BF16) for i in range(NKT)]
        obk_, obb_ = nextbank()
        reserved.add(bank_index(obk_))
        ostate = {"smat": smat, "smatb": bufs["smat"], "obank": [obk_[:, 0:256], obk_[:, 256:512]], "obankb": [obb_, obb_], "s": 0}
        zb = T("zb", [128, 512], BF16)
        S.op("pool", lambda e: e.memset(zb[:], 0.0), writes=B("zb"))
        S.op("pe", lambda e: e.matmul(obk_[:, :], zb[:, 0:128], zb[:, :], start=True, stop=False), reads=B("zb"), writes=[obb_])

        def prep(s):
            fr = kvfr[s % NFR]; frb = bufs[f"kvfr{s % NFR}"]
            n0 = 128 - s
            prs = [(fr[s:128, 0, 0:128], ck_d[s, 0:n0, :]), (fr[s:128, 0, 128:256], cv_d[s, 0:n0, :]),
                   (fr[s:s + 1, 1, :], kvs_f[s:s + 1, :])]
            if s > 0:
                prs += [(fr[0:s, 1, 0:128], ck_d[s, n0:128, :]), (fr[0:s, 1, 128:256], cv_d[s, n0:128, :])]
            S.dma("pool", prs, reads=B("kvs_f"), writes=[frb])
            kt = kTs[s % NKT]; ktb = bufs[f"kTs{s % NKT}"]
            tbk, tbb = nextbank()
            tv = tbk[:, :].bitcast(BF16).rearrange("p (c n) -> p c n", c=8)

            def fK(e, fr=fr, tv=tv):
                ins = None
                for slot in range(2):
                    ins = e.transpose(tv[:, slot, :], fr[:, slot, 0:128], identb[:, :])
                return ins
            S.op("pe", fK, reads=[frb, bufs["identb"]], writes=[tbb])
            S.op("dve", lambda e, kt=kt, tv=tv: e.tensor_copy(kt[:, :].rearrange("p (a b) -> p a b", a=2), tv[:, 0:2, :]), reads=[tbb], writes=[ktb])

        def samp_A(s):
            kt = kTs[s % NKT]; ktb = bufs[f"kTs{s % NKT}"]
            return attn_A(s % 2, lambda g, kvh: qsT[kvh * 64:(kvh + 1) * 64, g, :],
                          lambda kvh, kt=kt: kt[kvh * 64:(kvh + 1) * 64, 0:SNC],
                          B("qsT"), [ktb], False, rowm[:, s:s + 1], ncol=SNC)

        def samp_B(s, sts):
            st_ = dict(ostate); st_["s"] = s
            fr = kvfr[s % NFR]; frb = bufs[f"kvfr{s % NFR}"]
            yield from attn_B(s % 2, sts, lambda half, kvh, fr=fr: fr[:, half, 128 + kvh * 64:128 + (kvh + 1) * 64], [frb], "sample", st_, ncol=SNC)

        for s0 in range(3):
            prep(s0)
            yield
        pend = samp_A(0)
        yield
        for s in range(NS):
            if s + 3 < NS:
                prep(s + 3)
                yield
            nxt = samp_A(s + 1) if s + 1 < NS else None
            yield
            yield from samp_B(s, pend)
            pend = nxt
        reserved.discard(bank_index(obk_))
        smt = T("smt", [128, 8])
        S.op("dve", lambda e: e.tensor_reduce(smt[:, :], smat[:, :].rearrange("p (s h) -> p h s", h=8), AX.X, ALU.add), reads=B("smat"), writes=B("smt"))
        S.op("dve", lambda e: e.tensor_scalar(smt[:, :], smt[:, :], 1e-30, None, ALU.add), reads=B("smt"), writes=B("smt"))
        S.op("dve", lambda e: e.reciprocal(smt[:, :], smt[:, :]), reads=B("smt"), writes=B("smt"))
        ats, atsb = rot("attn", attn, "attn")
        for h in range(8):
            kvh, g = h // 4, h % 4
            ob = ostate["obank"][kvh]; obb = ostate["obankb"][kvh]
            col = g * 128 + kvh * 64
            S.op("act", lambda e, ob=ob, g=g, h=h, col=col: e.activation(ats[:, col:col + 64], ob[:, g * 64:(g + 1) * 64], AF.Copy, scale=smt[:, h:h + 1]),
                 reads=[obb, bufs["smt"]], writes=[atsb])
        yield
        transpose_to(ats, 128, 4, lambda: mixsT[:, 0:4, :], atsb, bufs["mixsT"], "dve")
        yield

        wout_block(lambda kc: mixsT[:, kc, 0:NS], NS, xs_t[:, :], B("xs_t"), B("mixsT"))
        yield
        h2s, h2sb = rot("hb", hb, "hb")
        rmsnorm(xs_t[:, :], NS, B("xs_t"), None, h2s[0:NS, :], h2sb)
        transpose_to(h2s, NS, 8, lambda: actT2[(NTILE - 1) % 2][:, :, TT:TT + NS], h2sb, bufs[f"actT{(NTILE - 1) % 2}"], "dve", gcm=g2b)


    stage(9)
    first_ffn = [True]
    prev_acc = [set()]
    ysems = [Buf(f"ysem{i}") for i in range(2 * NB)]
    xloaded = {}

    def load_x(t_):
        base_ = 128 + t_ * TT
        for b_ in range(NB):
            sl = (t_ % 2) * NB + b_
            S.dma("sp", [(xr[:, sl, :], xin[base_ + b_ * 128: base_ + (b_ + 1) * 128, :])], writes=[xrb[sl]])
            xloaded[(t_, b_)] = sl
    def conv_tail_gen(N, dst_fn, dstbuf):
        mbk, mbb = nextbank()

        def fM(e, mbk=mbk):
            ins = None
            for cc in range(4):
                ins = e.matmul(mbk[:, 0:N], onesf[:, :], cch[:, cc, 0:N], start=(cc == 0), stop=(cc == 3))
            return ins
        S.op("pe", fM, reads=B("onesf", "cch"), writes=[mbb])
        for cc in range(4):
            S.op("dve", lambda e, mbk=mbk, cc=cc: e.scalar_tensor_tensor(cch[:, cc, 0:N], mbk[:, 0:N], -1.0 / 512, cch[:, cc, 0:N], ALU.mult, ALU.add),
                 reads=[mbb] + B("cch"), writes=B("cch"))
        yield
        vbk, vbb = nextbank()
        vi = bank_index(vbk)
        reserved.add(vi)
        for cc in range(4):
            tq, tqb = rot("tmp", tmpA, "tmpA")
            S.op("act", lambda e, tq=tq, cc=cc: e.activation(tq[:, 0:N], cch[:, cc, 0:N], AF.Square), reads=B("cch"), writes=[tqb])
            S.op("pe", lambda e, tq=tq, cc=cc, vbk=vbk: e.matmul(vbk[:, 0:N], onesf[:, :], tq[:, 0:N], start=(cc == 0), stop=(cc == 3)),
                 reads=[tqb, bufs["onesf"]], writes=[vbb])
            if cc == 3:
                reserved.discard(vi)
            yield
        S.op("act", lambda e, vbk=vbk: e.activation(rstdb[:, 0:N], vbk[:, 0:N], AF.Ln, scale=1.0 / 512, bias=epsc[:, :]), reads=[vbb, bufs["epsc"]], writes=B("rstdb"))
        S.op("act", lambda e: e.activation(rstdb[:, 0:N], rstdb[:, 0:N], AF.Exp, scale=-0.5), reads=B("rstdb"), writes=B("rstdb"))
        yield
        for cc in range(4):
            S.op("dve", lambda e, cc=cc: e.tensor_tensor(cch[:, cc, 0:N], cch[:, cc, 0:N], rstdb[:, 0:N], ALU.mult), reads=B("cch", "rstdb"), writes=[cchb[cc]])
            S.op("dve", lambda e, cc=cc: e.tensor_scalar(cch[:, cc, 0:N], cch[:, cc, 0:N], cvec[:, 4 + cc:5 + cc], cvec[:, 8 + cc:9 + cc], ALU.mult, ALU.add),
                 reads=[cchb[cc], bufs["cvec"]], writes=[cchb[cc]])
        yield
        sgs = []
        for cc in range(4):
            if cc == 3:
                yield
            sgs.append(sigmoid_from(cch[:, cc, 0:N], 128, N, [cchb[cc]], -1.0, None))
            if cc >= 1:
                c2 = cc - 1
                sg, sgb = sgs[c2]
                S.op("dve", lambda e, c2=c2, sg=sg: e.tensor_tensor(dst_fn(c2), cch[:, c2, 0:N], sg[:, 0:N], ALU.mult), reads=[cchb[c2], sgb], writes=[dstbuf])
        yield
        sg, sgb = sgs[3]
        S.op("dve", lambda e, sg=sg: e.tensor_tensor(dst_fn(3), cch[:, 3, 0:N], sg[:, 0:N], ALU.mult), reads=[cchb[3], sgb], writes=[dstbuf])
        S.op("dve", lambda e: e.tensor_copy(zcol[:, :], zcol[:, :]), reads=cchb, writes=B("cch", "zcol"))
        yield

    def aphase(t):
        last = (t == NTILE - 1)
        dgs = {0: conv_diag(0), 1: conv_diag(1)}
        hs_ = []
        for b in range(NB):
            sl = xloaded[(t, b)]
            h_, hbb = rot("hb", hb, "hb")
            rmsnorm(xr[:, sl, :], 128, [xrb[sl]], None, h_[:, :], hbb)
            hs_.append((h_, hbb))
        yield
        yield
        yield
        for b in range(NB):
            h_, hbb = hs_[b]
            transpose_to(h_, 128, 8, lambda b=b: hT[:, :, b * 128:(b + 1) * 128], hbb, bufs["hT"], "dve", gcm=g1b)
            yield
        yield
        for g in range(4):
            bk, bb = win_fm(lambda kc: hT[:, kc, 0:TT], TT, CQ + g * 128, B("hT"))
            S.op("act", lambda e, bk=bk, g=g: e.activation(qT[:, g, :], bk[:, 0:TT], AF.Copy, scale=0.125), reads=[bb], writes=B("qT"))
            yield
        bk, bb = win_fm(lambda kc: hT[:, kc, 0:TT], TT, CK, B("hT"))
        S.op("act", lambda e, bk=bk: e.copy(kT[:, 128:128 + TT], bk[:, 0:TT]), reads=[bb], writes=B("kT"))
        yield
        for b in range(NB):
            bk, bb = win_tm(lambda kc, b=b: hT[:, kc, b * 128:(b + 1) * 128], 128, CK, 256, B("hT"))
            S.op("dve", lambda e, bk=bk, b=b: e.tensor_copy(vtok[:, 1 + b, :], bk[:, 128:256]), reads=[bb], writes=B("vtok"))
            if last and b == NB - 1:
                S.op("dve", lambda e, bk=bk: e.tensor_copy(kvtok, bk[:, 0:256]), reads=[bb], writes=B("kvtok"))
                S.dma("sp", [(nk_d, kvtok[:, 0:128]), (nv_d, kvtok[:, 128:256])], reads=B("kvtok"), sem_buf=bufs["kvtok"])
            yield
        for cc in range(4):
            ak, ab = win_fm(lambda kc: hT[:, kc, 0:TT], TT, CA + cc * 128, B("hT"))
            bk2, bb2 = win_fm(lambda kc: hT[:, kc, 0:TT], TT, CB + cc * 128, B("hT"))
            glu(ak[:, 0:TT], bk2[:, 0:TT], 128, TT, [ab], [bb2], uT[:, cc, 32:32 + TT], B("uT"))
            yield
        if last:
            ak, ab = win_tm(lambda kc: hT[:, kc, TT - 128:TT], 128, CA, 512, B("hT"))
            bk2, bb2 = win_tm(lambda kc: hT[:, kc, TT - 128:TT], 128, CB, 512, B("hT"))
            for half in range(2):
                hs_ = slice(half * 256, (half + 1) * 256)
                sg, sgb = sigmoid_from(bk2[:, hs_], 128, 256, [bb2], -1.0, None)
                S.op("dve", lambda e, ak=ak, sg=sg, hs_=hs_: e.tensor_tensor(utok[:, hs_], ak[:, hs_], sg[:, 0:256], ALU.mult), reads=[ab, sgb], writes=B("utok"))
            S.dma("sp", [(ncv_d, utok[:, :])], reads=B("utok"), sem_buf=bufs["utok"])
            yield

        def pr_A(b, t=t):
            return attn_A(b % 2, lambda g, kvh, b=b: qT[kvh * 64:(kvh + 1) * 64, g, b * 128:(b + 1) * 128],
                          lambda kvh, b=b: kT[kvh * 64:(kvh + 1) * 64, b * 128:b * 128 + 256],
                          B("qT"), B("kT"), (t == 0 and b == 0), None)

        def conv_chunks():
            for cc in range(4):
                dg, dgb = dgs[cc]
                bk, bb = nextbank()
                bi = bank_index(bk)
                reserved.add(bi)
                for j0, j1 in ((0, 16), (16, 31)):
                    def fC(e, bk=bk, dg=dg, cc=cc, j0=j0, j1=j1):
                        ins = None
                        for j in range(j0, j1):
                            ins = e.matmul(bk[:, 0:TT], dg[:, j, :], uT[:, cc, 2 + j:2 + j + TT], start=(j == 0), stop=(j == 30))
                        return ins
                    S.op("pe", fC, reads=[dgb, bufs["uT"]], writes=[bb])
                    if j1 == 31:
                        reserved.discard(bi)
                        S.op("act", lambda e, bk=bk, cc=cc: e.activation(cch[:, cc, 0:TT], bk[:, 0:TT], AF.Identity, bias=cvec[:, cc:cc + 1]),
                             reads=[bb, bufs["cvec"]], writes=B("cch"))
                        if cc + 2 < 4:
                            dgs[cc + 2] = conv_diag(cc + 2)
                    yield

        cgen = conv_chunks()
        yield "ATTN"
        pend = pr_A(0)
        yield
        for b in range(NB):
            nxt = pr_A(b + 1) if b < NB - 1 else None
            yield
            bgen = attn_B(b % 2, pend, lambda half, kvh, b=b: vtok[:, b + half, kvh * 64:(kvh + 1) * 64], B("vtok"), "prompt", None,
                          filler=lambda: next(cgen, None))
            res = None
            while True:
                try:
                    next(bgen)
                    yield
                except StopIteration as stp:
                    res = stp.value
                    break
            at, atb = res
            transpose_to(at, 128, 4, lambda b=b: mixT[:, 0:4, b * 128:(b + 1) * 128], atb, bufs["mixT"], "dve")
            pend = nxt
            yield
        for _ in cgen:
            yield
        yield from conv_tail_gen(TT, lambda cc: mixT[:, 4 + cc, :], bufs["mixT"])
        if not last:
            S.op("dve", lambda e: e.tensor_copy(kT[:, 0:128], kT[:, TT:TT + 128]), reads=B("kT"), writes=B("kT"))
            S.op("dve", lambda e: e.tensor_copy(vtok[:, 0, :], vtok[:, NB, :]), reads=B("vtok"), writes=B("vtok"))
            S.op("dve", lambda e: e.tensor_copy(uT[:, :, 0:32], uT[:, :, TT:TT + 32]), reads=B("uT"), writes=B("uT"))
        yield
        for _ in range(5):
            yield
        aT = actT2[t % 2]; aTb = bufs[f"actT{t % 2}"]
        hs2 = []
        for b in range(NB):
            sl = xloaded[(t, b)]
            wout_block(lambda kc, b=b: mixT[:, kc, b * 128:(b + 1) * 128], 128, xr[:, sl, :], [xrb[sl]], B("mixT"))
            yield
            h_, hbb = rot("hb", hb, "hb")
            rmsnorm(xr[:, sl, :], 128, [xrb[sl]], None, h_[:, :], hbb)
            hs2.append((h_, hbb))
            yield
        yield
        yield
        for b in range(NB):
            h_, hbb = hs2[b]
            transpose_to(h_, 128, 8, lambda b=b, aT=aT: aT[:, :, b * 128:(b + 1) * 128], hbb, aTb, "dve", gcm=g2b)
            yield

    def run_all(g):
        for _ in g:
            pass

    sgen = sample_gen()
    load_x(0)
    gA0 = aphase(0)
    _SENT = object()
    pre_done = False
    while True:
        prog = False
        for _ in range(9):
            if next(sgen, _SENT) is not _SENT:
                prog = True
        if not pre_done:
            v_ = next(gA0, _SENT)
            if v_ == "ATTN" or v_ is _SENT:
                pre_done = True
        if not prog:
            break
    run_all(gA0)
    def tile_setup(t):
        base = 128 + t * TT
        last = (t == NTILE - 1)
        aT = actT2[t % 2]; aTb = bufs[f"actT{t % 2}"]
        xs_ = [xloaded[(t, b)] for b in range(NB)]
        stage(13 + 10 * t)
        old_acc = prev_acc[0]
        fresh = [i for i in range(NBANK) if i not in old_acc and i not in reserved]
        stale = [i for i in range(NBANK) if i in old_acc]
        need = 2 * NB + (2 if last else 0)
        pick = fresh[:need // 2] + stale[:need - need // 2]
        pick += [i for i in fresh + stale if i not in pick][:need - len(pick)]
        flat = [(banks[i], bank_bufs[i]) for i in pick[:need]]
        acc = [[flat[2 * b], flat[2 * b + 1]] for b in range(len(flat) // 2)]
        for row in acc:
            for a_ in row:
                reserved.add(bank_index(a_[0]))
        prev_acc[0] = set(bank_index(a_[0]) for row in acc for a_ in row)
        rest_fresh = [i for i in fresh if i not in prev_acc[0]]
        if rest_fresh:
            bank_ctr[0] = rest_fresh[0]
        accbufs = [a_[1] for row in acc for a_ in row]
        grp = {}

        def emit_U(fg, last=last, aT=aT, aTb=aTb):
            wu, wub = rot("wup", wupb, "wupb")
            wd, wdb = rot("wdn", wdnb, "wdnb")
            hd, hdb = rot("hid", hidr, "hidr")
            grp[fg] = (wd, wdb, hd, hdb)
            extra = arena_alias if first_ffn[0] else []
            S.dma("sp", [(wu, wup_s[fg * 128:(fg + 1) * 128, :].rearrange("p (k f) -> p k f", k=8))], reads=[wsc[fg]], writes=[wub] + extra)
            S.dma("sp", [(wd, wdn_s[fg * 128:(fg + 1) * 128, :].rearrange("p (c n) -> p c n", c=2))], reads=[wsc[16 + fg]], writes=[wdb] + extra)
            if not last:
                bk, bb = nextbank()

                def fU(e, bk=bk, wu=wu, aT=aT):
                    ins = None
                    for fc in range(2):
                        for kc in range(8):
                            ins = e.matmul(bk[:, fc * TT:(fc + 1) * TT], wu[:, kc, fc * 128:(fc + 1) * 128], aT[:, kc, 0:TT], start=(kc == 0), stop=(kc == 7))
                    return ins
                S.op("pe", fU, reads=[wub, aTb], writes=[bb])
                r_, rb_ = rot("rl", rl, "rl")
                S.op("act", lambda e, bk=bk, r_=r_: e.activation(r_[:, :, 0:TT], bk[:, :].rearrange("p (c n) -> p c n", c=2), AF.Relu), reads=[bb], writes=[rb_])
                S.op("dve", lambda e, r_=r_, hd=hd: e.tensor_tensor(hd[:, :, 0:TT], r_[:, :, 0:TT], r_[:, :, 0:TT], ALU.mult), reads=[rb_], writes=[hdb])
                return
            for fc in range(2):
                bk, bb = nextbank()

                def fU(e, bk=bk, wu=wu, fc=fc, aT=aT):
                    ins = None
                    for kc in range(8):
                        ins = e.matmul(bk[:, 0:TT], wu[:, kc, fc * 128:(fc + 1) * 128], aT[:, kc, 0:TT], start=(kc == 0), stop=(kc == 7))
                    for kc in range(8):
                        ins = e.matmul(bk[:, TT:TT + NS], wu[:, kc, fc * 128:(fc + 1) * 128], aT[:, kc, TT:TT + NS], start=(kc == 0), stop=(kc == 7))
                    return ins
                S.op("pe", fU, reads=[wub, aTb], writes=[bb])
                ncol = TT + NS
                r_, rb_ = rot("rl", rl, "rl")
                S.op("act", lambda e, bk=bk, r_=r_, ncol=ncol: e.activation(r_[:, 0, 0:ncol], bk[:, 0:ncol], AF.Relu), reads=[bb], writes=[rb_])
                S.op("dve", lambda e, r_=r_, hd=hd, fc=fc, ncol=ncol: e.tensor_tensor(hd[:, fc, 0:ncol], r_[:, 0, 0:ncol], r_[:, 0, 0:ncol], ALU.mult), reads=[rb_], writes=[hdb])

        def emit_D(fg, last=last, acc=acc):
            wd, wdb, hd, hdb = grp[fg]

            def fD(e, wd=wd, hd=hd, fg=fg, acc=acc):
                ins = None
                for fc in range(2):
                    f = fg * 2 + fc
                    for b in range(NB):
                        for half in range(2):
                            ins = e.matmul(acc[b][half][0][:, :], hd[:, fc, b * 128:(b + 1) * 128], wd[:, fc, half * 512:(half + 1) * 512],
                                           start=(f == 0), stop=(f == 31))
                    if last:
                        for half in range(2):
                            ins = e.matmul(acc[NB][half][0][0:NS, :], hd[:, fc, TT:TT + NS], wd[:, fc, half * 512:(half + 1) * 512],
                                           start=(f == 0), stop=(f == 31))
                return ins
            S.op("pe", fD, reads=[wdb, hdb], writes=accbufs)


        return dict(t=t, last=last, acc=acc, accbufs=accbufs, grp=grp, emit_U=emit_U, emit_D=emit_D, xs_=xs_)

    def tile_body(cx):
        t = cx["t"]; last = cx["last"]; emit_U = cx["emit_U"]; emit_D = cx["emit_D"]
        nxtA = None
        for fg in range(16):
            if fg + 1 < 16:
                emit_U(fg + 1)
            if fg == 2:
                first_ffn[0] = False
            emit_D(fg)
            if t == SAMPLE_TILE:
                if fg == 1:
                    load_x(t + 1)
                if fg < 7:
                    for _ in range(14):
                        next(sgen, None)
                elif fg == 7:
                    run_all(sgen)
                    nxtA = aphase(t + 1)
                    next(nxtA, None)
                elif fg >= 9:
                    for _ in range(13):
                        next(nxtA, None)
                continue
            if fg == 1 and not last:
                load_x(t + 1)
                nxtA = aphase(t + 1)
                next(nxtA, None)
            if nxtA is not None and fg >= 2:
                for _ in range(APF):
                    next(nxtA, None)
        return nxtA

    def tile_release(cx):
        acc = cx["acc"]
        first_ffn[0] = False
        for row in acc:
            for a_ in row:
                reserved.discard(bank_index(a_[0]))

    def tile_epilogue(cx):
        t = cx["t"]; last = cx["last"]; acc = cx["acc"]; xs_ = cx["xs_"]
        stage(14 + 10 * t)
        for b in range(NB):
            for half in range(2):
                S.op("dve", lambda e, b=b, half=half, acc=acc, sl=xs_[b]: e.tensor_tensor(xr[:, sl, half * 512:(half + 1) * 512], xr[:, sl, half * 512:(half + 1) * 512],
                                                                     acc[b][half][0][:, :], ALU.add), reads=[acc[b][half][1], xrb[xs_[b]]], writes=[xrb[xs_[b]]])
            rmsnorm(xr[:, xs_[b], :], 128, [xrb[xs_[b]]], gfb, xr[:, xs_[b], :], xrb[xs_[b]], junk=(jkt[:, :], bufs["jkt"]))
            S.dma("pool", [(y_d[t * TT + b * 128: t * TT + (b + 1) * 128, :], xr[:, xs_[b], :])], reads=[xrb[xs_[b]]], sem_buf=ysems[xs_[b]])
        if last:
            for half in range(2):
                S.op("dve", lambda e, half=half, acc=acc: e.tensor_tensor(xs_t[:, half * 512:(half + 1) * 512], xs_t[:, half * 512:(half + 1) * 512],
                                                                acc[NB][half][0][0:NS, :], ALU.add), reads=[acc[NB][half][1]] + B("xs_t"), writes=B("xs_t"))
            rmsnorm(xs_t[:, :], NS, B("xs_t"), gfb, xr[0:NS, 0, :], xrb[0])
            S.dma("sp", [(ys_d, xr[0:NS, 0, :])], reads=[xrb[0]], sem_buf=xrb[0])

    cx = tile_setup(0)
    cx["emit_U"](0)
    for t in range(NTILE):
        nxtA = tile_body(cx)
        first_ffn[0] = False
        if nxtA is not None:
            run_all(nxtA)
        cur = cx
        if t + 1 < NTILE:
            cx = tile_setup(t + 1)
            cx["emit_U"](0)
        tile_epilogue(cur)
        keep_ = set(bank_index(a_[0]) for row in cx["acc"] for a_ in row) if cx is not cur else set()
        for row in cur["acc"]:
            for a_ in row:
                if bank_index(a_[0]) not in keep_:
                    reserved.discard(bank_index(a_[0]))


def _prep_shared(meta_tokens, rel_bias, norm1_g, w_in, attn_sinks, conv_w, conv_b, conv_ln_g, conv_ln_b,
                 w_out, norm2_g, w_up, w_down, norm_f_g):
    f = np.float32
    perm = _pair_perm()
    wi = np.asarray(w_in[0], f)
    cols = np.concatenate([perm, np.arange(512, 1792)])
    wi = wi[:, cols]
    win = np.ascontiguousarray(wi.reshape(8, 128, 1792).transpose(1, 0, 2))
    wo = np.asarray(w_out[0], f)
    rows = np.concatenate([perm, np.arange(512, 1024)])
    wo = wo[rows, :]
    wout = np.ascontiguousarray(wo.reshape(8, 128, 1024).transpose(1, 0, 2))
    wu = np.asarray(w_up[0], f)
    wup = np.ascontiguousarray(wu.reshape(8, 128, 16, 256).transpose(2, 1, 0, 3)).reshape(16 * 128, 2048)
    wd = np.asarray(w_down[0], f)
    wdn = np.ascontiguousarray(wd.reshape(16, 2, 128, 1024).transpose(0, 2, 1, 3)).reshape(16 * 128, 2048)
    cw = np.asarray(conv_w[0], f)
    cwcm = np.ascontiguousarray(cw.T.reshape(4, 128, 31).transpose(1, 0, 2))
    cvec = np.ascontiguousarray(np.concatenate([np.asarray(conv_b[0], f).reshape(4, 128).T,
                                                np.asarray(conv_ln_g[0], f).reshape(4, 128).T,
                                                np.asarray(conv_ln_b[0], f).reshape(4, 128).T], axis=1))
    rbx = np.concatenate([np.asarray(rel_bias, f), np.full((1, 8), NEG, f)], axis=0)
    onehot = np.zeros((33, 384), f)
    for j in range(384):
        dist = 255 - j
        if 0 <= dist <= 128:
            onehot[int(_t5_bucket_np(np.array([dist]))[0]), j] = 1.0
        else:
            onehot[32, j] = 1.0
    jmat = np.ascontiguousarray(np.eye(128, dtype=f)[::-1])
    ident = np.eye(128, dtype=f)
    rowmask = np.full((128, NS), NEG, f)
    for s in range(NS):
        rowmask[s, s] = 0.0
    ind = np.zeros((96, NS), f)
    for s in range(NS):
        ind[s * 6:(s + 1) * 6, s] = 1.0
    cwrep = np.ascontiguousarray(np.tile(cw[:30].reshape(6, 2560), (NS, 1)))
    srow = np.ascontiguousarray(np.stack([cw[30], np.asarray(conv_b[0], f), np.asarray(conv_ln_g[0], f), np.asarray(conv_ln_b[0], f)]))
    return dict(win=win, wout=wout, wup=wup, wdn=wdn, g1cm=np.ascontiguousarray(np.asarray(norm1_g[0], f).reshape(8, 128).T), g2cm=np.ascontiguousarray(np.asarray(norm2_g[0], f).reshape(8, 128).T),
                gf=np.asarray(norm_f_g, f), cwcm=cwcm, cvec=cvec, rbx=rbx, sinks=np.asarray(attn_sinks[0], f),
                onehot=onehot, jmat=jmat, ident=ident, rowmask=rowmask, ind=ind, cwrep=cwrep)


_NC_CACHE = {}


def kernel(x_prompt, x_sample, cache_k, cache_v, state_conv, meta_tokens, rel_bias,
           norm1_g, w_in, attn_sinks, conv_w, conv_b, conv_ln_g, conv_ln_b,
           w_out, norm2_g, w_up, w_down, norm_f_g):
    f = np.float32
    shared = _prep_shared(meta_tokens, rel_bias, norm1_g, w_in, attn_sinks, conv_w, conv_b, conv_ln_g, conv_ln_b,
                          w_out, norm2_g, w_up, w_down, norm_f_g)
    xp = np.asarray(x_prompt, f)[0]
    halo0 = np.concatenate([np.zeros((112, D), f), np.asarray(meta_tokens, f)], axis=0)
    in_maps = []
    for c in range(NCORES):
        halo = halo0 if c == 0 else xp[c * 2048 - 128:c * 2048]
        m = dict(shared)
        m["xin"] = np.ascontiguousarray(np.concatenate([halo, xp[c * 2048:(c + 1) * 2048]], axis=0))
        m["xs"] = np.ascontiguousarray(np.asarray(x_sample, f)[c * NS:(c + 1) * NS, 0, :])
        m["ck"] = np.ascontiguousarray(np.asarray(cache_k, f)[0, c * NS:(c + 1) * NS].reshape(NS, 128, 128))
        m["cv"] = np.ascontiguousarray(np.asarray(cache_v, f)[0, c * NS:(c + 1) * NS].reshape(NS, 128, 128))
        m["sc"] = np.ascontiguousarray(np.asarray(state_conv, f)[0, c * NS:(c + 1) * NS])
        km = np.zeros((256,), f)
        if c == 0:
            km[:112] = NEG
        m["kmask"] = km
        in_maps.append(m)
    if "nc" not in _NC_CACHE:
        _NC_CACHE["nc"] = build_program()
    nc = _NC_CACHE["nc"]
    res = run_bass_kernel_spmd(nc, in_maps, core_ids=list(range(NCORES)))
    R = res.results
    y_prompt = np.concatenate([R[c]["y"] for c in range(NCORES)], axis=0)[None]
    y_sample = np.concatenate([R[c]["ys"] for c in range(NCORES)], axis=0)[:, None, :]
    nk = R[NCORES - 1]["nk"].reshape(1, 1, 128, 2, 64)
    nv = R[NCORES - 1]["nv"].reshape(1, 1, 128, 2, 64)
    ncv = R[NCORES - 1]["ncv"][98:128].reshape(1, 1, 30, 512)
    nks = np.concatenate([R[c]["nks"] for c in range(NCORES)], axis=0).reshape(1, 128, 128, 2, 64)
    nvs = np.concatenate([R[c]["nvs"] for c in range(NCORES)], axis=0).reshape(1, 128, 128, 2, 64)
    ncs = np.concatenate([R[c]["ncs"] for c in range(NCORES)], axis=0).reshape(1, 128, 30, 512)
    return (y_prompt.astype(f), y_sample.astype(f), nk.astype(f), nv.astype(f), ncv.astype(f),
            nks.astype(f), nvs.astype(f), ncs.astype(f))
```

```python
import math
import numpy as np
import concourse.bass as bass
import concourse.mybir as mybir
from concourse.bass_utils import run_bass_kernel_spmd

F32 = mybir.dt.float32
BF16 = mybir.dt.bfloat16
AF = mybir.ActivationFunctionType
ALU = mybir.AluOpType
AX = mybir.AxisListType

ENGS = ("pe", "act", "dve", "pool", "sp")
NEG = -1.0e30
EPS = 1e-6
NCORES = 8
NTILE = 8
TT = 256
NB = TT // 128
SAMPLE_TILE = -1
APF = 5
D = 1024
DFF = 4096
NS = 16
SNC = 128 + NS


class Buf:
    __slots__ = ("name", "last_w", "readers", "dsem", "dcount")

    def __init__(self, name):
        self.name = name
        self.last_w = None
        self.readers = []
        self.dsem = None
        self.dcount = 0


class Op:
    __slots__ = ("eng", "fn", "deps", "idx", "pos", "sig", "tick", "dma", "dval", "dbuf")

    def __init__(self, eng, fn, deps):
        self.eng = eng
        self.fn = fn
        self.deps = deps
        self.sig = False
        self.tick = 0
        self.dma = False
        self.dval = 0
        self.dbuf = None


class Sched:
    def __init__(self, nc):
        self.nc = nc
        self.ops = []
        self.streams = {e: [] for e in ENGS}
        self.dma_bufs = []
        self.prepared = False

    def _mk(self, eng, fn, reads, writes):
        deps = []
        seen = set()
        for b in reads:
            if b.last_w is not None and id(b.last_w) not in seen:
                seen.add(id(b.last_w)); deps.append(b.last_w)
            if b.name.startswith("ps"):
                for r in b.readers:
                    if r.eng != eng and id(r) not in seen:
                        seen.add(id(r)); deps.append(r)
        for b in writes:
            if b.last_w is not None and id(b.last_w) not in seen:
                seen.add(id(b.last_w)); deps.append(b.last_w)
            for r in b.readers:
                if id(r) not in seen:
                    seen.add(id(r)); deps.append(r)
        o = Op(eng, fn, deps)
        o.idx = len(self.ops)
        o.pos = len(self.streams[eng])
        self.ops.append(o)
        self.streams[eng].append(o)
        for b in reads:
            b.readers.append(o)
        for b in writes:
            b.last_w = o
            b.readers = []
        return o

    def op(self, eng, fn, reads=(), writes=()):
        return self._mk(eng, fn, list(reads), list(writes))

    def dma(self, eng, pairs, reads=(), writes=(), sem_buf=None):
        reads = list(reads); writes = list(writes)
        if sem_buf is None:
            sem_buf = writes[0]
        o = self._mk(eng, None, reads, writes)
        o.dma = True
        o.fn = pairs
        o.dbuf = sem_buf
        sem_buf.dcount += 16 * len(pairs)
        o.dval = sem_buf.dcount
        if sem_buf not in self.dma_bufs:
            self.dma_bufs.append(sem_buf)
        return o

    @staticmethod
    def _skip(o, d):
        if o.dma:
            return False
        if d.eng == o.eng:
            if d.eng == "pe":
                return True
            if (o.pos - d.pos) > 3:
                return True
        return False

    def prepare(self):
        nc = self.nc
        for o in self.ops:
            for d in o.deps:
                if not d.dma and not self._skip(o, d):
                    d.sig = True
        self.esem = {}
        for e in ENGS:
            if any(o.sig for o in self.streams[e]):
                self.esem[e] = nc.alloc_semaphore(name=f"es_{e}")
        for b in self.dma_bufs:
            b.dsem = nc.alloc_semaphore(name=f"ds_{b.name}")
        for e in ENGS:
            t = 0
            for o in self.streams[e]:
                if o.sig:
                    t += 1
                o.tick = t
        self.prepared = True

    def emit_one(self, e, h, final_eng="sp"):
        if not self.prepared:
            self.prepare()
        esem = self.esem
        waited = {}
        for o in self.streams[e]:
            need = {}
            for d in o.deps:
                if d.dma:
                    key = ("d", id(d.dbuf)); sem = d.dbuf.dsem; val = d.dval
                else:
                    if self._skip(o, d):
                        continue
                    key = ("e", d.eng); sem = esem[d.eng]; val = d.tick
                if key not in need or need[key][1] < val:
                    need[key] = (sem, val)
            for key, (sem, val) in need.items():
                if waited.get(key, 0) >= val:
                    continue
                waited[key] = val
                h.wait_ge(sem, val)
            if o.dma:
                for (out_ap, in_ap) in o.fn:
                    h.dma_start(out=out_ap, in_=in_ap).then_inc(o.dbuf.dsem, 16)
            else:
                ins = o.fn(h)
                if o.sig:
                    ins.then_inc(esem[e], 1)
        if e == final_eng:
            for b in self.dma_bufs:
                if b.dcount > 0:
                    h.wait_ge(b.dsem, b.dcount)


def _t5_bucket_np(d):
    max_exact = 16
    d = np.asarray(d)
    d_f = np.maximum(d, 1).astype(np.float32)
    large = max_exact + (np.log(d_f / np.float32(max_exact)) / np.float32(math.log(128 / max_exact))
                         * np.float32(32 - max_exact)).astype(np.int32)
    large = np.minimum(large, 31)
    return np.where(d < max_exact, d, large)


def _pair_perm():
    idx = np.zeros(512, dtype=np.int64)
    for g in range(4):
        for kvh in range(2):
            for d in range(64):
                idx[g * 128 + kvh * 64 + d] = (kvh * 4 + g) * 64 + d
    return idx


class _Stop(Exception):
    pass


def build_program():
    import os
    nc = bass.Bass("TRN2", target_bir_lowering=False)
    S = Sched(nc)
    STAGE = int(os.environ.get("KSTAGE", "99"))

    def stage(n):
        if STAGE < n:
            raise _Stop()
    try:
        _record(nc, S, stage)
    except _Stop:
        pass
    with nc.Block() as block:
        @block.tensor
        def _(e): S.emit_one("pe", e)

        @block.scalar
        def _(e): S.emit_one("act", e)

        @block.vector
        def _(e): S.emit_one("dve", e)

        @block.gpsimd
        def _(e): S.emit_one("pool", e)

        @block.sync
        def _(e): S.emit_one("sp", e)
    return nc


def _record(nc, S, stage):
    import os

    def din(name, shape, dt=F32):
        return nc.dram_tensor(name, list(shape), dt, kind="ExternalInput").ap()

    def dout(name, shape):
        return nc.dram_tensor(name, list(shape), F32, kind="ExternalOutput").ap()

    xin = din("xin", [128 + NTILE * TT, D])
    xs_d = din("xs", [NS, D])
    ck_d = din("ck", [NS, 128, 128])
    cv_d = din("cv", [NS, 128, 128])
    sc_d = din("sc", [NS, 30, 512])
    win_d = din("win", [128, 8, 1792])
    wout_d = din("wout", [128, 8, 1024])
    wup_d = din("wup", [16 * 128, 2048])
    wdn_d = din("wdn", [16 * 128, 2048])
    g1_d = din("g1cm", [128, 8]); g2_d = din("g2cm", [128, 8]); gf_d = din("gf", [D])
    cw_d = din("cwcm", [128, 4, 31])
    cvec_d = din("cvec", [128, 12])
    rbx_d = din("rbx", [33, 8])
    sinks_d = din("sinks", [8])
    oh_d = din("onehot", [33, 384])
    J_d = din("jmat", [128, 128])
    id_d = din("ident", [128, 128])
    kmask_d = din("kmask", [256])
    rowm_d = din("rowmask", [128, NS])
    ind_d = din("ind", [96, NS])
    cwrep_d = din("cwrep", [96, 2560])

    y_d = dout("y", [NTILE * TT, D])
    ys_d = dout("ys", [NS, D])
    nk_d = dout("nk", [128, 128]); nv_d = dout("nv", [128, 128])
    ncv_d = dout("ncv", [128, 512])
    nks_d = dout("nks", [NS, 128, 128]); nvs_d = dout("nvs", [NS, 128, 128])
    ncs_d = dout("ncs", [NS, 30, 512])

    wup_s = nc.dram_tensor("wup_s", [16 * 128, 2048], BF16, kind="Internal").ap()
    wdn_s = nc.dram_tensor("wdn_s", [16 * 128, 2048], BF16, kind="Internal").ap()
    biasG = nc.dram_tensor("biasG", [8, 384], F32, kind="Internal").ap()

    bufs = {}

    def T(name, shape, dt=F32):
        t = nc.alloc_sbuf_tensor("s_" + name, list(shape), dt)
        bufs[name] = Buf(name)
        return t

    def B(*names):
        return [bufs[n] for n in names]

    NBANK = 8
    banks = [nc.alloc_psum_tensor(f"ps{i}", [128, 512], F32) for i in range(NBANK)]
    bank_bufs = [Buf(f"ps{i}") for i in range(NBANK)]
    bank_ctr = [0]

    reserved = set()

    def nextbank():
        while True:
            i = bank_ctr[0] % NBANK
            bank_ctr[0] += 1
            if i not in reserved:
                return banks[i], bank_bufs[i]

    def bank_index(bk_):
        for i_, b_ in enumerate(banks):
            if b_ is bk_:
                return i_
        raise KeyError

    win = T("win", [128, 8, 1792], BF16)
    wout = T("wout", [128, 8, 1024], BF16)
    g1b = T("g1b", [128, 8]); g2b = T("g2b", [128, 8]); gfb = T("gfb", [128, D])
    cwcm = T("cwcm", [128, 4, 31]); cvec = T("cvec", [128, 12]); ncvec = T("ncvec", [128, 12])
    identf = T("identf", [128, 128]); identb = T("identb", [128, 128], BF16)
    jf = T("jf", [128, 128]); onesf = T("onesf", [128, 128])
    rowm = T("rowm", [128, NS]); zcol = T("zcol", [128, 1])
    bias = T("bias", [128, 8, 256])
    kmask = T("kmask", [128, 256])
    Ssb2 = [T(f"Ssbx{p}", [128, 8, 260]) for p in range(2)]
    for p_ in range(2):
        for hp_ in range(4):
            bufs[f"Ssb{p_}_{hp_}"] = Buf(f"Ssb{p_}_{hp_}")
    sinkb = T("sinkb", [128, 8])

    S.dma("sp", [(identf[:], id_d), (jf[:], J_d)], writes=B("identf", "jf"), sem_buf=bufs["identf"])
    S.dma("sp", [(cwcm[:], cw_d), (cvec[:], cvec_d), (rowm[:], rowm_d)], writes=B("cwcm", "cvec", "rowm"), sem_buf=bufs["cwcm"])
    S.dma("sp", [(g1b[:], g1_d), (g2b[:], g2_d),
                 (gfb[:], bass.AP(gf_d.tensor, 0, [[0, 128], [1, D]])),
                 (kmask[:], bass.AP(kmask_d.tensor, 0, [[0, 128], [1, 256]])),
                 (sinkb[:], bass.AP(sinks_d.tensor, 0, [[0, 128], [1, 8]]))],
          writes=B("g1b", "g2b", "gfb", "kmask", "sinkb"), sem_buf=bufs["g1b"])
    S.dma("pool", [(win[:, kc, :], win_d[:, kc, :]) for kc in range(8)], writes=B("win"))
    S.dma("pool", [(wout[:, kc, :], wout_d[:, kc, :]) for kc in range(8)], writes=B("wout"))
    S.op("dve", lambda e: e.tensor_copy(identb[:], identf[:]), reads=B("identf"), writes=B("identb"))
    cwcmb = T("cwcmb", [128, 4, 31], BF16)
    S.op("dve", lambda e: e.tensor_copy(cwcmb[:], cwcm[:]), reads=B("cwcm"), writes=B("cwcmb"))
    S.op("pool", lambda e: e.memset(onesf[:], 1.0), writes=B("onesf"))
    S.op("pool", lambda e: e.memset(zcol[:], 0.0), writes=B("zcol"))
    S.op("dve", lambda e: e.tensor_scalar(ncvec[:], cvec[:], -1.0, None, ALU.mult), reads=B("cvec"), writes=B("ncvec"))
    for p_ in range(2):
        S.op("dve", lambda e, p_=p_: e.tensor_copy(Ssb2[p_][:, :, 256], sinkb[:]), reads=B("sinkb"),
             writes=[bufs[f"Ssb{p_}_{hp_}"] for hp_ in range(4)])

    stage(1)
    rbx = T("rbx", [33, 8]); cch = T("cch", [128, 4, TT])
    arena = T("arena", [128, 12288], BF16)
    biasT = arena[:, 0:4096].bitcast(F32); bufs["biasT"] = Buf("biasT")
    utok = T("utok", [128, 512]); kvtok = utok[:, 0:256]; bufs["kvtok"] = bufs["utok"]
    oh = biasT[0:33, 0:384]; bufs["oh"] = bufs["biasT"]; gsb = utok[0:8, 0:384]; bufs["gsb"] = bufs["utok"]
    S.dma("sp", [(rbx[:], rbx_d), (oh, oh_d)], writes=B("rbx", "oh"), sem_buf=bufs["rbx"])
    bk, bb = nextbank()
    S.op("pe", lambda e: e.matmul(bk[0:8, 0:384], rbx[:, :], oh, start=True, stop=True), reads=B("rbx", "oh"), writes=[bb])
    S.op("dve", lambda e: e.tensor_copy(gsb, bk[0:8, 0:384]), reads=[bb], writes=B("gsb"))
    bufs["biasG"] = Buf("biasG")
    S.dma("sp", [(biasG, gsb)], reads=B("gsb"), writes=B("biasG"))
    S.dma("sp", [(biasT[:, h * 256:(h + 1) * 256], bass.AP(biasG.tensor, h * 384, [[1, 128], [1, 256]])) for h in range(8)],
          reads=B("biasG"), writes=B("biasT"))
    def bias_flip():
        for hp in range(4):
            bk, bb = nextbank()
            S.op("pe", lambda e, bk=bk, hp=hp: e.matmul(bk[:, :], jf[:, :], biasT[:, hp * 512:(hp + 1) * 512], start=True, stop=True),
                 reads=B("jf", "biasT"), writes=[bb])
            S.op("dve", lambda e, bk=bk, hp=hp: e.tensor_copy(bias[:, 2 * hp:2 * hp + 2, :], bk[:, :].rearrange("p (a b) -> p a b", a=2)),
                 reads=[bb], writes=B("bias"))

    xr = T("xr", [128, 2 * NB, D])
    xrb = [Buf(f"xr_{i}") for i in range(2 * NB)]
    hb = [T(f"hb{i}", [128, D], BF16) for i in range(2)]
    actT2 = [T(f"actT{i}", [128, 8, TT + 16], BF16) for i in range(2)]
    hT = T("hT", [128, 8, TT], BF16)
    qT = T("qT", [128, 4, TT], BF16)
    kT = T("kT", [128, 128 + TT], BF16)
    vtok = T("vtok", [128, NB + 1, 128], BF16)
    uT = T("uT", [128, 4, 32 + TT], BF16)
    Pb = [T(f"Pb{i}", [128, 2, 260], BF16) for i in range(2)]
    PTs = [T(f"PTs{i}", [128, 4, 128], BF16) for i in range(2)]
    attn = [T(f"attn{i}", [128, 512], BF16) for i in range(2)]
    mixT = T("mixT", [128, 8, TT], BF16)
    diag = [T(f"diag{i}", [128, 31, 128], BF16) for i in range(2)]
    tmpA = [T(f"tmpA{i}", [128, TT]) for i in range(3)]
    cchb = [Buf(f"cch_{i}") for i in range(4)]
    rstdb = T("rstdb", [128, TT])
    wupb = [arena[:, i * 2048:(i + 1) * 2048].rearrange("p (k f) -> p k f", k=8) for i in range(3)]
    wdnb = [arena[:, 6144 + i * 2048:6144 + (i + 1) * 2048].rearrange("p (c n) -> p c n", c=2) for i in range(3)]
    for i in range(3):
        bufs[f"wupb{i}"] = Buf(f"wupb{i}")
        bufs[f"wdnb{i}"] = Buf(f"wdnb{i}")
    hidr = [T(f"hidr{i}", [128, 2, TT + 16], BF16) for i in range(3)]
    rl = [T(f"rl{i}", [128, 2, TT + 16]) for i in range(2)]
    jkt = T("jkt", [128, D], BF16)
    NST = 24
    stt = [T(f"st{i}", [128, 4]) for i in range(NST)]
    st_ctr = [0]

    def nextst():
        i = st_ctr[0] % NST
        st_ctr[0] += 1
        return stt[i], bufs[f"st{i}"]

    rr = {"hb": 0, "P": 0, "PT": 0, "attn": 0, "tmp": 0, "rl": 0, "wup": 0, "wdn": 0, "diag": 0, "hid": 0}

    def rot(key, lst, prefix):
        i = rr[key] % len(lst)
        rr[key] += 1
        return lst[i], bufs[f"{prefix}{i}"]

    def rmsnorm(xap, P, xbufs, gb, outap, outbuf, junk=None):
        st, stb = nextst()
        jap, jbuf = (outap, outbuf) if junk is None else junk
        S.op("act", lambda e: e.activation(jap, xap, AF.Square, accum_out=st[0:P, 0:1]), reads=xbufs, writes=[jbuf, stb])
        S.op("act", lambda e: e.activation(st[0:P, 1:2], st[0:P, 0:1], AF.Ln, scale=1.0 / D, bias=epsc[0:P, :]), reads=[stb, bufs["epsc"]], writes=[stb])
        S.op("act", lambda e: e.activation(st[0:P, 2:3], st[0:P, 1:2], AF.Exp, scale=-0.5), reads=[stb], writes=[stb])
        if gb is None:
            S.op("dve", lambda e: e.tensor_scalar(outap, xap, st[0:P, 2:3], None, ALU.mult), reads=xbufs + [stb], writes=[outbuf])
        else:
            S.op("dve", lambda e: e.scalar_tensor_tensor(outap, xap, st[0:P, 2:3], gb[0:P, :], ALU.mult, ALU.mult),
                 reads=xbufs + [stb, gbuf[id(gb)]], writes=[outbuf])

    epsc = T("epsc", [128, 1])
    onec = T("onec", [128, 1])
    S.op("pool", lambda e: e.memset(epsc[:], EPS), writes=B("epsc"))
    S.op("pool", lambda e: e.memset(onec[:], 1.0), writes=B("onec"))
    gbuf = {id(g1b): bufs["g1b"], id(g2b): bufs["g2b"], id(gfb): bufs["gfb"]}

    def transpose_to(src_bf, P, nchunk, dst_fn, srcbuf, dstbuf, evac_eng, gcm=None):
        bk, bb = nextbank()
        pv = bk[:, :].bitcast(BF16).rearrange("p (c n) -> p c n", c=8)

        def f(e):
            ins = None
            for c in range(nchunk):
                ins = e.transpose(pv[:, c, 0:P], src_bf[0:P, c * 128:(c + 1) * 128], identb[0:P, 0:P])
            return ins
        S.op("pe", f, reads=[srcbuf, bufs["identb"]], writes=[bb])
        if gcm is not None:
            S.op("dve", lambda e: e.tensor_tensor(dst_fn(), pv[:, 0:nchunk, 0:P], gcm[:, 0:nchunk].unsqueeze(2).to_broadcast([128, nchunk, P]), ALU.mult),
                 reads=[bb, gbuf[id(gcm)]], writes=[dstbuf])
        elif evac_eng == "act":
            S.op("act", lambda e: e.copy(dst_fn(), pv[:, 0:nchunk, 0:P]), reads=[bb], writes=[dstbuf])
        else:
            S.op("dve", lambda e: e.tensor_copy(dst_fn(), pv[:, 0:nchunk, 0:P]), reads=[bb], writes=[dstbuf])

    def sigmoid_from(src_ap, P, N, srcbufs, scale_neg, bias_neg):
        t, tb = rot("tmp", tmpA, "tmpA")
        kw = {}
        if bias_neg is not None:
            kw["bias"] = bias_neg
        S.op("act", lambda e: e.activation(t[0:P, 0:N], src_ap, AF.Exp, scale=scale_neg, **kw), reads=srcbufs, writes=[tb])
        S.op("act", lambda e: e.activation(t[0:P, 0:N], t[0:P, 0:N], AF.Ln, bias=onec[0:P, :]), reads=[tb, bufs["onec"]], writes=[tb])
        S.op("act", lambda e: e.activation(t[0:P, 0:N], t[0:P, 0:N], AF.Exp, scale=-1.0), reads=[tb], writes=[tb])
        return t, tb

    def attn_A(par, q_ap_fn, kT_ap_fn, qbufs, kbufs, first_mask, rowmask_col, ncol=256):
        Sx = Ssb2[par]
        sts = []
        for hp in range(4):
            sbk, sbb = nextbank()
            heads = [2 * hp, 2 * hp + 1]
            sxb = bufs[f"Ssb{par}_{hp}"]

            def fS(e, sbk=sbk, heads=heads):
                ins = None
                for i, h in enumerate(heads):
                    kvh, g = h // 4, h % 4
                    ins = e.matmul(sbk[:, i * 256:i * 256 + ncol], q_ap_fn(g, kvh), kT_ap_fn(kvh), start=True, stop=True)
                return ins
            S.op("pe", fS, reads=qbufs + kbufs, writes=[sbb])
            S.op("dve", lambda e, sbk=sbk, hp=hp, Sx=Sx: e.tensor_tensor(Sx[:, 2 * hp:2 * hp + 2, 0:ncol], sbk[:, :].rearrange("p (a b) -> p a b", a=2)[:, :, 0:ncol],
                                                                         bias[:, 2 * hp:2 * hp + 2, 0:ncol], ALU.add),
                 reads=[sbb, bufs["bias"]], writes=[sxb])
            if ncol < 256:
                S.op("dve", lambda e, hp=hp, Sx=Sx: e.tensor_copy(Sx[:, 2 * hp:2 * hp + 2, ncol], sinkb[:, 2 * hp:2 * hp + 2]),
                     reads=B("sinkb"), writes=[sxb])
            if first_mask:
                for h in heads:
                    S.op("dve", lambda e, h=h, Sx=Sx: e.tensor_tensor(Sx[:, h, 0:256], Sx[:, h, 0:256], kmask[:, :], ALU.add),
                         reads=[sxb, bufs["kmask"]], writes=[sxb])
            st, stb = nextst()
            S.op("dve", lambda e, hp=hp, st=st, Sx=Sx: e.tensor_reduce(st[:, 0:2], Sx[:, 2 * hp:2 * hp + 2, 0:ncol + 1], AX.X, ALU.max),
                 reads=[sxb], writes=[stb])
            if rowmask_col is None:
                S.op("dve", lambda e, st=st: e.tensor_scalar(st[:, 2:4], st[:, 0:2], -1.0, None, ALU.mult), reads=[stb], writes=[stb])
            else:
                S.op("dve", lambda e, st=st: e.tensor_scalar(st[:, 2:4], st[:, 0:2], -1.0, rowmask_col, ALU.mult, ALU.add),
                     reads=[stb, bufs["rowm"]], writes=[stb])
            sts.append((st, stb))
        return sts

    def attn_B(par, sts, v_ap_fn, vbufs, out_mode, out_state, filler=None, ncol=256):
        Sx = Ssb2[par]
        if out_mode == "prompt":
            at, atb = rot("attn", attn, "attn")
            atv = at[:, :].rearrange("p (g k d) -> p g k d", g=4, k=2)
        pbs = {}
        pts = {}

        def do_exp(hp):
            st, stb = sts[hp]
            sxb = bufs[f"Ssb{par}_{hp}"]
            pb, pbb = rot("P", Pb, "Pb")
            pbs[hp] = (pb, pbb)
            for i, h in enumerate((2 * hp, 2 * hp + 1)):
                if out_mode == "prompt":
                    acc_ap = st[:, i:i + 1]
                    wr = [pbb, stb]
                else:
                    c = out_state["s"] * 8 + h
                    acc_ap = out_state["smat"][:, c:c + 1]
                    wr = [pbb, out_state["smatb"]]
                S.op("act", lambda e, pb=pb, h=h, st=st, i=i, acc_ap=acc_ap: e.activation(pb[:, i, 0:ncol + 1], Sx[:, h, 0:ncol + 1], AF.Exp,
                                                                                         bias=st[:, 2 + i:3 + i], accum_out=acc_ap),
                     reads=[sxb, stb], writes=wr)

        def do_T(hp):
            pb, pbb = pbs[hp]
            tbk, tbb = nextbank()
            tv = tbk[:, :].bitcast(BF16).rearrange("p (c n) -> p c n", c=8)

            def fT(e, pb=pb, tv=tv):
                ins = None
                w1 = ncol - 128
                for i in range(2):
                    ins = e.transpose(tv[:, i * 2, :], pb[:, i, 0:128], identb[:, :])
                    ins = e.transpose(tv[0:w1, i * 2 + 1, :], pb[:, i, 128:ncol], identb[:, :])
                return ins
            S.op("pe", fT, reads=[pbb, bufs["identb"]], writes=[tbb])
            pt, ptb = rot("PT", PTs, "PTs")
            pts[hp] = (pt, ptb)
            if ncol == 256:
                S.op("act", lambda e, pt=pt, tv=tv: e.copy(pt[:, :, :], tv[:, 0:4, :]), reads=[tbb], writes=[ptb])
            else:
                w1 = ncol - 128
                tv2 = tv[:, 0:4, :].rearrange("p (i h) n -> p i h n", h=2)
                pt2 = pt[:, :, :].rearrange("p (i h) n -> p i h n", h=2)

                def fCp(e, pt2=pt2, tv2=tv2, w1=w1):
                    e.copy(pt2[:, :, 0, :], tv2[:, :, 0, :])
                    return e.copy(pt2[0:w1, :, 1, :], tv2[0:w1, :, 1, :])
                S.op("act", fCp, reads=[tbb], writes=[ptb])

        def do_PV(hp):
            st, stb = sts[hp]
            pt, ptb = pts[hp]
            kvh = (2 * hp) // 4
            g0 = (2 * hp) % 4
            if out_mode == "prompt":
                obk, obb = nextbank()

                def fO(e, pt=pt, obk=obk, kvh=kvh):
                    ins = None
                    for i in range(2):
                        for half in range(2):
                            ins = e.matmul(obk[:, i * 64:(i + 1) * 64], pt[:, i * 2 + half, :], v_ap_fn(half, kvh), start=(half == 0), stop=(half == 1))
                    return ins
                S.op("pe", fO, reads=[ptb] + vbufs, writes=[obb])
                S.op("dve", lambda e, st=st: e.reciprocal(st[:, 0:2], st[:, 0:2]), reads=[stb], writes=[stb])
                S.op("dve", lambda e, obk=obk, st=st, g0=g0, kvh=kvh: e.tensor_tensor(
                    atv[:, g0:g0 + 2, kvh, :], obk[:, 0:128].rearrange("p (a d) -> p a d", a=2),
                    st[:, 0:2].unsqueeze(2).to_broadcast([128, 2, 64]), ALU.mult),
                    reads=[obb, stb], writes=[atb])
            else:
                s_ = out_state["s"]
                ob = out_state["obank"][kvh]
                obb = out_state["obankb"][kvh]

                def fO2(e, pt=pt, ob=ob, g0=g0, kvh=kvh, s_=s_):
                    ins = None
                    w1 = ncol - 128
                    for i in range(2):
                        c0 = (g0 + i) * 64
                        ins = e.matmul(ob[:, c0:c0 + 64], pt[:, i * 2, :], v_ap_fn(0, kvh), start=False, stop=False)
                        ins = e.matmul(ob[:, c0:c0 + 64], pt[0:w1, i * 2 + 1, :], v_ap_fn(1, kvh)[0:w1, :],
                                       start=False, stop=(s_ == NS - 1 and (g0 + i) == 3 and kvh == 1))
                    return ins
                S.op("pe", fO2, reads=[ptb] + vbufs, writes=[obb])

        def step():
            if filler is not None:
                filler()

        do_exp(0)
        do_exp(1)
        yield
        do_T(0)
        do_exp(2)
        step()
        yield
        do_T(1)
        do_PV(0)
        do_exp(3)
        step()
        yield
        do_T(2)
        do_PV(1)
        step()
        yield
        do_T(3)
        do_PV(2)
        step()
        yield
        do_PV(3)
        yield
        if out_mode == "prompt":
            return at, atb
        return None, None

    def conv_diag(cc):
        dg, dgb = rot("diag", diag, "diag")
        S.op("dve", lambda e, dg=dg: e.tensor_tensor(dg[:, :, :], identb[:, :].unsqueeze(1).to_broadcast([128, 31, 128]),
                                                     cwcmb[:, cc, :].unsqueeze(2).to_broadcast([128, 31, 128]), ALU.mult),
             reads=B("identb", "cwcmb"), writes=[dgb])
        return dg, dgb

    def glu(a_ap, b_ap, P, N, abufs, bbufs, out_ap, outbufs):
        sg, sgb = sigmoid_from(b_ap, P, N, bbufs, -1.0, None)
        S.op("dve", lambda e: e.tensor_tensor(out_ap, a_ap, sg[0:P, 0:N], ALU.mult), reads=abufs + [sgb], writes=outbufs)

    def win_fm(rhs_fn, N, col0, rbufs):
        bk, bb = nextbank()

        def f(e):
            ins = None
            for kc in range(8):
                ins = e.matmul(bk[:, 0:N], win[:, kc, col0:col0 + 128], rhs_fn(kc), start=(kc == 0), stop=(kc == 7))
            return ins
        S.op("pe", f, reads=[bufs["win"]] + rbufs, writes=[bb])
        return bk, bb

    def win_tm(lhs_fn, P, col0, ncol, lbufs):
        bk, bb = nextbank()

        def f(e):
            ins = None
            for kc in range(8):
                ins = e.matmul(bk[0:P, 0:ncol], lhs_fn(kc), win[:, kc, col0:col0 + ncol], start=(kc == 0), stop=(kc == 7))
            return ins
        S.op("pe", f, reads=[bufs["win"]] + lbufs, writes=[bb])
        return bk, bb

    CQ, CK, CV, CA, CB = 0, 512, 640, 768, 1280

    def conv_tail(N, dst_fn, dstbuf):
        mbk, mbb = nextbank()

        def fM(e, mbk=mbk):
            ins = None
            for cc in range(4):
                ins = e.matmul(mbk[:, 0:N], onesf[:, :], cch[:, cc, 0:N], start=(cc == 0), stop=(cc == 3))
            return ins
        S.op("pe", fM, reads=B("onesf", "cch"), writes=[mbb])
        for cc in range(4):
            S.op("dve", lambda e, mbk=mbk, cc=cc: e.scalar_tensor_tensor(cch[:, cc, 0:N], mbk[:, 0:N], -1.0 / 512, cch[:, cc, 0:N], ALU.mult, ALU.add),
                 reads=[mbb] + B("cch"), writes=B("cch"))
        vbk, vbb = nextbank()
        for cc in range(4):
            tq, tqb = rot("tmp", tmpA, "tmpA")
            S.op("act", lambda e, tq=tq, cc=cc: e.activation(tq[:, 0:N], cch[:, cc, 0:N], AF.Square), reads=B("cch"), writes=[tqb])
            S.op("pe", lambda e, tq=tq, cc=cc, vbk=vbk: e.matmul(vbk[:, 0:N], onesf[:, :], tq[:, 0:N], start=(cc == 0), stop=(cc == 3)),
                 reads=[tqb, bufs["onesf"]], writes=[vbb])
        S.op("act", lambda e, vbk=vbk: e.activation(rstdb[:, 0:N], vbk[:, 0:N], AF.Ln, scale=1.0 / 512, bias=epsc[:, :]), reads=[vbb, bufs["epsc"]], writes=B("rstdb"))
        S.op("act", lambda e: e.activation(rstdb[:, 0:N], rstdb[:, 0:N], AF.Exp, scale=-0.5), reads=B("rstdb"), writes=B("rstdb"))
        for cc in range(4):
            S.op("dve", lambda e, cc=cc: e.tensor_tensor(cch[:, cc, 0:N], cch[:, cc, 0:N], rstdb[:, 0:N], ALU.mult), reads=B("cch", "rstdb"), writes=B("cch"))
            S.op("dve", lambda e, cc=cc: e.tensor_scalar(cch[:, cc, 0:N], cch[:, cc, 0:N], cvec[:, 4 + cc:5 + cc], cvec[:, 8 + cc:9 + cc], ALU.mult, ALU.add),
                 reads=B("cch", "cvec"), writes=B("cch"))
            sg, sgb = sigmoid_from(cch[:, cc, 0:N], 128, N, B("cch"), -1.0, None)
            S.op("dve", lambda e, cc=cc, sg=sg: e.tensor_tensor(dst_fn(cc), cch[:, cc, 0:N], sg[:, 0:N], ALU.mult), reads=B("cch") + [sgb], writes=[dstbuf])

    def wout_block(mix_fn, P, xres_ap, xresbufs, mixbufs):
        for half in range(2):
            bk_, bb_ = nextbank()

            def f(e, bk_=bk_, half=half):
                ins = None
                for kc in range(8):
                    ins = e.matmul(bk_[0:P, :], mix_fn(kc), wout[:, kc, half * 512:(half + 1) * 512], start=(kc == 0), stop=(kc == 7))
                return ins
            S.op("pe", f, reads=mixbufs + B("wout"), writes=[bb_])
            S.op("dve", lambda e, bk_=bk_, half=half: e.tensor_tensor(xres_ap[:, half * 512:(half + 1) * 512], xres_ap[:, half * 512:(half + 1) * 512],
                                                                     bk_[0:P, :], ALU.add), reads=[bb_] + xresbufs, writes=xresbufs)

    stage(2)
    xh, xhb = xr[:, 2, :], xrb[2]
    S.dma("sp", [(xh, xin[0:128, :])], writes=[xhb])
    hh, hhb = rot("hb", hb, "hb")
    rmsnorm(xh, 128, [xhb], None, hh[:, :], hhb)
    transpose_to(hh, 128, 8, lambda: hT[:, :, 0:128], hhb, bufs["hT"], "dve", gcm=g1b)
    bk, bb = win_fm(lambda kc: hT[:, kc, 0:128], 128, CK, B("hT"))
    S.op("act", lambda e, bk=bk: e.copy(kT[:, 0:128], bk[:, 0:128]), reads=[bb], writes=B("kT"))
    bk, bb = win_tm(lambda kc: hT[:, kc, 0:128], 128, CV, 128, B("hT"))
    S.op("dve", lambda e, bk=bk: e.tensor_copy(vtok[:, 0, :], bk[:, 0:128]), reads=[bb], writes=B("vtok"))
    for cc in range(4):
        ak, ab = win_fm(lambda kc: hT[:, kc, 0:128], 128, CA + cc * 128, B("hT"))
        bk2, bb2 = win_fm(lambda kc: hT[:, kc, 0:128], 128, CB + cc * 128, B("hT"))
        sg, sgb = sigmoid_from(bk2[:, 96:128], 128, 32, [bb2], -1.0, None)
        S.op("dve", lambda e, ak=ak, sg=sg, cc=cc: e.tensor_tensor(uT[:, cc, 0:32], ak[:, 96:128], sg[:, 0:32], ALU.mult),
             reads=[ab, sgb], writes=B("uT"))

    stage(3)
    xs_t = T("xs_t", [NS, D])
    S.dma("sp", [(xs_t[:], xs_d)], writes=B("xs_t"))
    hs, hsb = rot("hb", hb, "hb")
    rmsnorm(xs_t[:, :], NS, B("xs_t"), None, hs[0:NS, :], hsb)
    hsT = T("hsT", [128, 8, NS], BF16)
    transpose_to(hs, NS, 8, lambda: hsT[:, :, :], hsb, bufs["hsT"], "dve", gcm=g1b)
    qsT = T("qsT", [128, 4, 128], BF16)
    S.op("pool", lambda e: e.memset(qsT[:], 0.0), writes=B("qsT"))
    for g in range(4):
        bk, bb = win_fm(lambda kc: hsT[:, kc, :], NS, CQ + g * 128, B("hsT"))
        S.op("act", lambda e, bk=bk, g=g: e.activation(qsT[:, g, 0:NS], bk[:, 0:NS], AF.Copy, scale=0.125), reads=[bb], writes=B("qsT"))
    usT = T("usT", [128, 4, NS])
    for cc in range(4):
        ak, ab = win_fm(lambda kc: hsT[:, kc, :], NS, CA + cc * 128, B("hsT"))
        bk2, bb2 = win_fm(lambda kc: hsT[:, kc, :], NS, CB + cc * 128, B("hsT"))
        glu(ak[:, 0:NS], bk2[:, 0:NS], 128, NS, [ab], [bb2], usT[:, cc, :], B("usT"))
    kvs_f = T("kvs_f", [NS, 256]); us_f = T("us_f", [NS, 512])
    bk, bb = win_tm(lambda kc: hsT[:, kc, :], NS, CK, 256, B("hsT"))
    S.op("dve", lambda e, bk=bk: e.tensor_copy(kvs_f[:, :], bk[0:NS, 0:256]), reads=[bb], writes=B("kvs_f"))
    ak, ab = win_tm(lambda kc: hsT[:, kc, :], NS, CA, 512, B("hsT"))
    bk2, bb2 = win_tm(lambda kc: hsT[:, kc, :], NS, CB, 512, B("hsT"))
    S.op("act", lambda e, bk2=bk2: e.activation(utok[0:NS, :], bk2[0:NS, :], AF.Exp, scale=-1.0), reads=[bb2], writes=B("utok"))
    S.op("act", lambda e: e.activation(utok[0:NS, :], utok[0:NS, :], AF.Ln, bias=onec[0:NS, :]), reads=B("utok", "onec"), writes=B("utok"))
    S.op("act", lambda e: e.activation(utok[0:NS, :], utok[0:NS, :], AF.Exp, scale=-1.0), reads=B("utok"), writes=B("utok"))
    S.op("dve", lambda e, ak=ak: e.tensor_tensor(us_f[:, :], ak[0:NS, :], utok[0:NS, :], ALU.mult), reads=[ab] + B("utok"), writes=B("us_f"))
    bias_flip()
    stage(4)
    ob1 = Buf("ob1"); bufs["ob1"] = ob1
    S.dma("sp", [(nks_d[:, 0:127, :], ck_d[:, 1:128, :]), (nvs_d[:, 0:127, :], cv_d[:, 1:128, :]),
                 (ncs_d[:, 0:29, :], sc_d[:, 1:30, :])], writes=[ob1])
    S.dma("sp", [(nks_d[:, 127, :], kvs_f[:, 0:128]), (nvs_d[:, 127, :], kvs_f[:, 128:256])], reads=B("kvs_f"), sem_buf=bufs["kvs_f"])
    S.dma("sp", [(ncs_d[:, 29, :], us_f[:, :])], reads=B("us_f"), sem_buf=bufs["us_f"])

    stage(5)
    wsc = [Buf(f"wsc{i}") for i in range(32)]
    wchain = Buf("wchain"); bufs["wchain"] = wchain
    for fg in range(16):
        S.dma("pool", [(wup_s[fg * 128:(fg + 1) * 128, :], wup_d[fg * 128:(fg + 1) * 128, :])], reads=B("kT"), writes=[wsc[fg]])
        S.dma("pool", [(wdn_s[fg * 128:(fg + 1) * 128, :], wdn_d[fg * 128:(fg + 1) * 128, :])], reads=B("kT"), writes=[wsc[16 + fg]])

    stage(6)
    stA = arena[0:96, 0:5120].bitcast(F32)
    cwrep = arena[0:96, 5120:10240].bitcast(F32)
    part = arena[0:96, 10240:11264].bitcast(F32)
    for n_ in ("stA", "cwrep", "part"):
        bufs[n_] = Buf(n_)
    arena_alias = B("stA", "cwrep", "part")
    ind = T("ind", [96, NS])
    S.dma("sp", [(stA, sc_d.rearrange("s (a t) c -> (s a) (t c)", a=6)), (cwrep, cwrep_d), (ind[:], ind_d)],
          writes=B("stA", "cwrep", "ind", "biasT"), sem_buf=bufs["stA"])
    S.op("dve", lambda e: e.tensor_tensor(stA, stA, cwrep, ALU.mult), reads=B("stA", "cwrep"), writes=B("stA"))
    S.op("dve", lambda e: e.tensor_reduce(part, stA.rearrange("p (t c) -> p c t", t=5), AX.X, ALU.add), reads=B("stA"), writes=B("part"))
    for cc in range(4):
        cbk, cbb = nextbank()
        S.op("pe", lambda e, cbk=cbk, cc=cc: e.matmul(cbk[:, 0:NS], part[:, cc * 128:(cc + 1) * 128], ind[:, :], start=True, stop=True),
             reads=B("ind", "part"), writes=[cbb])
        S.op("dve", lambda e, cbk=cbk, cc=cc: e.scalar_tensor_tensor(cch[:, cc, 0:NS], usT[:, cc, :], cwcm[:, cc, 30:31], cbk[:, 0:NS], ALU.mult, ALU.add),
             reads=[cbb] + B("usT", "cwcm", "cch"), writes=B("cch"))
        S.op("dve", lambda e, cc=cc: e.tensor_scalar(cch[:, cc, 0:NS], cch[:, cc, 0:NS], cvec[:, cc:cc + 1], None, ALU.add), reads=B("cch", "cvec"), writes=B("cch"))
    mixsT = T("mixsT", [128, 8, 128], BF16)
    conv_tail(NS, lambda cc: mixsT[:, 4 + cc, 0:NS], bufs["mixsT"])

    def sample_gen():
        NFR = 6
        kvfr = [T(f"kvfr{i}", [128, 2, 256], BF16) for i in range(NFR)]
        for i in range(NFR):
            S.op("pool", lambda e, i=i: e.memset(kvfr[i][:], 0.0), writes=[bufs[f"kvfr{i}"]])
        smat = T("smat", [128, NS * 8])
        S.op("pool", lambda e: e.memset(smat[:], 0.0), writes=B("smat"))
        NKT = 4
        kTs = [T(f"kTs{i}", [128, 256], BF16) for i in range(NKT)]
        obk_, obb_ = nextbank()
        reserved.add(bank_index(obk_))
        ostate = {"smat": smat, "smatb": bufs["smat"], "obank": [obk_[:, 0:256], obk_[:, 256:512]], "obankb": [obb_, obb_], "s": 0}
        zb = T("zb", [128, 512], BF16)
        S.op("pool", lambda e: e.memset(zb[:], 0.0), writes=B("zb"))
        S.op("pe", lambda e: e.matmul(obk_[:, :], zb[:, 0:128], zb[:, :], start=True, stop=False), reads=B("zb"), writes=[obb_])

        def prep(s):
            fr = kvfr[s % NFR]; frb = bufs[f"kvfr{s % NFR}"]
            n0 = 128 - s
            prs = [(fr[s:128, 0, 0:128], ck_d[s, 0:n0, :]), (fr[s:128, 0, 128:256], cv_d[s, 0:n0, :]),
                   (fr[s:s + 1, 1, :], kvs_f[s:s + 1, :])]
            if s > 0:
                prs += [(fr[0:s, 1, 0:128], ck_d[s, n0:128, :]), (fr[0:s, 1, 128:256], cv_d[s, n0:128, :])]
            S.dma("pool", prs, reads=B("kvs_f"), writes=[frb])
            kt = kTs[s % NKT]; ktb = bufs[f"kTs{s % NKT}"]
            tbk, tbb = nextbank()
            tv = tbk[:, :].bitcast(BF16).rearrange("p (c n) -> p c n", c=8)

            def fK(e, fr=fr, tv=tv):
                ins = None
                for slot in range(2):
                    ins = e.transpose(tv[:, slot, :], fr[:, slot, 0:128], identb[:, :])
                return ins
            S.op("pe", fK, reads=[frb, bufs["identb"]], writes=[tbb])
            S.op("dve", lambda e, kt=kt, tv=tv: e.tensor_copy(kt[:, :].rearrange("p (a b) -> p a b", a=2), tv[:, 0:2, :]), reads=[tbb], writes=[ktb])

        def samp_A(s):
            kt = kTs[s % NKT]; ktb = bufs[f"kTs{s % NKT}"]
            return attn_A(s % 2, lambda g, kvh: qsT[kvh * 64:(kvh + 1) * 64, g, :],
                          lambda kvh, kt=kt: kt[kvh * 64:(kvh + 1) * 64, 0:SNC],
                          B("qsT"), [ktb], False, rowm[:, s:s + 1], ncol=SNC)

        def samp_B(s, sts):
            st_ = dict(ostate); st_["s"] = s
            fr = kvfr[s % NFR]; frb = bufs[f"kvfr{s % NFR}"]
            yield from attn_B(s % 2, sts, lambda half, kvh, fr=fr: fr[:, half, 128 + kvh * 64:128 + (kvh + 1) * 64], [frb], "sample", st_, ncol=SNC)

        for s0 in range(3):
            prep(s0)
            yield
        pend = samp_A(0)
        yield
        for s in range(NS):
            if s + 3 < NS:
                prep(s + 3)
                yield
            nxt = samp_A(s + 1) if s + 1 < NS else None
            yield
            yield from samp_B(s, pend)
            pend = nxt
        reserved.discard(bank_index(obk_))
        smt = T("smt", [128, 8])
        S.op("dve", lambda e: e.tensor_reduce(smt[:, :], smat[:, :].rearrange("p (s h) -> p h s", h=8), AX.X, ALU.add), reads=B("smat"), writes=B("smt"))
        S.op("dve", lambda e: e.tensor_scalar(smt[:, :], smt[:, :], 1e-30, None, ALU.add), reads=B("smt"), writes=B("smt"))
        S.op("dve", lambda e: e.reciprocal(smt[:, :], smt[:, :]), reads=B("smt"), writes=B("smt"))
        ats, atsb = rot("attn", attn, "attn")
        for h in range(8):
            kvh, g = h // 4, h % 4
            ob = ostate["obank"][kvh]; obb = ostate["obankb"][kvh]
            col = g * 128 + kvh * 64
            S.op("act", lambda e, ob=ob, g=g, h=h, col=col: e.activation(ats[:, col:col + 64], ob[:, g * 64:(g + 1) * 64], AF.Copy, scale=smt[:, h:h + 1]),
                 reads=[obb, bufs["smt"]], writes=[atsb])
        yield
        transpose_to(ats, 128, 4, lambda: mixsT[:, 0:4, :], atsb, bufs["mixsT"], "dve")
        yield

        wout_block(lambda kc: mixsT[:, kc, 0:NS], NS, xs_t[:, :], B("xs_t"), B("mixsT"))
        yield
        h2s, h2sb = rot("hb", hb, "hb")
        rmsnorm(xs_t[:, :], NS, B("xs_t"), None, h2s[0:NS, :], h2sb)
        transpose_to(h2s, NS, 8, lambda: actT2[(NTILE - 1) % 2][:, :, TT:TT + NS], h2sb, bufs[f"actT{(NTILE - 1) % 2}"], "dve", gcm=g2b)


    stage(9)
    first_ffn = [True]
    prev_acc = [set()]
    ysems = [Buf(f"ysem{i}") for i in range(2 * NB)]
    xloaded = {}

    def load_x(t_):
        base_ = 128 + t_ * TT
        for b_ in range(NB):
            sl = (t_ % 2) * NB + b_
            S.dma("sp", [(xr[:, sl, :], xin[base_ + b_ * 128: base_ + (b_ + 1) * 128, :])], writes=[xrb[sl]])
            xloaded[(t_, b_)] = sl
    def conv_tail_gen(N, dst_fn, dstbuf):
        mbk, mbb = nextbank()

        def fM(e, mbk=mbk):
            ins = None
            for cc in range(4):
                ins = e.matmul(mbk[:, 0:N], onesf[:, :], cch[:, cc, 0:N], start=(cc == 0), stop=(cc == 3))
            return ins
        S.op("pe", fM, reads=B("onesf", "cch"), writes=[mbb])
        for cc in range(4):
            S.op("dve", lambda e, mbk=mbk, cc=cc: e.scalar_tensor_tensor(cch[:, cc, 0:N], mbk[:, 0:N], -1.0 / 512, cch[:, cc, 0:N], ALU.mult, ALU.add),
                 reads=[mbb] + B("cch"), writes=B("cch"))
        yield
        vbk, vbb = nextbank()
        vi = bank_index(vbk)
        reserved.add(vi)
        for cc in range(4):
            tq, tqb = rot("tmp", tmpA, "tmpA")
            S.op("act", lambda e, tq=tq, cc=cc: e.activation(tq[:, 0:N], cch[:, cc, 0:N], AF.Square), reads=B("cch"), writes=[tqb])
            S.op("pe", lambda e, tq=tq, cc=cc, vbk=vbk: e.matmul(vbk[:, 0:N], onesf[:, :], tq[:, 0:N], start=(cc == 0), stop=(cc == 3)),
                 reads=[tqb, bufs["onesf"]], writes=[vbb])
            if cc == 3:
                reserved.discard(vi)
            yield
        S.op("act", lambda e, vbk=vbk: e.activation(rstdb[:, 0:N], vbk[:, 0:N], AF.Ln, scale=1.0 / 512, bias=epsc[:, :]), reads=[vbb, bufs["epsc"]], writes=B("rstdb"))
        S.op("act", lambda e: e.activation(rstdb[:, 0:N], rstdb[:, 0:N], AF.Exp, scale=-0.5), reads=B("rstdb"), writes=B("rstdb"))
        yield
        for cc in range(4):
            S.op("dve", lambda e, cc=cc: e.tensor_tensor(cch[:, cc, 0:N], cch[:, cc, 0:N], rstdb[:, 0:N], ALU.mult), reads=B("cch", "rstdb"), writes=[cchb[cc]])
            S.op("dve", lambda e, cc=cc: e.tensor_scalar(cch[:, cc, 0:N], cch[:, cc, 0:N], cvec[:, 4 + cc:5 + cc], cvec[:, 8 + cc:9 + cc], ALU.mult, ALU.add),
                 reads=[cchb[cc], bufs["cvec"]], writes=[cchb[cc]])
        yield
        sgs = []
        for cc in range(4):
            if cc == 3:
                yield
            sgs.append(sigmoid_from(cch[:, cc, 0:N], 128, N, [cchb[cc]], -1.0, None))
            if cc >= 1:
                c2 = cc - 1
                sg, sgb = sgs[c2]
                S.op("dve", lambda e, c2=c2, sg=sg: e.tensor_tensor(dst_fn(c2), cch[:, c2, 0:N], sg[:, 0:N], ALU.mult), reads=[cchb[c2], sgb], writes=[dstbuf])
        yield
        sg, sgb = sgs[3]
        S.op("dve", lambda e, sg=sg: e.tensor_tensor(dst_fn(3), cch[:, 3, 0:N], sg[:, 0:N], ALU.mult), reads=[cchb[3], sgb], writes=[dstbuf])
        S.op("dve", lambda e: e.tensor_copy(zcol[:, :], zcol[:, :]), reads=cchb, writes=B("cch", "zcol"))
        yield

    def aphase(t):
        last = (t == NTILE - 1)
        dgs = {0: conv_diag(0), 1: conv_diag(1)}
        hs_ = []
        for b in range(NB):
            sl = xloaded[(t, b)]
            h_, hbb = rot("hb", hb, "hb")
            rmsnorm(xr[:, sl, :], 128, [xrb[sl]], None, h_[:, :], hbb)
            hs_.append((h_, hbb))
        yield
        yield
        yield
        for b in range(NB):
            h_, hbb = hs_[b]
            transpose_to(h_, 128, 8, lambda b=b: hT[:, :, b * 128:(b + 1) * 128], hbb, bufs["hT"], "dve", gcm=g1b)
            yield
        yield
        for g in range(4):
            bk, bb = win_fm(lambda kc: hT[:, kc, 0:TT], TT, CQ + g * 128, B("hT"))
            S.op("act", lambda e, bk=bk, g=g: e.activation(qT[:, g, :], bk[:, 0:TT], AF.Copy, scale=0.125), reads=[bb], writes=B("qT"))
            yield
        bk, bb = win_fm(lambda kc: hT[:, kc, 0:TT], TT, CK, B("hT"))
        S.op("act", lambda e, bk=bk: e.copy(kT[:, 128:128 + TT], bk[:, 0:TT]), reads=[bb], writes=B("kT"))
        yield
        for b in range(NB):
            bk, bb = win_tm(lambda kc, b=b: hT[:, kc, b * 128:(b + 1) * 128], 128, CK, 256, B("hT"))
            S.op("dve", lambda e, bk=bk, b=b: e.tensor_copy(vtok[:, 1 + b, :], bk[:, 128:256]), reads=[bb], writes=B("vtok"))
            if last and b == NB - 1:
                S.op("dve", lambda e, bk=bk: e.tensor_copy(kvtok, bk[:, 0:256]), reads=[bb], writes=B("kvtok"))
                S.dma("sp", [(nk_d, kvtok[:, 0:128]), (nv_d, kvtok[:, 128:256])], reads=B("kvtok"), sem_buf=bufs["kvtok"])
            yield
        for cc in range(4):
            ak, ab = win_fm(lambda kc: hT[:, kc, 0:TT], TT, CA + cc * 128, B("hT"))
            bk2, bb2 = win_fm(lambda kc: hT[:, kc, 0:TT], TT, CB + cc * 128, B("hT"))
            glu(ak[:, 0:TT], bk2[:, 0:TT], 128, TT, [ab], [bb2], uT[:, cc, 32:32 + TT], B("uT"))
            yield
        if last:
            ak, ab = win_tm(lambda kc: hT[:, kc, TT - 128:TT], 128, CA, 512, B("hT"))
            bk2, bb2 = win_tm(lambda kc: hT[:, kc, TT - 128:TT], 128, CB, 512, B("hT"))
            for half in range(2):
                hs_ = slice(half * 256, (half + 1) * 256)
                sg, sgb = sigmoid_from(bk2[:, hs_], 128, 256, [bb2], -1.0, None)
                S.op("dve", lambda e, ak=ak, sg=sg, hs_=hs_: e.tensor_tensor(utok[:, hs_], ak[:, hs_], sg[:, 0:256], ALU.mult), reads=[ab, sgb], writes=B("utok"))
            S.dma("sp", [(ncv_d, utok[:, :])], reads=B("utok"), sem_buf=bufs["utok"])
            yield

        def pr_A(b, t=t):
            return attn_A(b % 2, lambda g, kvh, b=b: qT[kvh * 64:(kvh + 1) * 64, g, b * 128:(b + 1) * 128],
                          lambda kvh, b=b: kT[kvh * 64:(kvh + 1) * 64, b * 128:b * 128 + 256],
                          B("qT"), B("kT"), (t == 0 and b == 0), None)

        def conv_chunks():
            for cc in range(4):
                dg, dgb = dgs[cc]
                bk, bb = nextbank()
                bi = bank_index(bk)
                reserved.add(bi)
                for j0, j1 in ((0, 16), (16, 31)):
                    def fC(e, bk=bk, dg=dg, cc=cc, j0=j0, j1=j1):
                        ins = None
                        for j in range(j0, j1):
                            ins = e.matmul(bk[:, 0:TT], dg[:, j, :], uT[:, cc, 2 + j:2 + j + TT], start=(j == 0), stop=(j == 30))
                        return ins
                    S.op("pe", fC, reads=[dgb, bufs["uT"]], writes=[bb])
                    if j1 == 31:
                        reserved.discard(bi)
                        S.op("act", lambda e, bk=bk, cc=cc: e.activation(cch[:, cc, 0:TT], bk[:, 0:TT], AF.Identity, bias=cvec[:, cc:cc + 1]),
                             reads=[bb, bufs["cvec"]], writes=B("cch"))
                        if cc + 2 < 4:
                            dgs[cc + 2] = conv_diag(cc + 2)
                    yield

        cgen = conv_chunks()
        yield "ATTN"
        pend = pr_A(0)
        yield
        for b in range(NB):
            nxt = pr_A(b + 1) if b < NB - 1 else None
            yield
            bgen = attn_B(b % 2, pend, lambda half, kvh, b=b: vtok[:, b + half, kvh * 64:(kvh + 1) * 64], B("vtok"), "prompt", None,
                          filler=lambda: next(cgen, None))
            res = None
            while True:
                try:
                    next(bgen)
                    yield
                except StopIteration as stp:
                    res = stp.value
                    break
            at, atb = res
            transpose_to(at, 128, 4, lambda b=b: mixT[:, 0:4, b * 128:(b + 1) * 128], atb, bufs["mixT"], "dve")
            pend = nxt
            yield
        for _ in cgen:
            yield
        yield from conv_tail_gen(TT, lambda cc: mixT[:, 4 + cc, :], bufs["mixT"])
        if not last:
            S.op("dve", lambda e: e.tensor_copy(kT[:, 0:128], kT[:, TT:TT + 128]), reads=B("kT"), writes=B("kT"))
            S.op("dve", lambda e: e.tensor_copy(vtok[:, 0, :], vtok[:, NB, :]), reads=B("vtok"), writes=B("vtok"))
            S.op("dve", lambda e: e.tensor_copy(uT[:, :, 0:32], uT[:, :, TT:TT + 32]), reads=B("uT"), writes=B("uT"))
        yield
        for _ in range(8):
            yield
        aT = actT2[t % 2]; aTb = bufs[f"actT{t % 2}"]
        hs2 = []
        for b in range(NB):
            sl = xloaded[(t, b)]
            wout_block(lambda kc, b=b: mixT[:, kc, b * 128:(b + 1) * 128], 128, xr[:, sl, :], [xrb[sl]], B("mixT"))
            yield
            h_, hbb = rot("hb", hb, "hb")
            rmsnorm(xr[:, sl, :], 128, [xrb[sl]], None, h_[:, :], hbb)
            hs2.append((h_, hbb))
            yield
        yield
        yield
        for b in range(NB):
            h_, hbb = hs2[b]
            transpose_to(h_, 128, 8, lambda b=b, aT=aT: aT[:, :, b * 128:(b + 1) * 128], hbb, aTb, "dve", gcm=g2b)
            yield

    def run_all(g):
        for _ in g:
            pass

    sgen = sample_gen()
    load_x(0)
    gA0 = aphase(0)
    _SENT = object()
    pre_done = False
    while True:
        prog = False
        for _ in range(9):
            if next(sgen, _SENT) is not _SENT:
                prog = True
        if not pre_done:
            v_ = next(gA0, _SENT)
            if v_ == "ATTN" or v_ is _SENT:
                pre_done = True
        if not prog:
            break
    run_all(gA0)
    def tile_setup(t):
        base = 128 + t * TT
        last = (t == NTILE - 1)
        aT = actT2[t % 2]; aTb = bufs[f"actT{t % 2}"]
        xs_ = [xloaded[(t, b)] for b in range(NB)]
        stage(13 + 10 * t)
        old_acc = prev_acc[0]
        fresh = [i for i in range(NBANK) if i not in old_acc and i not in reserved]
        stale = [i for i in range(NBANK) if i in old_acc]
        need = 2 * NB + (2 if last else 0)
        pick = fresh[:need // 2] + stale[:need - need // 2]
        pick += [i for i in fresh + stale if i not in pick][:need - len(pick)]
        flat = [(banks[i], bank_bufs[i]) for i in pick[:need]]
        acc = [[flat[2 * b], flat[2 * b + 1]] for b in range(len(flat) // 2)]
        for row in acc:
            for a_ in row:
                reserved.add(bank_index(a_[0]))
        prev_acc[0] = set(bank_index(a_[0]) for row in acc for a_ in row)
        rest_fresh = [i for i in fresh if i not in prev_acc[0]]
        if rest_fresh:
            bank_ctr[0] = rest_fresh[0]
        accbufs = [a_[1] for row in acc for a_ in row]
        grp = {}

        def emit_U(fg, last=last, aT=aT, aTb=aTb):
            wu, wub = rot("wup", wupb, "wupb")
            wd, wdb = rot("wdn", wdnb, "wdnb")
            hd, hdb = rot("hid", hidr, "hidr")
            grp[fg] = (wd, wdb, hd, hdb)
            extra = arena_alias if first_ffn[0] else []
            S.dma("sp", [(wu, wup_s[fg * 128:(fg + 1) * 128, :].rearrange("p (k f) -> p k f", k=8))], reads=[wsc[fg]], writes=[wub] + extra)
            S.dma("sp", [(wd, wdn_s[fg * 128:(fg + 1) * 128, :].rearrange("p (c n) -> p c n", c=2))], reads=[wsc[16 + fg]], writes=[wdb] + extra)
            if not last:
                bk, bb = nextbank()

                def fU(e, bk=bk, wu=wu, aT=aT):
                    ins = None
                    for fc in range(2):
                        for kc in range(8):
                            ins = e.matmul(bk[:, fc * TT:(fc + 1) * TT], wu[:, kc, fc * 128:(fc + 1) * 128], aT[:, kc, 0:TT], start=(kc == 0), stop=(kc == 7))
                    return ins
                S.op("pe", fU, reads=[wub, aTb], writes=[bb])
                r_, rb_ = rot("rl", rl, "rl")
                S.op("act", lambda e, bk=bk, r_=r_: e.activation(r_[:, :, 0:TT], bk[:, :].rearrange("p (c n) -> p c n", c=2), AF.Relu), reads=[bb], writes=[rb_])
                S.op("act", lambda e, r_=r_, hd=hd: e.activation(hd[:, :, 0:TT], r_[:, :, 0:TT], AF.Square), reads=[rb_], writes=[hdb])
                return
            for fc in range(2):
                bk, bb = nextbank()

                def fU(e, bk=bk, wu=wu, fc=fc, aT=aT):
                    ins = None
                    for kc in range(8):
                        ins = e.matmul(bk[:, 0:TT], wu[:, kc, fc * 128:(fc + 1) * 128], aT[:, kc, 0:TT], start=(kc == 0), stop=(kc == 7))
                    for kc in range(8):
                        ins = e.matmul(bk[:, TT:TT + NS], wu[:, kc, fc * 128:(fc + 1) * 128], aT[:, kc, TT:TT + NS], start=(kc == 0), stop=(kc == 7))
                    return ins
                S.op("pe", fU, reads=[wub, aTb], writes=[bb])
                ncol = TT + NS
                r_, rb_ = rot("rl", rl, "rl")
                S.op("act", lambda e, bk=bk, r_=r_, ncol=ncol: e.activation(r_[:, 0, 0:ncol], bk[:, 0:ncol], AF.Relu), reads=[bb], writes=[rb_])
                S.op("dve", lambda e, r_=r_, hd=hd, fc=fc, ncol=ncol: e.tensor_tensor(hd[:, fc, 0:ncol], r_[:, 0, 0:ncol], r_[:, 0, 0:ncol], ALU.mult), reads=[rb_], writes=[hdb])

        def emit_D(fg, last=last, acc=acc):
            wd, wdb, hd, hdb = grp[fg]

            def fD(e, wd=wd, hd=hd, fg=fg, acc=acc):
                ins = None
                for fc in range(2):
                    f = fg * 2 + fc
                    for b in range(NB):
                        for half in range(2):
                            ins = e.matmul(acc[b][half][0][:, :], hd[:, fc, b * 128:(b + 1) * 128], wd[:, fc, half * 512:(half + 1) * 512],
                                           start=(f == 0), stop=(f == 31))
                    if last:
                        for half in range(2):
                            ins = e.matmul(acc[NB][half][0][0:NS, :], hd[:, fc, TT:TT + NS], wd[:, fc, half * 512:(half + 1) * 512],
                                           start=(f == 0), stop=(f == 31))
                return ins
            S.op("pe", fD, reads=[wdb, hdb], writes=accbufs)


        return dict(t=t, last=last, acc=acc, accbufs=accbufs, grp=grp, emit_U=emit_U, emit_D=emit_D, xs_=xs_)

    def tile_body(cx):
        t = cx["t"]; last = cx["last"]; emit_U = cx["emit_U"]; emit_D = cx["emit_D"]
        nxtA = None
        for fg in range(16):
            if fg + 1 < 16:
                emit_U(fg + 1)
            if fg == 2:
                first_ffn[0] = False
            emit_D(fg)
            if t == SAMPLE_TILE:
                if fg == 1:
                    load_x(t + 1)
                if fg < 7:
                    for _ in range(14):
                        next(sgen, None)
                elif fg == 7:
                    run_all(sgen)
                    nxtA = aphase(t + 1)
                    next(nxtA, None)
                elif fg >= 9:
                    for _ in range(13):
                        next(nxtA, None)
                continue
            if fg == 1 and not last:
                load_x(t + 1)
                nxtA = aphase(t + 1)
                next(nxtA, None)
            if nxtA is not None and fg >= 2:
                for _ in range(APF):
                    next(nxtA, None)
        return nxtA

    def tile_release(cx):
        acc = cx["acc"]
        first_ffn[0] = False
        for row in acc:
            for a_ in row:
                reserved.discard(bank_index(a_[0]))

    def tile_epilogue(cx):
        t = cx["t"]; last = cx["last"]; acc = cx["acc"]; xs_ = cx["xs_"]
        stage(14 + 10 * t)
        for b in range(NB):
            for half in range(2):
                S.op("dve", lambda e, b=b, half=half, acc=acc, sl=xs_[b]: e.tensor_tensor(xr[:, sl, half * 512:(half + 1) * 512], xr[:, sl, half * 512:(half + 1) * 512],
                                                                     acc[b][half][0][:, :], ALU.add), reads=[acc[b][half][1], xrb[xs_[b]]], writes=[xrb[xs_[b]]])
            rmsnorm(xr[:, xs_[b], :], 128, [xrb[xs_[b]]], gfb, xr[:, xs_[b], :], xrb[xs_[b]], junk=(jkt[:, :], bufs["jkt"]))
            S.dma("pool", [(y_d[t * TT + b * 128: t * TT + (b + 1) * 128, :], xr[:, xs_[b], :])], reads=[xrb[xs_[b]]], sem_buf=ysems[xs_[b]])
        if last:
            for half in range(2):
                S.op("dve", lambda e, half=half, acc=acc: e.tensor_tensor(xs_t[:, half * 512:(half + 1) * 512], xs_t[:, half * 512:(half + 1) * 512],
                                                                acc[NB][half][0][0:NS, :], ALU.add), reads=[acc[NB][half][1]] + B("xs_t"), writes=B("xs_t"))
            rmsnorm(xs_t[:, :], NS, B("xs_t"), gfb, xr[0:NS, 0, :], xrb[0])
            S.dma("sp", [(ys_d, xr[0:NS, 0, :])], reads=[xrb[0]], sem_buf=xrb[0])

    cx = tile_setup(0)
    cx["emit_U"](0)
    for t in range(NTILE):
        nxtA = tile_body(cx)
        first_ffn[0] = False
        if nxtA is not None:
            run_all(nxtA)
        cur = cx
        if t + 1 < NTILE:
            cx = tile_setup(t + 1)
            cx["emit_U"](0)
        tile_epilogue(cur)
        keep_ = set(bank_index(a_[0]) for row in cx["acc"] for a_ in row) if cx is not cur else set()
        for row in cur["acc"]:
            for a_ in row:
                if bank_index(a_[0]) not in keep_:
                    reserved.discard(bank_index(a_[0]))


def _prep_shared(meta_tokens, rel_bias, norm1_g, w_in, attn_sinks, conv_w, conv_b, conv_ln_g, conv_ln_b,
                 w_out, norm2_g, w_up, w_down, norm_f_g):
    f = np.float32
    perm = _pair_perm()
    wi = np.asarray(w_in[0], f)
    cols = np.concatenate([perm, np.arange(512, 1792)])
    wi = wi[:, cols]
    win = np.ascontiguousarray(wi.reshape(8, 128, 1792).transpose(1, 0, 2))
    wo = np.asarray(w_out[0], f)
    rows = np.concatenate([perm, np.arange(512, 1024)])
    wo = wo[rows, :]
    wout = np.ascontiguousarray(wo.reshape(8, 128, 1024).transpose(1, 0, 2))
    wu = np.asarray(w_up[0], f)
    wup = np.ascontiguousarray(wu.reshape(8, 128, 16, 256).transpose(2, 1, 0, 3)).reshape(16 * 128, 2048)
    wd = np.asarray(w_down[0], f)
    wdn = np.ascontiguousarray(wd.reshape(16, 2, 128, 1024).transpose(0, 2, 1, 3)).reshape(16 * 128, 2048)
    cw = np.asarray(conv_w[0], f)
    cwcm = np.ascontiguousarray(cw.T.reshape(4, 128, 31).transpose(1, 0, 2))
    cvec = np.ascontiguousarray(np.concatenate([np.asarray(conv_b[0], f).reshape(4, 128).T,
                                                np.asarray(conv_ln_g[0], f).reshape(4, 128).T,
                                                np.asarray(conv_ln_b[0], f).reshape(4, 128).T], axis=1))
    rbx = np.concatenate([np.asarray(rel_bias, f), np.full((1, 8), NEG, f)], axis=0)
    onehot = np.zeros((33, 384), f)
    for j in range(384):
        dist = 255 - j
        if 0 <= dist <= 128:
            onehot[int(_t5_bucket_np(np.array([dist]))[0]), j] = 1.0
        else:
            onehot[32, j] = 1.0
    jmat = np.ascontiguousarray(np.eye(128, dtype=f)[::-1])
    ident = np.eye(128, dtype=f)
    rowmask = np.full((128, NS), NEG, f)
    for s in range(NS):
        rowmask[s, s] = 0.0
    ind = np.zeros((96, NS), f)
    for s in range(NS):
        ind[s * 6:(s + 1) * 6, s] = 1.0
    cwrep = np.ascontiguousarray(np.tile(cw[:30].reshape(6, 2560), (NS, 1)))
    srow = np.ascontiguousarray(np.stack([cw[30], np.asarray(conv_b[0], f), np.asarray(conv_ln_g[0], f), np.asarray(conv_ln_b[0], f)]))
    return dict(win=win, wout=wout, wup=wup, wdn=wdn, g1cm=np.ascontiguousarray(np.asarray(norm1_g[0], f).reshape(8, 128).T), g2cm=np.ascontiguousarray(np.asarray(norm2_g[0], f).reshape(8, 128).T),
                gf=np.asarray(norm_f_g, f), cwcm=cwcm, cvec=cvec, rbx=rbx, sinks=np.asarray(attn_sinks[0], f),
                onehot=onehot, jmat=jmat, ident=ident, rowmask=rowmask, ind=ind, cwrep=cwrep)


_NC_CACHE = {}


def kernel(x_prompt, x_sample, cache_k, cache_v, state_conv, meta_tokens, rel_bias,
           norm1_g, w_in, attn_sinks, conv_w, conv_b, conv_ln_g, conv_ln_b,
           w_out, norm2_g, w_up, w_down, norm_f_g):
    f = np.float32
    shared = _prep_shared(meta_tokens, rel_bias, norm1_g, w_in, attn_sinks, conv_w, conv_b, conv_ln_g, conv_ln_b,
                          w_out, norm2_g, w_up, w_down, norm_f_g)
    xp = np.asarray(x_prompt, f)[0]
    halo0 = np.concatenate([np.zeros((112, D), f), np.asarray(meta_tokens, f)], axis=0)
    in_maps = []
    for c in range(NCORES):
        halo = halo0 if c == 0 else xp[c * 2048 - 128:c * 2048]
        m = dict(shared)
        m["xin"] = np.ascontiguousarray(np.concatenate([halo, xp[c * 2048:(c + 1) * 2048]], axis=0))
        m["xs"] = np.ascontiguousarray(np.asarray(x_sample, f)[c * NS:(c + 1) * NS, 0, :])
        m["ck"] = np.ascontiguousarray(np.asarray(cache_k, f)[0, c * NS:(c + 1) * NS].reshape(NS, 128, 128))
        m["cv"] = np.ascontiguousarray(np.asarray(cache_v, f)[0, c * NS:(c + 1) * NS].reshape(NS, 128, 128))
        m["sc"] = np.ascontiguousarray(np.asarray(state_conv, f)[0, c * NS:(c + 1) * NS])
        km = np.zeros((256,), f)
        if c == 0:
            km[:112] = NEG
        m["kmask"] = km
        in_maps.append(m)
    if "nc" not in _NC_CACHE:
        _NC_CACHE["nc"] = build_program()
    nc = _NC_CACHE["nc"]
    res = run_bass_kernel_spmd(nc, in_maps, core_ids=list(range(NCORES)))
    R = res.results
    y_prompt = np.concatenate([R[c]["y"] for c in range(NCORES)], axis=0)[None]
    y_sample = np.concatenate([R[c]["ys"] for c in range(NCORES)], axis=0)[:, None, :]
    nk = R[NCORES - 1]["nk"].reshape(1, 1, 128, 2, 64)
    nv = R[NCORES - 1]["nv"].reshape(1, 1, 128, 2, 64)
    ncv = R[NCORES - 1]["ncv"][98:128].reshape(1, 1, 30, 512)
    nks = np.concatenate([R[c]["nks"] for c in range(NCORES)], axis=0).reshape(1, 128, 128, 2, 64)
    nvs = np.concatenate([R[c]["nvs"] for c in range(NCORES)], axis=0).reshape(1, 128, 128, 2, 64)
    ncs = np.concatenate([R[c]["ncs"] for c in range(NCORES)], axis=0).reshape(1, 128, 30, 512)
    return (y_prompt.astype(f), y_sample.astype(f), nk.astype(f), nv.astype(f), ncv.astype(f),
            nks.astype(f), nvs.astype(f), ncs.astype(f))
```
